# Optimizing a Trainium2 kernel written in Bass

```python
import math
import jax, jax.numpy as jnp
from jax import lax
import numpy as np

D_MODEL = 1024
BATCH = 8
SEQ = 2048
DEPTH = 4

POOL_WIDTH = D_MODEL // 4
POOL_WINDOWS = (2, 4, 8, 16)
POOL_GROUPS = len(POOL_WINDOWS)
POOL_GROUP_DIM = POOL_WIDTH // POOL_GROUPS
CONV_WIDTH = D_MODEL // 4
CONV_K = 3
HEAD_DIM = 64
ATTN_HEADS = 8
ATTN_WIDTH = ATTN_HEADS * HEAD_DIM
ROPE_DIM = HEAD_DIM // 4
ROPE_THETA = 500000.0
MOBA_BLOCK = 256
MOBA_TOPK = 3
Q_CHUNK = 32
N_BRANCH = 3
IN_WIDTH = POOL_WIDTH + 3 * CONV_WIDTH + 3 * ATTN_WIDTH + N_BRANCH * D_MODEL
D_FF = -(-8 * D_MODEL // (3 * 256)) * 256
EPS = 1e-6
NEG = -1e30
POS_OFFSET_MAX = 1024

kernel_name = "hybrid_pool_conv_moba_adaln_block"


def rmsnorm(x, w):
    xf = x.astype(jnp.float32)
    xf = xf * lax.rsqrt(jnp.mean(xf * xf, axis=-1, keepdims=True) + EPS)
    return xf.astype(x.dtype) * w


def rotary_tables(positions):
    freqs = ROPE_THETA ** (-jnp.arange(0, ROPE_DIM, 2, dtype=jnp.float32) / ROPE_DIM)
    ang = positions.astype(jnp.float32)[..., None] * freqs
    return jnp.cos(ang), jnp.sin(ang)


def apply_partial_rope(x, cos, sin):
    half = ROPE_DIM // 2
    xr = x[..., :ROPE_DIM].astype(jnp.float32)
    x1, x2 = xr[..., :half], xr[..., half:]
    c = cos[:, :, None, :]
    s = sin[:, :, None, :]
    rot = jnp.concatenate([x1 * c - x2 * s, x2 * c + x1 * s], axis=-1).astype(x.dtype)
    return jnp.concatenate([rot, x[..., ROPE_DIM:]], axis=-1)


def pool_mixer(u, pool_w, pool_scale):
    B, S, _ = u.shape
    uf = u.astype(jnp.float32)
    cs = jnp.pad(jnp.cumsum(uf, axis=1), ((0, 0), (1, 0), (0, 0)))
    t = jnp.arange(S)
    outs = []
    for g, w in enumerate(POOL_WINDOWS):
        sl = slice(g * POOL_GROUP_DIM, (g + 1) * POOL_GROUP_DIM)
        lo = jnp.maximum(t + 1 - w, 0)
        cs_g = cs[..., sl]
        window_sum = cs_g[:, 1:] - jnp.take(cs_g, lo, axis=1)
        count = (t + 1 - lo).astype(jnp.float32)[None, :, None]
        outs.append(window_sum / count - uf[..., sl])
    pooled = jnp.concatenate(outs, axis=-1).astype(u.dtype).reshape(B, S, POOL_GROUPS, POOL_GROUP_DIM)
    y = jnp.einsum('bsgi,gio->bsgo', pooled, pool_w).reshape(B, S, POOL_WIDTH)
    return y * pool_scale


def causal_short_conv(u, conv_w):
    C = u.shape[-1]
    return lax.conv_general_dilated(
        u, conv_w[:, None, :], window_strides=(1,), padding=((CONV_K - 1, 0),),
        dimension_numbers=('NWC', 'WIO', 'NWC'), feature_group_count=C)


def moba_attention(q, k, v):
    B, H, S, hd = q.shape
    nb = -(-S // MOBA_BLOCK)
    pad = nb * MOBA_BLOCK - S
    kb = jnp.pad(k, ((0, 0), (0, 0), (0, pad), (0, 0))).reshape(B, H, nb, MOBA_BLOCK, hd)
    vb = jnp.pad(v, ((0, 0), (0, 0), (0, pad), (0, 0))).reshape(B, H, nb, MOBA_BLOCK, hd)
    kmean = jnp.mean(kb.astype(jnp.float32), axis=3)
    n_sel = max(1, min(MOBA_TOPK, nb - 1))
    nc = S // Q_CHUNK
    q_chunks = q.reshape(B, H, nc, Q_CHUNK, hd).transpose(2, 0, 1, 3, 4)
    starts = jnp.arange(nc, dtype=jnp.int32) * Q_CHUNK
    scale = 1.0 / math.sqrt(hd)
    gather_blocks = jax.vmap(jax.vmap(lambda blocks, ix: blocks[ix]))

    def one_chunk(args):
        q_i, start = args
        qpos = start + jnp.arange(Q_CHUNK, dtype=jnp.int32)
        blk = start // MOBA_BLOCK
        gate = jnp.einsum('bhqd,bhnd->bhqn', q_i.astype(jnp.float32), kmean)
        past = jnp.arange(nb) < blk
        gate = jnp.where(past, gate, NEG)
        _, idx = lax.top_k(gate, n_sel)
        valid = idx < blk
        k_sel = gather_blocks(kb, idx)
        v_sel = gather_blocks(vb, idx)
        s_sel = jnp.einsum('bhqd,bhqjkd->bhqjk', q_i, k_sel).astype(jnp.float32) * scale
        s_sel = jnp.where(valid[..., None], s_sel, NEG).reshape(B, H, Q_CHUNK, n_sel * MOBA_BLOCK)
        k_own = lax.dynamic_index_in_dim(kb, blk, axis=2, keepdims=False)
        v_own = lax.dynamic_index_in_dim(vb, blk, axis=2, keepdims=False)
        s_own = jnp.einsum('bhqd,bhkd->bhqk', q_i, k_own).astype(jnp.float32) * scale
        kpos = blk * MOBA_BLOCK + jnp.arange(MOBA_BLOCK, dtype=jnp.int32)
        s_own = jnp.where(kpos[None, :] <= qpos[:, None], s_own, NEG)
        p = jax.nn.softmax(jnp.concatenate([s_sel, s_own], axis=-1), axis=-1).astype(v.dtype)
        p_sel = p[..., :n_sel * MOBA_BLOCK].reshape(B, H, Q_CHUNK, n_sel, MOBA_BLOCK)
        p_own = p[..., n_sel * MOBA_BLOCK:]
        return (jnp.einsum('bhqjk,bhqjkd->bhqd', p_sel, v_sel)
                + jnp.einsum('bhqk,bhkd->bhqd', p_own, v_own))

    out = lax.map(one_chunk, (q_chunks, starts))
    return out.transpose(1, 2, 0, 3, 4).reshape(B, H, S, hd)


def hybrid_layer(x, c_act, cos, sin, norm1_w, norm2_w, w_ada, b_ada, w_in, pool_w, pool_scale,
                 conv_w, q_norm_w, k_norm_w, p_pool, p_conv, p_attn, w_out, w_gate, w_up, w_down):
    B, S, D = x.shape
    mod = (c_act @ w_ada + b_ada)[:, None, :]
    sh1, sc1, g1, sh2, sc2, g2 = jnp.split(mod, 6, axis=-1)
    h = rmsnorm(x, norm1_w) * (1 + sc1) + sh1
    u = h @ w_in
    widths = [POOL_WIDTH, CONV_WIDTH, CONV_WIDTH, CONV_WIDTH, ATTN_WIDTH, ATTN_WIDTH, ATTN_WIDTH]
    offsets = [int(o) for o in np.cumsum(widths)]
    u_pool, u_conv, b_conv, c_conv, q, k, v, gate_logits = jnp.split(u, offsets, axis=-1)
    y_pool = pool_mixer(u_pool, pool_w, pool_scale)
    y_conv = b_conv * causal_short_conv(c_conv * u_conv, conv_w)
    q = apply_partial_rope(rmsnorm(q.reshape(B, S, ATTN_HEADS, HEAD_DIM), q_norm_w), cos, sin)
    k = apply_partial_rope(rmsnorm(k.reshape(B, S, ATTN_HEADS, HEAD_DIM), k_norm_w), cos, sin)
    v = v.reshape(B, S, ATTN_HEADS, HEAD_DIM)
    o = moba_attention(q.transpose(0, 2, 1, 3), k.transpose(0, 2, 1, 3), v.transpose(0, 2, 1, 3))
    y_attn = o.transpose(0, 2, 1, 3).reshape(B, S, ATTN_WIDTH)
    gates = jax.nn.sigmoid(gate_logits).reshape(B, S, N_BRANCH, D)
    merged = (gates[:, :, 0] * (y_pool @ p_pool)
              + gates[:, :, 1] * (y_conv @ p_conv)
              + gates[:, :, 2] * (y_attn @ p_attn))
    x = x + g1 * (merged @ w_out)
    h2 = rmsnorm(x, norm2_w) * (1 + sc2) + sh2
    x = x + g2 * ((jax.nn.silu(h2 @ w_gate) * (h2 @ w_up)) @ w_down)
    return x


def setup_inputs(seed: int = 0) -> dict:
    key = jax.random.key(seed)
    ks = jax.random.split(key, 24)
    L = DEPTH

    def nrm(k, shape, fan_in, gain=1.0):
        return gain * fan_in ** -0.5 * jax.random.normal(k, shape, jnp.float32)

    def gain_init(k, shape):
        return 1.0 + 0.1 * jax.random.normal(k, shape, jnp.float32)

    x = jax.random.normal(ks[0], (BATCH, SEQ, D_MODEL), jnp.float32)
    c = jax.random.normal(ks[1], (BATCH, D_MODEL), jnp.float32)
    positions = (jnp.arange(SEQ, dtype=jnp.int32)[None, :]
                 + jax.random.randint(ks[2], (BATCH, 1), 0, POS_OFFSET_MAX, dtype=jnp.int32))
    return {
        "x": x,
        "c": c,
        "positions": positions,
        "norm1_w": gain_init(ks[3], (L, D_MODEL)),
        "norm2_w": gain_init(ks[4], (L, D_MODEL)),
        "w_ada": nrm(ks[5], (L, D_MODEL, 6 * D_MODEL), D_MODEL, 0.5),
        "b_ada": 0.02 * jax.random.normal(ks[6], (L, 6 * D_MODEL), jnp.float32),
        "w_in": nrm(ks[7], (L, D_MODEL, IN_WIDTH), D_MODEL),
        "pool_w": nrm(ks[8], (L, POOL_GROUPS, POOL_GROUP_DIM, POOL_GROUP_DIM), POOL_GROUP_DIM),
        "pool_scale": gain_init(ks[9], (L, POOL_WIDTH)),
        "conv_w": nrm(ks[10], (L, CONV_K, CONV_WIDTH), CONV_K),
        "q_norm_w": gain_init(ks[11], (L, HEAD_DIM)),
        "k_norm_w": gain_init(ks[12], (L, HEAD_DIM)),
        "p_pool": nrm(ks[13], (L, POOL_WIDTH, D_MODEL), POOL_WIDTH),
        "p_conv": nrm(ks[14], (L, CONV_WIDTH, D_MODEL), CONV_WIDTH),
        "p_attn": nrm(ks[15], (L, ATTN_WIDTH, D_MODEL), ATTN_WIDTH),
        "w_out": nrm(ks[16], (L, D_MODEL, D_MODEL), D_MODEL),
        "w_gate": nrm(ks[17], (L, D_MODEL, D_FF), D_MODEL),
        "w_up": nrm(ks[18], (L, D_MODEL, D_FF), D_MODEL),
        "w_down": nrm(ks[19], (L, D_FF, D_MODEL), D_FF),
    }


def reference(x, c, positions, norm1_w, norm2_w, w_ada, b_ada, w_in, pool_w, pool_scale, conv_w,
              q_norm_w, k_norm_w, p_pool, p_conv, p_attn, w_out, w_gate, w_up, w_down):
    c_act = jax.nn.silu(c)
    cos, sin = rotary_tables(positions)
    for l in range(DEPTH):
        x = hybrid_layer(x, c_act, cos, sin, norm1_w[l], norm2_w[l], w_ada[l], b_ada[l], w_in[l],
                         pool_w[l], pool_scale[l], conv_w[l], q_norm_w[l], k_norm_w[l],
                         p_pool[l], p_conv[l], p_attn[l], w_out[l], w_gate[l], w_up[l], w_down[l])
    return x
```

```python
import math
from contextlib import ExitStack

import numpy as np
import concourse.bass as bass
import concourse.mybir as mybir
from concourse.bass_utils import run_bass_kernel_spmd

F32 = mybir.dt.float32
BF16 = mybir.dt.bfloat16
I32 = mybir.dt.int32
ALU = mybir.AluOpType
AF = mybir.ActivationFunctionType
AX = mybir.AxisListType

FUSED = True
DEPTH = 4
D = 1024
S_ = 2048
NT = 16
NTC = 4
DFF = 2816
EPS = 1e-6
NEGB = -30000.0
O_N1W, O_N2W, O_BADA, O_PSC, O_CW, O_QKW, O_CT, NS = 0, 32, 64, 256, 264, 288, 800, 808
O_IDF, O_TRI, O_FREQ, O_INVW, O_CORR, NCF = 0, 128, 256, 264, 266, 298
POOL_W = (2, 4, 8, 16)

ENGS = ["pe", "act", "dve", "pool", "sp"]


class Op:
    __slots__ = ("eng", "fn", "deps", "is_dma", "sig", "has_dep", "name", "pos")

    def __init__(self, eng, fn, is_dma, name=None):
        self.eng = eng
        self.fn = fn
        self.deps = []
        self.is_dma = is_dma
        self.sig = None
        self.has_dep = False
        self.name = name


class Sched:
    def __init__(self, nc, n_dma_sems=8, maxv=12000):
        self.nc = nc
        self.ops = {e: [] for e in ENGS}
        self.last_writer = {}
        self.readers = {}
        self.n_dma_sems = n_dma_sems
        self.maxv = maxv
        self.cow = {}
        self.old_r = {}
        self.old_w = {}

    def new_gen(self, k):
        self.old_r[k] = self.readers.get(k, [])
        self.old_w[k] = self.cow.get(k, [])
        self.readers[k] = []
        self.cow[k] = []

    def op(self, eng, fn, reads=(), writes=(), dma=False, name=None, joins=()):
        o = Op(eng, fn, dma, name)
        deps = {}
        lw = self.last_writer
        rd = self.readers
        if eng != "pe":
            extra = [k for k in reads if isinstance(k, tuple) and k[0] == "bank"]
            if extra:
                writes = list(writes) + extra
        for k in joins:
            for r in self.old_r.get(k, ()):
                deps[id(r)] = (r, "war")
            for w in self.old_w.get(k, ()):
                if id(w) not in deps:
                    deps[id(w)] = (w, "waw")
            self.cow.setdefault(k, []).append(o)
        for k in reads:
            w = lw.get(k)
            if w is not None:
                deps[id(w)] = (w, "raw")
            for w in self.cow.get(k, ()):
                deps[id(w)] = (w, "raw")
        for k in writes:
            w = lw.get(k)
            if w is not None and id(w) not in deps:
                deps[id(w)] = (w, "waw")
            for r in rd.get(k, ()):
                if id(r) not in deps:
                    deps[id(r)] = (r, "war")
        for k in reads:
            rd.setdefault(k, []).append(o)
        for k in writes:
            lw[k] = o
            rd[k] = []
        best = {}
        for d, kind in deps.values():
            if d.eng == eng and not d.is_dma and not dma:
                if eng == "pe" or kind != "raw":
                    continue
            if d.is_dma:
                o.deps.append(d)
                d.has_dep = True
            else:
                b = best.get(d.eng)
                if b is None or d.pos > b.pos:
                    best[d.eng] = d
        for d in best.values():
            o.deps.append(d)
            d.has_dep = True
        o.pos = len(self.ops[eng])
        self.ops[eng].append(o)
        return o

    def emit(self, stack):
        nc = self.nc
        for e in ENGS:
            cnt = 0
            nsem = 0
            cur = None
            for o in self.ops[e]:
                if o.is_dma or not o.has_dep:
                    continue
                if cur is None or cnt >= self.maxv:
                    cur = stack.enter_context(nc.semaphore(f"s_{e}_{nsem}"))
                    nsem += 1
                    cnt = 0
                cnt += 1
                o.sig = (cur, cnt, 1)
        for e in ENGS:
            dl = [o for o in self.ops[e] if o.is_dma]
            if not dl:
                continue
            sems = [stack.enter_context(nc.semaphore(f"d_{e}_{i}")) for i in range(min(self.n_dma_sems, len(dl)))]
            cnts = [0] * len(sems)
            for i, o in enumerate(dl):
                j = i % len(sems)
                cnts[j] += 1
                o.sig = (sems[j], 16 * cnts[j], 16)

        def run_engine(ename, eng):
            waited = {}
            for o in self.ops[ename]:
                need = {}
                for d in o.deps:
                    s, v = d.sig[0], d.sig[1]
                    if waited.get(s, 0) >= v:
                        continue
                    if need.get(s, 0) < v:
                        need[s] = v
                if o.is_dma:
                    s, v = o.sig[0], o.sig[1]
                    if v > 16 and waited.get(s, 0) < v - 16:
                        need[s] = max(need.get(s, 0), v - 16)
                for s, v in need.items():
                    eng.wait_ge(s, v)
                    waited[s] = v
                ins = o.fn(eng)
                if o.sig is not None:
                    ins.then_inc(o.sig[0], o.sig[2])
            last = {}
            for o in self.ops[ename]:
                if o.is_dma:
                    last[o.sig[0]] = o.sig[1]
            for s, v in last.items():
                if waited.get(s, 0) < v:
                    eng.wait_ge(s, v)

        with nc.Block() as block:
            @block.tensor
            def _(e):
                run_engine("pe", e)

            @block.scalar
            def _(e):
                run_engine("act", e)

            @block.vector
            def _(e):
                run_engine("dve", e)

            @block.gpsimd
            def _(e):
                run_engine("pool", e)

            @block.sync
            def _(e):
                run_engine("sp", e)


ARENA = 65536
BLK = 1024


def build(L):
    nc = bass.Bass("TRN2", target_bir_lowering=False)

    def din(name, shape, dt=F32):
        return nc.dram_tensor(name, shape, dt, kind="ExternalInput").ap()

    x_d = din("x", [S_, D])
    small_d = din("small", [128, NS])
    consts_d = din("consts", [128, NCF])
    pos_d = din("pos", [128, NT], I32)
    w_ada_d = din("w_ada", [L, D, 6 * D])
    w_in_d = din("w_in", [L, D, 5632])
    pool_w_d = din("pool_w", [L, 4, 64, 64])
    p_pool_d = din("p_pool", [L, 256, D])
    p_conv_d = din("p_conv", [L, 256, D])
    p_attn_d = din("p_attn", [L, 512, D])
    w_out_d = din("w_out", [L, D, D])
    w_gate_d = din("w_gate", [L, D, DFF])
    w_up_d = din("w_up", [L, D, DFF])
    w_down_d = din("w_down", [L, DFF, D])
    lidx_d = None
    out_d = nc.dram_tensor("out", [S_, D], F32, kind="ExternalOutput").ap()

    with ExitStack() as st:
        S = Sched(nc)

        def sb(name, shape, dt):
            return st.enter_context(nc.sbuf_tensor("sb_" + name, shape, dt))

        xT = sb("xT", [128, 8, S_], F32)
        hT = sb("hT", [128, 8, S_], BF16)
        wsl = [sb(f"ws{i}", [128, 4096], BF16) for i in range(4)]
        small = sb("small", [128, NS], F32)
        consts = sb("consts", [128, NCF], F32)
        identb = sb("identb", [128, 128], BF16)
        trib = sb("trib", [128, 128], BF16)
        onesb = sb("onesb", [128, 128], BF16)
        modT = sb("modT", [128, 48], F32)
        avec = sb("avec", [128, 32], F32)
        posi = sb("posi", [128, NT], I32)
        posf = sb("posf", [128, NT], F32)
        ang = sb("ang", [128, NT, 8], F32)
        cosT = sb("cosT", [128, NT, 8], F32)
        sinT = sb("sinT", [128, NT, 8], F32)
        cact = sb("cact", [128, 8], F32)
        cactb = sb("cactb", [128, 8], BF16)
        poolW = sb("poolW", [128, 2, 128], BF16)
        kmT = sb("kmT", [64, 4, 8], BF16)
        kmf = sb("kmf", [64, 4], F32)
        arena = sb("arena", [128, ARENA // 4], F32)
        arena_b = arena.bitcast(BF16)
        banks = [st.enter_context(nc.psum_tensor(f"bank{i}", [128, 512], F32)) for i in range(8)]
        banks_b = [b.bitcast(BF16) for b in banks]

        class AV:
            def __init__(self, base, P, fshape, dt):
                self.base = base
                self.P = P
                self.fshape = list(fshape)
                self.dt = dt
                self.esz = 2 if dt == BF16 else 4
                n = int(np.prod(fshape))
                self.n = n
                assert base % 32 == 0 and base + n * self.esz <= ARENA, (base, n, self.esz)
                src = arena_b if dt == BF16 else arena
                e0 = base // self.esz
                flat = src[0:P, e0:e0 + n]
                if len(fshape) == 1:
                    self.ap = flat
                elif len(fshape) == 2:
                    self.ap = flat.rearrange("p (a b) -> p a b", a=fshape[0])
                elif len(fshape) == 3:
                    self.ap = flat.rearrange("p (a b c) -> p a b c", a=fshape[0], b=fshape[1])
                else:
                    raise ValueError

            def keys(self, lo=0, n=None):
                if n is None:
                    n = self.n - lo
                b0 = (self.base + lo * self.esz) // BLK
                b1 = (self.base + (lo + n) * self.esz - 1) // BLK
                return [("A", b) for b in range(b0, b1 + 1)]

            def kidx(self, *idx):
                strides = []
                s = 1
                for d in reversed(self.fshape):
                    strides.append(s)
                    s *= d
                strides = strides[::-1]
                lo = sum(i * st_ for i, st_ in zip(idx, strides))
                n = strides[len(idx) - 1] if idx else self.n
                return self.keys(lo, n)

            def krange(self, idx, lo, n):
                strides = []
                s = 1
                for d in reversed(self.fshape):
                    strides.append(s)
                    s *= d
                strides = strides[::-1]
                base = sum(i * st_ for i, st_ in zip(idx, strides))
                return self.keys(base + lo, n)

        bank_rr = {"g": [0, [0, 1, 2, 3]], "s": [0, [4, 5]], "a": [0, [6, 7]], "G": [0, [0, 1, 2, 3, 4, 5, 6, 7]]}

        def nbank(pool):
            st_ = bank_rr[pool]
            b = st_[1][st_[0] % len(st_[1])]
            st_[0] += 1
            return b

        def bk(b):
            return ("bank", b)

        ws_rr = [0]

        def wslot():
            i = ws_rr[0] % 4
            ws_rr[0] += 1
            S.new_gen(("ws", i))
            return i

        def wload(slot, dst_ap, src_ap):
            S.op("pool", lambda e: e.dma_start(out=dst_ap, in_=src_ap), joins=[("ws", slot)], dma=True)

        def kx(c, tc):
            return ("xT", c, tc)

        def kh(c, tc):
            return ("hT", c, tc)

        def tcs(tc):
            return slice(tc * 512, (tc + 1) * 512)

        evac_rr = [0]

        def evac_eng():
            evac_rr[0] += 1
            return "act" if evac_rr[0] % 2 else "dve"

        def copy_op(eng, out_ap, in_ap, reads, writes):
            if eng == "act":
                S.op("act", lambda e: e.copy(out=out_ap, in_=in_ap), reads=reads, writes=writes)
            else:
                S.op(eng, lambda e: e.tensor_copy(out=out_ap, in_=in_ap), reads=reads, writes=writes)

        S.op("sp", lambda e: e.dma_start(out=small[:], in_=small_d), writes=["small"], dma=True)
        S.op("sp", lambda e: e.dma_start(out=consts[:], in_=consts_d), writes=["consts"], dma=True)
        S.op("sp", lambda e: e.dma_start(out=posi[:], in_=pos_d), writes=["posi"], dma=True)
        S.op("dve", lambda e: e.tensor_copy(out=identb[:], in_=consts[:, O_IDF:O_IDF + 128]), reads=["consts"], writes=["identb"])
        S.op("dve", lambda e: e.tensor_copy(out=trib[:], in_=consts[:, O_TRI:O_TRI + 128]), reads=["consts"], writes=["trib"])
        S.op("dve", lambda e: e.memset(onesb[:], 1.0), writes=["onesb"])
        S.op("dve", lambda e: e.tensor_copy(out=posf[:], in_=posi[:]), reads=["posi"], writes=["posf"])
        S.op("dve", lambda e: e.tensor_tensor(out=ang[:], in0=posf[:].unsqueeze(2).to_broadcast([128, NT, 8]),
                                              in1=consts[:, O_FREQ:O_FREQ + 8].unsqueeze(1).to_broadcast([128, NT, 8]),
                                              op=ALU.mult), reads=["posf", "consts"], writes=["ang"])
        TWO_PI = 2.0 * math.pi
        ki = sb("ki", [128, NT, 8], I32)
        kf = sb("kf", [128, NT, 8], F32)
        rr = sb("rr", [128, NT, 8], F32)
        mm_ = sb("mm_", [128, NT, 8], F32)

        def sincos(dst, shift, nm):
            S.op("dve", lambda e: e.tensor_scalar(out=rr[:], in0=ang[:], scalar1=shift, scalar2=1.0 / TWO_PI, op0=ALU.add, op1=ALU.mult),
                 reads=["ang"], writes=["rr"])
            S.op("dve", lambda e: e.tensor_copy(out=ki[:], in_=rr[:]), reads=["rr"], writes=["ki"])
            S.op("dve", lambda e: e.tensor_copy(out=kf[:], in_=ki[:]), reads=["ki"], writes=["kf"])
            S.op("dve", lambda e: e.tensor_scalar(out=rr[:], in0=ang[:], scalar1=shift, scalar2=None, op0=ALU.add),
                 reads=["ang", "kf"], writes=["rr"])
            S.op("dve", lambda e: e.scalar_tensor_tensor(out=rr[:], in0=kf[:], scalar=-TWO_PI, in1=rr[:], op0=ALU.mult, op1=ALU.add),
                 reads=["kf", "rr"], writes=["rr"])
            S.op("dve", lambda e: e.tensor_scalar(out=mm_[:], in0=rr[:], scalar1=math.pi, scalar2=-TWO_PI, op0=ALU.is_gt, op1=ALU.mult),
                 reads=["rr"], writes=["mm_"])
            S.op("dve", lambda e: e.tensor_tensor(out=rr[:], in0=rr[:], in1=mm_[:], op=ALU.add), reads=["rr", "mm_"], writes=["rr"])
            S.op("dve", lambda e: e.tensor_scalar(out=mm_[:], in0=rr[:], scalar1=-math.pi, scalar2=TWO_PI, op0=ALU.is_lt, op1=ALU.mult),
                 reads=["rr"], writes=["mm_"])
            S.op("dve", lambda e: e.tensor_tensor(out=rr[:], in0=rr[:], in1=mm_[:], op=ALU.add), reads=["rr", "mm_"], writes=["rr"])
            S.op("act", lambda e: e.activation(out=dst[:], in_=rr[:], func=AF.Sin, scale=1.0 - 1e-6), reads=["rr"], writes=[nm])

        sincos(sinT, 0.0, "sinT")
        sincos(cosT, 0.5 * math.pi, "cosT")
        S.op("act", lambda e: e.activation(out=cact[:], in_=small[:, O_CT:O_CT + 8], func=AF.Silu), reads=["small"], writes=["cact"])
        S.op("dve", lambda e: e.tensor_copy(out=cactb[:], in_=cact[:]), reads=["cact"], writes=["cactb"])

        xin = [AV(0, 128, [4, D], F32), AV(16384, 128, [4, D], F32)]
        for tc in range(NTC):
            xv = xin[tc % 2]
            S.op("sp", (lambda e, xv=xv, tc=tc: e.dma_start(
                out=xv.ap, in_=x_d[tc * 512:(tc + 1) * 512, :].rearrange("(t p) d -> p t d", p=128))),
                writes=xv.keys(), dma=True)
            for c in range(8):
                b = nbank("G")
                for t in range(4):
                    S.op("pe", (lambda e, b=b, xv=xv, t=t, c=c: e.transpose(
                        banks[b][:, t * 128:(t + 1) * 128], xv.ap[:, t, c * 128:(c + 1) * 128], consts[:, O_IDF:O_IDF + 128])),
                        reads=xv.keys() + ["consts"], writes=[bk(b)])
                copy_op(evac_eng(), xT[:, c, tcs(tc)], banks[b][:], [bk(b)], [kx(c, tc)])

        def modulation(l):
            pm = nbank("G")
            for j in range(12):
                s = wslot()
                wv = wsl[s][:, :].rearrange("p (k n) -> p k n", k=8)
                wload(s, wv, w_ada_d[l][:, j * 512:(j + 1) * 512].rearrange("(kc p) n -> p kc n", p=128))
                for oc in range(4):
                    col = j * 4 + oc
                    for kc in range(8):
                        S.op("pe", (lambda e, wv=wv, oc=oc, kc=kc, col=col, pm=pm: e.matmul(
                            banks[pm][:, col:col + 1], lhsT=wv[:, kc, oc * 128:(oc + 1) * 128], rhs=cactb[:, kc:kc + 1],
                            start=(kc == 0), stop=(kc == 7), skip_group_check=True)),
                            reads=[("ws", s), "cactb"], writes=[bk(pm)])
            S.op("dve", (lambda e, pm=pm, l=l: e.tensor_tensor(out=modT[:], in0=banks[pm][:, 0:48],
                                                               in1=small[:, O_BADA + l * 48:O_BADA + (l + 1) * 48], op=ALU.add)),
                 reads=[bk(pm), "small"], writes=["modT"])
            S.op("dve", (lambda e, l=l: e.scalar_tensor_tensor(out=avec[:, 0:8], in0=modT[:, 8:16], scalar=1.0,
                                                               in1=small[:, O_N1W + l * 8:O_N1W + (l + 1) * 8],
                                                               op0=ALU.add, op1=ALU.mult)),
                 reads=["modT", "small"], writes=["avec"])
            S.op("dve", (lambda e, l=l: e.scalar_tensor_tensor(out=avec[:, 8:16], in0=modT[:, 32:40], scalar=1.0,
                                                               in1=small[:, O_N2W + l * 8:O_N2W + (l + 1) * 8],
                                                               op0=ALU.add, op1=ALU.mult)),
                 reads=["modT", "small"], writes=["avec"])

        def norm_phase(a_off, sh_off):
            sqb = AV(0, 128, [8, 512], BF16)
            rst = [AV(8192, 128, [512], F32), AV(10240, 128, [512], F32)]
            tmp = [AV(12288, 128, [512], F32), AV(14336, 128, [512], F32), AV(16384, 128, [512], F32)]
            ti = 0
            for tc in range(NTC):
                for c in range(8):
                    if c % 2 == 0:
                        S.op("act", (lambda e, c=c, tc=tc: e.activation(out=sqb.ap[:, c, :], in_=xT[:, c, tcs(tc)], func=AF.Square)),
                             reads=[kx(c, tc)], writes=sqb.kidx(c))
                    else:
                        S.op("dve", (lambda e, c=c, tc=tc: e.tensor_tensor(out=sqb.ap[:, c, :], in0=xT[:, c, tcs(tc)],
                                                                           in1=xT[:, c, tcs(tc)], op=ALU.mult)),
                             reads=[kx(c, tc)], writes=sqb.kidx(c))
                b = nbank("G")
                for c in range(8):
                    S.op("pe", (lambda e, b=b, c=c: e.matmul(banks[b][:], lhsT=onesb[:], rhs=sqb.ap[:, c, :],
                                                            start=(c == 0), stop=(c == 7))),
                         reads=sqb.kidx(c) + ["onesb"], writes=[bk(b)])
                r = rst[tc % 2]
                S.op("act", (lambda e, b=b, r=r: e.activation(out=r.ap, in_=banks[b][:], func=AF.Sqrt, bias=epsb[:, 0:1], scale=1.0 / D)),
                     reads=[bk(b), "epsb"], writes=r.keys())
                S.op("dve", (lambda e, r=r: e.reciprocal(out=r.ap, in_=r.ap)), reads=r.keys(), writes=r.keys())
                for c in range(8):
                    t_ = tmp[ti % 3]
                    ti += 1
                    S.op("dve", (lambda e, t_=t_, r=r, c=c, tc=tc: e.tensor_tensor(out=t_.ap, in0=xT[:, c, tcs(tc)], in1=r.ap, op=ALU.mult)),
                         reads=[kx(c, tc)] + r.keys(), writes=t_.keys())
                    S.op("act", (lambda e, t_=t_, c=c, tc=tc: e.activation(
                        out=hT[:, c, tcs(tc)], in_=t_.ap, func=AF.Identity,
                        scale=avec[:, a_off + c:a_off + c + 1], bias=modT[:, sh_off + c:sh_off + c + 1])),
                        reads=t_.keys() + ["avec", "modT"], writes=[kh(c, tc)])

        epsb = sb("epsb", [128, 1], F32)
        S.op("dve", lambda e: e.memset(epsb[:], EPS), writes=["epsb"])

        yattnT = AV(0, 128, [4, S_], BF16)
        ypoolT = AV(16384, 128, [2, S_], BF16)
        yconvT = AV(24576, 128, [2, S_], BF16)
        A1 = 32768
        KaT = AV(A1, 72, [4, S_], BF16)
        Va = AV(A1 + 16384, 128, [NT, 4, 65], BF16)
        QaT = AV(A1 + 24736, 72, [4, 512], BF16)
        o_ = A1 + 24736 + 4096
        qk_tok = [AV(o_, 128, [8, 72], BF16), AV(o_ + 1152, 128, [8, 72], BF16)]
        o_ += 2304
        mb_tok = [AV(o_, 128, [4, 72], BF16), AV(o_ + 576, 128, [4, 72], BF16)]
        o_ += 1152
        sq_s = AV(16384, 128, [512], F32)
        qn_s = [AV(18432, 128, [8, 64], F32), AV(20480, 128, [8, 64], F32)]
        ytok = AV(22528, 128, [4, 256], BF16)
        pT = [AV(24576, 128, [512], BF16), AV(25600, 128, [512], BF16), AV(26624, 128, [512], BF16)]
        ssq8 = AV(27648, 128, [8], F32)
        rs8 = AV(27680, 128, [8], F32)
        gsb = AV(27712, 128, [4, 8], F32)
        cmp_s = AV(27840, 128, [4, 8, 8], F32)
        rank_s = AV(28864, 128, [4, 8], F32)
        rec_s = AV(28992, 128, [4], F32)
        rt = [AV(29056 + i * 256, 128, [8, 8], F32) for i in range(4)]
        qkw_bc_off = O_QKW

        def attention_phase(l):
            for hg in range(2):
                sqk = wslot()
                wqk = wsl[sqk][:, :].rearrange("p (k n) -> p k n", k=8)
                wload(sqk, wqk[:, :, 0:256], w_in_d[l][:, 1024 + hg * 256:1024 + (hg + 1) * 256].rearrange("(kc p) n -> p kc n", p=128))
                wload(sqk, wqk[:, :, 256:512], w_in_d[l][:, 1536 + hg * 256:1536 + (hg + 1) * 256].rearrange("(kc p) n -> p kc n", p=128))
                sv = wslot()
                wv = wsl[sv][:, 0:2048].rearrange("p (k n) -> p k n", k=8)
                wload(sv, wv, w_in_d[l][:, 2048 + hg * 256:2048 + (hg + 1) * 256].rearrange("(kc p) n -> p kc n", p=128))
                S.op("dve", lambda e: e.memset(Va.ap[:, :, :, 64:65], 1.0), writes=Va.keys())
                for m_ in mb_tok:
                    S.op("dve", (lambda e, m_=m_: e.memset(m_.ap[:, :, 0:64], 0.0)), writes=m_.keys())
                for qc in range(NTC):
                    for j in range(4):
                        t = qc * 4 + j
                        blk = t // 2
                        bq = nbank("g")
                        for kc in range(8):
                            S.op("pe", (lambda e, bq=bq, kc=kc, t=t, wqk=wqk: e.matmul(
                                banks[bq][:], lhsT=hT[:, kc, t * 128:(t + 1) * 128], rhs=wqk[:, kc, :],
                                start=(kc == 0), stop=(kc == 7))),
                                reads=[kh(kc, t // 4), ("ws", sqk)], writes=[bk(bq)])
                        bv = nbank("g")
                        for kc in range(8):
                            S.op("pe", (lambda e, bv=bv, kc=kc, t=t, wv=wv: e.matmul(
                                banks[bv][:, 0:256], lhsT=hT[:, kc, t * 128:(t + 1) * 128], rhs=wv[:, kc, :],
                                start=(kc == 0), stop=(kc == 7))),
                                reads=[kh(kc, t // 4), ("ws", sv)], writes=[bk(bv)])
                        S.op("act", (lambda e, bv=bv, t=t: e.copy(
                            out=Va.ap[:, t, :, 0:64], in_=banks[bv][:, 0:256].rearrange("p (h d) -> p h d", h=4))),
                            reads=[bk(bv)], writes=Va.kidx(t))
                        S.op("act", (lambda e, bq=bq: e.activation(out=sq_s.ap, in_=banks[bq][:], func=AF.Square)),
                             reads=[bk(bq)], writes=sq_s.keys())
                        S.op("dve", lambda e: e.tensor_reduce(out=ssq8.ap, in_=sq_s.ap.rearrange("p (h d) -> p h d", h=8),
                                                              axis=AX.X, op=ALU.add),
                             reads=sq_s.keys(), writes=ssq8.keys())
                        S.op("act", lambda e: e.activation(out=rs8.ap, in_=ssq8.ap, func=AF.Sqrt, bias=epsb[:, 0:1], scale=1.0 / 64),
                             reads=ssq8.keys() + ["epsb"], writes=rs8.keys())
                        S.op("dve", lambda e: e.reciprocal(out=rs8.ap, in_=rs8.ap), reads=rs8.keys(), writes=rs8.keys())
                        qn = qn_s[t % 2]
                        S.op("dve", (lambda e, bq=bq, qn=qn: e.tensor_tensor(
                            out=qn.ap, in0=banks[bq][:].rearrange("p (h d) -> p h d", h=8),
                            in1=rs8.ap.unsqueeze(2).to_broadcast([128, 8, 64]), op=ALU.mult)),
                            reads=[bk(bq)] + rs8.keys(), writes=qn.keys())
                        S.op("dve", (lambda e, qn=qn, l=l: e.tensor_tensor(
                            out=qn.ap.rearrange("p (a h) d -> p a h d", a=2), in0=qn.ap.rearrange("p (a h) d -> p a h d", a=2),
                            in1=small[:, O_QKW + l * 128:O_QKW + (l + 1) * 128].rearrange("p (a d) -> p a d", a=2)
                            .unsqueeze(2).to_broadcast([128, 2, 4, 64]), op=ALU.mult)),
                            reads=qn.keys() + ["small"], writes=qn.keys())
                        qt = qk_tok[t % 2]
                        S.op("act", (lambda e, qn=qn, qt=qt: e.copy(out=qt.ap[:, :, 16:64], in_=qn.ap[:, :, 16:64])),
                             reads=qn.keys(), writes=qt.keys())
                        cb = cosT[:, t, :].unsqueeze(1).to_broadcast([128, 8, 8])
                        sbb = sinT[:, t, :].unsqueeze(1).to_broadcast([128, 8, 8])
                        x1 = qn.ap[:, :, 0:8]
                        x2 = qn.ap[:, :, 8:16]
                        S.op("dve", (lambda e, x1=x1, cb=cb: e.tensor_tensor(out=rt[0].ap, in0=x1, in1=cb, op=ALU.mult)),
                             reads=qn.keys() + ["cosT"], writes=rt[0].keys())
                        S.op("dve", (lambda e, x2=x2, sbb=sbb: e.tensor_tensor(out=rt[1].ap, in0=x2, in1=sbb, op=ALU.mult)),
                             reads=qn.keys() + ["sinT"], writes=rt[1].keys())
                        S.op("dve", (lambda e, qt=qt: e.tensor_tensor(out=qt.ap[:, :, 0:8], in0=rt[0].ap, in1=rt[1].ap, op=ALU.subtract)),
                             reads=rt[0].keys() + rt[1].keys(), writes=qt.keys())
                        S.op("dve", (lambda e, x2=x2, cb=cb: e.tensor_tensor(out=rt[2].ap, in0=x2, in1=cb, op=ALU.mult)),
                             reads=qn.keys() + ["cosT"], writes=rt[2].keys())
                        S.op("dve", (lambda e, x1=x1, sbb=sbb: e.tensor_tensor(out=rt[3].ap, in0=x1, in1=sbb, op=ALU.mult)),
                             reads=qn.keys() + ["sinT"], writes=rt[3].keys())
                        S.op("dve", (lambda e, qt=qt: e.tensor_tensor(out=qt.ap[:, :, 8:16], in0=rt[2].ap, in1=rt[3].ap, op=ALU.add)),
                             reads=rt[2].keys() + rt[3].keys(), writes=qt.keys())
                        S.op("dve", (lambda e, qt=qt: e.memset(qt.ap[:, 4:8, 64:72], 0.0)), writes=qt.keys())
                        S.op("dve", (lambda e, qt=qt, blk=blk: e.memset(qt.ap[:, 4:8, 64 + blk:65 + blk], 1.0)), writes=qt.keys())
                        btr = nbank("g")
                        for h in range(4):
                            S.op("pe", (lambda e, btr=btr, qt=qt, h=h: e.transpose(
                                banks_b[btr][0:72, h * 128:(h + 1) * 128], qt.ap[:, 4 + h, 0:72], identb[:])),
                                reads=qt.keys() + ["identb"], writes=[bk(btr)])
                            S.op("pe", (lambda e, btr=btr, qt=qt, h=h: e.transpose(
                                banks_b[btr][0:64, 512 + h * 128:512 + (h + 1) * 128], qt.ap[:, h, 0:64], identb[:])),
                                reads=qt.keys() + ["identb"], writes=[bk(btr)])
                        S.op("act", (lambda e, btr=btr, t=t: e.copy(
                            out=KaT.ap[:, :, t * 128:(t + 1) * 128], in_=banks_b[btr][0:72, 0:512].rearrange("p (h n) -> p h n", h=4))),
                            reads=[bk(btr)], writes=[k_ for h in range(4) for k_ in KaT.krange((h,), t * 128, 128)])
                        S.op("dve", (lambda e, btr=btr, j=j: e.tensor_copy(
                            out=QaT.ap[0:64, :, j * 128:(j + 1) * 128], in_=banks_b[btr][0:64, 512:1024].rearrange("p (h n) -> p h n", h=4))),
                            reads=[bk(btr)], writes=QaT.keys())
                        if t % 2 == 1 and blk < 7:
                            S.op("dve", (lambda e, blk=blk: e.tensor_reduce(
                                out=kmf[:], in_=KaT.ap[0:64, :, blk * 256:(blk + 1) * 256], axis=AX.X, op=ALU.add)),
                                reads=[k_ for h in range(4) for k_ in KaT.krange((h,), blk * 256, 256)], writes=["kmf"])
                            S.op("dve", (lambda e, blk=blk: e.tensor_scalar(
                                out=kmT[:, :, blk:blk + 1], in0=kmf[:].unsqueeze(2), scalar1=1.0 / 256, scalar2=None, op0=ALU.mult)),
                                reads=["kmf"], writes=["kmT"])
                        mb = mb_tok[t % 2]
                        S.op("dve", (lambda e, mb=mb: e.memset(mb.ap[:, :, 64:72], NEGB)), writes=mb.keys())
                        S.op("dve", (lambda e, mb=mb, blk=blk: e.memset(mb.ap[:, :, 64:65 + blk], 0.0)), writes=mb.keys())
                        if blk >= 4:
                            bg = nbank("g")
                            for h in range(4):
                                S.op("pe", (lambda e, bg=bg, h=h, j=j, blk=blk: e.matmul(
                                    banks[bg][:, h * 8:h * 8 + blk], lhsT=QaT.ap[0:64, h, j * 128:(j + 1) * 128],
                                    rhs=kmT[:, h, 0:blk], start=True, stop=True, skip_group_check=True)),
                                    reads=QaT.keys() + ["kmT"], writes=[bk(bg)])
                            S.op("act", (lambda e, bg=bg, blk=blk: e.copy(
                                out=gsb.ap[:, :, 0:blk], in_=banks[bg][:, 0:32].rearrange("p (h n) -> p h n", h=4)[:, :, 0:blk])),
                                reads=[bk(bg)], writes=gsb.keys())
                            S.op("dve", (lambda e, blk=blk: e.tensor_tensor(
                                out=cmp_s.ap[:, :, 0:blk, 0:blk],
                                in0=gsb.ap[:, :, 0:blk].unsqueeze(2).to_broadcast([128, 4, blk, blk]),
                                in1=gsb.ap[:, :, 0:blk].unsqueeze(3).to_broadcast([128, 4, blk, blk]), op=ALU.is_gt)),
                                reads=gsb.keys(), writes=cmp_s.keys())
                            S.op("dve", (lambda e, blk=blk: e.tensor_reduce(
                                out=rank_s.ap[:, :, 0:blk], in_=cmp_s.ap[:, :, 0:blk, 0:blk], axis=AX.X, op=ALU.add)),
                                reads=cmp_s.keys(), writes=rank_s.keys())
                            S.op("dve", (lambda e, mb=mb, blk=blk: e.tensor_scalar(
                                out=mb.ap[:, :, 64:64 + blk], in0=rank_s.ap[:, :, 0:blk], scalar1=2.5, scalar2=NEGB,
                                op0=ALU.is_gt, op1=ALU.mult)),
                                reads=rank_s.keys(), writes=mb.keys())
                        bm = nbank("g")
                        for h in range(4):
                            S.op("pe", (lambda e, bm=bm, mb=mb, h=h: e.transpose(
                                banks_b[bm][0:72, h * 128:(h + 1) * 128], mb.ap[:, h, 0:72], identb[:])),
                                reads=mb.keys() + ["identb"], writes=[bk(bm)])
                        S.op("act", (lambda e, bm=bm, j=j: e.copy(
                            out=QaT.ap[64:72, :, j * 128:(j + 1) * 128], in_=banks_b[bm][64:72, 0:512].rearrange("p (h n) -> p h n", h=4))),
                            reads=[bk(bm)], writes=QaT.keys())
                    nk = 4 * (qc + 1)
                    pi = 0
                    for h in range(4):
                        ba = nbank("a")
                        accv = banks[ba][:].rearrange("p (j n) -> p j n", j=4)
                        for kt in range(nk):
                            j0 = max(0, kt - 4 * qc)
                            bs = nbank("s")
                            S.op("pe", (lambda e, bs=bs, h=h, kt=kt, j0=j0: e.matmul(
                                banks[bs][:, j0 * 128:512], lhsT=KaT.ap[0:72, h, kt * 128:(kt + 1) * 128],
                                rhs=QaT.ap[0:72, h, j0 * 128:512], start=True, stop=True)),
                                reads=KaT.krange((h,), kt * 128, 128) + QaT.keys(), writes=[bk(bs)])
                            p_ = pT[pi % 3]
                            pi += 1
                            S.op("act", (lambda e, bs=bs, p_=p_, j0=j0: e.activation(
                                out=p_.ap[:, j0 * 128:512], in_=banks[bs][:, j0 * 128:512], func=AF.Exp, scale=0.125)),
                                reads=[bk(bs)], writes=p_.keys())
                            if kt >= 4 * qc:
                                S.op("dve", (lambda e, p_=p_, j0=j0: e.tensor_tensor(
                                    out=p_.ap[:, j0 * 128:(j0 + 1) * 128], in0=p_.ap[:, j0 * 128:(j0 + 1) * 128], in1=trib[:], op=ALU.mult)),
                                    reads=p_.keys() + ["trib"], writes=p_.keys())
                            for jj in range(j0, 4):
                                S.op("pe", (lambda e, accv=accv, p_=p_, jj=jj, kt=kt, h=h, qc=qc: e.matmul(
                                    accv[:, jj, 0:65], lhsT=p_.ap[:, jj * 128:(jj + 1) * 128], rhs=Va.ap[:, kt, h, :],
                                    start=(kt == 0 and jj == 0), stop=(kt == 4 * qc + jj), skip_group_check=True)),
                                    reads=p_.keys() + Va.kidx(kt), writes=[bk(ba)])
                        S.op("dve", (lambda e, accv=accv: e.reciprocal(out=rec_s.ap, in_=accv[:, :, 64])),
                             reads=[bk(ba)], writes=rec_s.keys())
                        S.op("dve", (lambda e, accv=accv, h=h: e.tensor_tensor(
                            out=ytok.ap[:, :, h * 64:(h + 1) * 64], in0=accv[:, :, 0:64],
                            in1=rec_s.ap.unsqueeze(2).to_broadcast([128, 4, 64]), op=ALU.mult)),
                            reads=[bk(ba)] + rec_s.keys(), writes=ytok.keys())
                    by = nbank("g")
                    for jj in range(4):
                        for cc in range(2):
                            S.op("pe", (lambda e, by=by, jj=jj, cc=cc: e.transpose(
                                banks_b[by][:, (cc * 4 + jj) * 128:(cc * 4 + jj + 1) * 128], ytok.ap[:, jj, cc * 128:(cc + 1) * 128], identb[:])),
                                reads=ytok.keys() + ["identb"], writes=[bk(by)])
                    for cc in range(2):
                        copy_op("act" if cc == 0 else "dve", yattnT.ap[:, hg * 2 + cc, qc * 512:(qc + 1) * 512],
                                banks_b[by][:, cc * 512:(cc + 1) * 512], [bk(by)], yattnT.krange((hg * 2 + cc,), qc * 512, 512))

        PW = S_ + 16
        up_s = AV(A1, 128, [PW], F32)
        sA_s = AV(A1 + 8256, 128, [PW], F32)
        sB_s = AV(A1 + 16512, 128, [PW], F32)

        def fm_proj(wv_of_kc, b, tc):
            for kc in range(8):
                lhsT, rk = wv_of_kc(kc)
                S.op("pe", (lambda e, lhsT=lhsT, kc=kc, b=b, tc=tc: e.matmul(
                    banks[b][:], lhsT=lhsT, rhs=hT[:, kc, tcs(tc)], start=(kc == 0), stop=(kc == 7))),
                    reads=[kh(kc, tc)] + rk, writes=[bk(b)])

        def pool_phase(l):
            S.op("dve", lambda e: e.memset(poolW[:], 0.0), writes=[("poolW", g) for g in range(4)])
            for g in range(4):
                r0 = (g % 2) * 64
                S.op("pool", (lambda e, g=g, r0=r0, l=l: e.dma_start(out=poolW[r0:r0 + 64, g // 2, r0:r0 + 64], in_=pool_w_d[l, g])),
                     writes=[("poolW", g)], dma=True)
            s = wslot()
            wv = wsl[s][:, 0:2048].rearrange("p (k n) -> p k n", k=8)
            wload(s, wv, w_in_d[l][:, 0:256].rearrange("(kc p) n -> p kc n", p=128))
            for v_ in (up_s, sA_s, sB_s):
                S.op("dve", (lambda e, v_=v_: e.memset(v_.ap[:, 0:16], 0.0)), writes=v_.keys(0, 16))
            for cc in range(2):
                for tc in range(NTC):
                    b = nbank("G")
                    fm_proj(lambda kc, cc=cc: (wv[:, kc, cc * 128:(cc + 1) * 128], [("ws", s)]), b, tc)
                    copy_op(evac_eng(), up_s.ap[:, 16 + tc * 512:16 + (tc + 1) * 512], banks[b][:], [bk(b)],
                            up_s.keys(16 + tc * 512, 512))
                n_lv = 2 if cc == 0 else 4
                cur = up_s
                nxt = [sA_s, sB_s]
                srcs = {}
                for lv in range(1, n_lv + 1):
                    sh = 1 << (lv - 1)
                    dst = nxt[(lv - 1) % 2]
                    p0 = 64 if lv == n_lv else 0
                    S.op("dve", (lambda e, dst=dst, cur=cur, sh=sh, p0=p0: e.tensor_tensor(
                        out=dst.ap[p0:128, 16:PW], in0=cur.ap[p0:128, 16:PW], in1=cur.ap[p0:128, 16 - sh:PW - sh], op=ALU.add)),
                        reads=cur.keys(), writes=dst.keys(16, S_))
                    srcs[lv] = dst
                    cur = dst
                lo_src = srcs[n_lv - 1]
                hi_src = srcs[n_lv]
                pooledT = yconvT
                for (p0, p1, src) in ((0, 64, lo_src), (64, 128, hi_src)):
                    S.op("dve", (lambda e, p0=p0, p1=p1, src=src, cc=cc: e.tensor_tensor(
                        out=src.ap[p0:p1, 16:32], in0=src.ap[p0:p1, 16:32],
                        in1=consts[p0:p1, O_CORR + cc * 16:O_CORR + (cc + 1) * 16], op=ALU.mult)),
                        reads=src.keys() + ["consts"], writes=src.keys(16, 16))
                    S.op("dve", (lambda e, p0=p0, p1=p1, src=src, cc=cc: e.tensor_tensor(
                        out=pooledT.ap[p0:p1, cc, 0:16], in0=src.ap[p0:p1, 16:32], in1=up_s.ap[p0:p1, 16:32], op=ALU.subtract)),
                        reads=src.keys() + up_s.keys(), writes=pooledT.krange((cc,), 0, 16))
                    S.op("dve", (lambda e, p0=p0, p1=p1, src=src, cc=cc: e.scalar_tensor_tensor(
                        out=pooledT.ap[p0:p1, cc, 16:S_], in0=src.ap[p0:p1, 32:PW], scalar=consts[p0:p1, O_INVW + cc:O_INVW + cc + 1],
                        in1=up_s.ap[p0:p1, 32:PW], op0=ALU.mult, op1=ALU.subtract)),
                        reads=src.keys() + up_s.keys() + ["consts"], writes=pooledT.kidx(cc))
                for tc in range(NTC):
                    b = nbank("G")
                    S.op("pe", (lambda e, b=b, cc=cc, tc=tc: e.matmul(banks[b][:], lhsT=poolW[:, cc, :], rhs=pooledT.ap[:, cc, tcs(tc)],
                                                                      start=True, stop=True)),
                         reads=[("poolW", 2 * cc), ("poolW", 2 * cc + 1)] + pooledT.krange((cc,), tc * 512, 512), writes=[bk(b)])
                    S.op("act", (lambda e, b=b, cc=cc, tc=tc, l=l: e.activation(
                        out=ypoolT.ap[:, cc, tcs(tc)], in_=banks[b][:], func=AF.Identity,
                        scale=small[:, O_PSC + l * 2 + cc:O_PSC + l * 2 + cc + 1])),
                        reads=[bk(b), "small"], writes=ypoolT.krange((cc,), tc * 512, 512))

        def conv_phase(l):
            cu_s, cc_s, ac_s = up_s, sA_s, sB_s
            for cc in range(2):
                s = wslot()
                wv = wsl[s][:, 0:3072].rearrange("p (k i n) -> p k i n", k=8, i=3)
                for i in range(3):
                    c0 = 256 + i * 256 + cc * 128
                    wload(s, wv[:, :, i, :], w_in_d[l][:, c0:c0 + 128].rearrange("(kc p) n -> p kc n", p=128))
                S.op("dve", lambda e: e.memset(cu_s.ap[:, 0:16], 0.0), writes=cu_s.keys(0, 16))
                for tc in range(NTC):
                    b = nbank("G")
                    fm_proj(lambda kc: (wv[:, kc, 0, :], [("ws", s)]), b, tc)
                    copy_op("act", cu_s.ap[:, 16 + tc * 512:16 + (tc + 1) * 512], banks[b][:], [bk(b)], cu_s.keys(16 + tc * 512, 512))
                    b2 = nbank("G")
                    fm_proj(lambda kc: (wv[:, kc, 2, :], [("ws", s)]), b2, tc)
                    S.op("dve", (lambda e, b2=b2, tc=tc: e.tensor_tensor(
                        out=cu_s.ap[:, 16 + tc * 512:16 + (tc + 1) * 512], in0=banks[b2][:],
                        in1=cu_s.ap[:, 16 + tc * 512:16 + (tc + 1) * 512], op=ALU.mult)),
                        reads=[bk(b2)] + cu_s.keys(16 + tc * 512, 512), writes=cu_s.keys(16 + tc * 512, 512))
                cw = lambda k, cc=cc, l=l: small[:, O_CW + l * 6 + k * 2 + cc:O_CW + l * 6 + k * 2 + cc + 1]
                S.op("dve", (lambda e, cw=cw: e.tensor_scalar(out=ac_s.ap[:, 16:PW], in0=cu_s.ap[:, 16:PW], scalar1=cw(2), scalar2=None, op0=ALU.mult)),
                     reads=cu_s.keys() + ["small"], writes=ac_s.keys(16, S_))
                S.op("dve", (lambda e, cw=cw: e.scalar_tensor_tensor(out=ac_s.ap[:, 16:PW], in0=cu_s.ap[:, 15:PW - 1], scalar=cw(1),
                                                                     in1=ac_s.ap[:, 16:PW], op0=ALU.mult, op1=ALU.add)),
                     reads=cu_s.keys() + ac_s.keys() + ["small"], writes=ac_s.keys(16, S_))
                S.op("dve", (lambda e, cw=cw: e.scalar_tensor_tensor(out=ac_s.ap[:, 16:PW], in0=cu_s.ap[:, 14:PW - 2], scalar=cw(0),
                                                                     in1=ac_s.ap[:, 16:PW], op0=ALU.mult, op1=ALU.add)),
                     reads=cu_s.keys() + ac_s.keys() + ["small"], writes=ac_s.keys(16, S_))
                for tc in range(NTC):
                    b = nbank("G")
                    fm_proj(lambda kc: (wv[:, kc, 1, :], [("ws", s)]), b, tc)
                    S.op("dve", (lambda e, b=b, cc=cc, tc=tc: e.tensor_tensor(
                        out=yconvT.ap[:, cc, tcs(tc)], in0=banks[b][:], in1=ac_s.ap[:, 16 + tc * 512:16 + (tc + 1) * 512], op=ALU.mult)),
                        reads=[bk(b)] + ac_s.keys(16 + tc * 512, 512), writes=yconvT.krange((cc,), tc * 512, 512))

        merged = AV(A1, 128, [8, 1024], BF16)
        mo = A1 + 16384
        s_t = [AV(mo, 128, [512], F32), AV(mo + 2048, 128, [512], F32)]
        m_t = [AV(mo + 4096, 128, [512], F32), AV(mo + 6144, 128, [512], F32)]
        t_t = [AV(mo + 8192, 128, [512], F32), AV(mo + 10240, 128, [512], F32)]

        def merge_phase(l):
            ysrc = [(ypoolT, 2, 0), (yconvT, 2, 2), (yattnT, 4, 4)]
            pds = [p_pool_d, p_conv_d, p_attn_d]
            cnt = 0
            for hh in range(2):
                for c in range(8):
                    s = wslot()
                    wg = wsl[s][:, 0:3072].rearrange("p (k i n) -> p k i n", k=8, i=3)
                    wp = wsl[s][:, 3072:4096].rearrange("p (k n) -> p k n", k=8)
                    for i in range(3):
                        c0 = 2560 + i * 1024 + c * 128
                        wload(s, wg[:, :, i, :], w_in_d[l][:, c0:c0 + 128].rearrange("(kc p) n -> p kc n", p=128))
                    wload(s, wp[:, 0:2, :], p_pool_d[l][:, c * 128:(c + 1) * 128].rearrange("(kc p) n -> p kc n", p=128))
                    wload(s, wp[:, 2:4, :], p_conv_d[l][:, c * 128:(c + 1) * 128].rearrange("(kc p) n -> p kc n", p=128))
                    wload(s, wp[:, 4:8, :], p_attn_d[l][:, c * 128:(c + 1) * 128].rearrange("(kc p) n -> p kc n", p=128))
                    for th in range(2):
                        tc = hh * 2 + th
                        mt = m_t[cnt % 2]
                        for i in range(3):
                            bgt = nbank("G")
                            fm_proj(lambda kc, i=i: (wg[:, kc, i, :], [("ws", s)]), bgt, tc)
                            st_ = s_t[(cnt * 3 + i) % 2]
                            S.op("act", (lambda e, bgt=bgt, st_=st_: e.activation(out=st_.ap, in_=banks[bgt][:], func=AF.Tanh, scale=0.5)),
                                 reads=[bk(bgt)], writes=st_.keys())
                            ysv, nkk, koff = ysrc[i]
                            bp = nbank("G")
                            for kk in range(nkk):
                                S.op("pe", (lambda e, bp=bp, kk=kk, koff=koff, ysv=ysv, tc=tc, nkk=nkk, wp=wp: e.matmul(
                                    banks[bp][:], lhsT=wp[:, koff + kk, :], rhs=ysv.ap[:, kk, tcs(tc)], start=(kk == 0), stop=(kk == nkk - 1))),
                                    reads=[("ws", s)] + ysv.krange((kk,), tc * 512, 512), writes=[bk(bp)])
                            if i == 0:
                                S.op("dve", (lambda e, bp=bp, st_=st_, mt=mt: e.scalar_tensor_tensor(
                                    out=mt.ap, in0=st_.ap, scalar=1.0, in1=banks[bp][:], op0=ALU.add, op1=ALU.mult)),
                                    reads=[bk(bp)] + st_.keys(), writes=mt.keys())
                            else:
                                tt_ = t_t[(cnt * 3 + i) % 2]
                                S.op("dve", (lambda e, bp=bp, st_=st_, tt_=tt_: e.scalar_tensor_tensor(
                                    out=tt_.ap, in0=st_.ap, scalar=1.0, in1=banks[bp][:], op0=ALU.add, op1=ALU.mult)),
                                    reads=[bk(bp)] + st_.keys(), writes=tt_.keys())
                                if i == 1:
                                    S.op("dve", (lambda e, tt_=tt_, mt=mt: e.tensor_tensor(out=mt.ap, in0=mt.ap, in1=tt_.ap, op=ALU.add)),
                                         reads=mt.keys() + tt_.keys(), writes=mt.keys())
                                else:
                                    S.op("dve", (lambda e, tt_=tt_, mt=mt, c=c, th=th: e.tensor_tensor(
                                        out=merged.ap[:, c, th * 512:(th + 1) * 512], in0=mt.ap, in1=tt_.ap, op=ALU.add)),
                                        reads=mt.keys() + tt_.keys(), writes=merged.krange((c,), th * 512, 512))
                        cnt += 1
                for ob in range(2):
                    s = wslot()
                    wo = wsl[s][:, :].rearrange("p (k n) -> p k n", k=8)
                    wload(s, wo, w_out_d[l][:, ob * 512:(ob + 1) * 512].rearrange("(kc p) n -> p kc n", p=128))
                    for oo in range(4):
                        o = ob * 4 + oo
                        for th in range(2):
                            tc = hh * 2 + th
                            b = nbank("G")
                            for c in range(8):
                                S.op("pe", (lambda e, b=b, c=c, oo=oo, th=th, wo=wo: e.matmul(
                                    banks[b][:], lhsT=wo[:, c, oo * 128:(oo + 1) * 128], rhs=merged.ap[:, c, th * 512:(th + 1) * 512],
                                    start=(c == 0), stop=(c == 7))),
                                    reads=[("ws", s)] + merged.krange((c,), th * 512, 512), writes=[bk(b)])
                            S.op("dve", (lambda e, b=b, o=o, tc=tc: e.scalar_tensor_tensor(
                                out=xT[:, o, tcs(tc)], in0=banks[b][:], scalar=avec[:, 16 + o:17 + o], in1=xT[:, o, tcs(tc)],
                                op0=ALU.mult, op1=ALU.add)),
                                reads=[bk(b), kx(o, tc), "avec"], writes=[kx(o, tc)])

        actT = [AV(0, 128, [4, S_], BF16), AV(16384, 128, [4, S_], BF16)]
        sl_t = [AV(32768, 128, [512], F32), AV(32768 + 2048, 128, [512], F32)]

        def ffn_phase(l):
            nfb = 6
            cnt = 0
            for fb in range(nfb):
                nch = 4 if fb < 5 else 2
                f0 = fb * 512
                ncol = nch * 128
                sg = wslot()
                wg = wsl[sg][:, 0:8 * ncol].rearrange("p (k n) -> p k n", k=8)
                wload(sg, wg, w_gate_d[l][:, f0:f0 + ncol].rearrange("(kc p) n -> p kc n", p=128))
                su = wslot()
                wu = wsl[su][:, 0:8 * ncol].rearrange("p (k n) -> p k n", k=8)
                wload(su, wu, w_up_d[l][:, f0:f0 + ncol].rearrange("(kc p) n -> p kc n", p=128))
                sd = wslot()
                wd = wsl[sd][:, 0:nch * 1024].rearrange("p (j n) -> p j n", j=nch)
                wload(sd, wd, w_down_d[l][f0:f0 + ncol, :].rearrange("(j p) n -> p j n", p=128))
                at = actT[fb % 2]
                for j in range(nch):
                    for tc in range(NTC):
                        bg_ = nbank("G")
                        fm_proj(lambda kc, j=j: (wg[:, kc, j * 128:(j + 1) * 128], [("ws", sg)]), bg_, tc)
                        bu_ = nbank("G")
                        fm_proj(lambda kc, j=j: (wu[:, kc, j * 128:(j + 1) * 128], [("ws", su)]), bu_, tc)
                        sl = sl_t[cnt % 2]
                        cnt += 1
                        S.op("act", (lambda e, bg_=bg_, sl=sl: e.activation(out=sl.ap, in_=banks[bg_][:], func=AF.Silu)),
                             reads=[bk(bg_)], writes=sl.keys())
                        S.op("dve", (lambda e, bu_=bu_, sl=sl, at=at, j=j, tc=tc: e.tensor_tensor(
                            out=at.ap[:, j, tcs(tc)], in0=banks[bu_][:], in1=sl.ap, op=ALU.mult)),
                            reads=[bk(bu_)] + sl.keys(), writes=at.krange((j,), tc * 512, 512))
                for o in range(8):
                    for tc in range(NTC):
                        b = nbank("G")
                        for j in range(nch):
                            S.op("pe", (lambda e, b=b, j=j, o=o, tc=tc, at=at, nch=nch, wd=wd: e.matmul(
                                banks[b][:], lhsT=wd[:, j, o * 128:(o + 1) * 128], rhs=at.ap[:, j, tcs(tc)],
                                start=(j == 0), stop=(j == nch - 1))),
                                reads=[("ws", sd)] + at.krange((j,), tc * 512, 512), writes=[bk(b)])
                        S.op("dve", (lambda e, b=b, o=o, tc=tc: e.scalar_tensor_tensor(
                            out=xT[:, o, tcs(tc)], in0=banks[b][:], scalar=modT[:, 40 + o:41 + o], in1=xT[:, o, tcs(tc)],
                            op0=ALU.mult, op1=ALU.add)),
                            reads=[bk(b), kx(o, tc), "modT"], writes=[kx(o, tc)])

        import os as _os
        KSTOP = int(_os.environ.get("KSTOP", "99"))
        for l in range(L):
            if KSTOP < 1:
                break
            modulation(l)
            S.op("dve", lambda e: e.tensor_scalar(out=avec[:, 16:24], in0=modT[:, 16:24], scalar1=0.5, scalar2=None, op0=ALU.mult),
                 reads=["modT"], writes=["avec"])
            if KSTOP >= 2:
                norm_phase(0, 0)
            if KSTOP >= 3:
                attention_phase(l)
            if KSTOP >= 4:
                pool_phase(l)
            if KSTOP >= 5:
                conv_phase(l)
            if KSTOP >= 6:
                merge_phase(l)
            if KSTOP >= 7:
                norm_phase(8, 24)
            if KSTOP >= 8:
                ffn_phase(l)

        xo = [AV(0, 128, [D], F32), AV(4096, 128, [D], F32)]
        for t in range(NT):
            xv = xo[t % 2]
            for half in range(2):
                b = nbank("G")
                for cq in range(4):
                    c = half * 4 + cq
                    S.op("pe", (lambda e, b=b, cq=cq, c=c, t=t: e.transpose(
                        banks[b][:, cq * 128:(cq + 1) * 128], xT[:, c, t * 128:(t + 1) * 128], consts[:, O_IDF:O_IDF + 128])),
                        reads=[kx(c, t // 4), "consts"], writes=[bk(b)])
                copy_op(evac_eng(), xv.ap[:, half * 512:(half + 1) * 512], banks[b][:], [bk(b)], xv.keys(half * 512, 512))
            S.op("sp", (lambda e, xv=xv, t=t: e.dma_start(out=out_d[t * 128:(t + 1) * 128, :], in_=xv.ap)),
                 reads=xv.keys(), dma=True)
        S.emit(st)
    return nc


def _consts():
    c = np.zeros((128, NCF), np.float32)
    c[:, O_IDF:O_IDF + 128] = np.eye(128, dtype=np.float32)
    kk = np.arange(128)[:, None]
    qq = np.arange(128)[None, :]
    c[:, O_TRI:O_TRI + 128] = (kk <= qq).astype(np.float32)
    c[:, O_FREQ:O_FREQ + 8] = (500000.0 ** (-np.arange(0, 16, 2, dtype=np.float32) / 16.0)).astype(np.float32)[None, :]
    for p in range(128):
        for cc in range(2):
            w = POOL_W[cc * 2 + (1 if p >= 64 else 0)]
            c[p, O_INVW + cc] = 1.0 / w
            for t in range(16):
                c[p, O_CORR + cc * 16 + t] = 1.0 / min(t + 1, w)
    return c


def _small(b, c, norm1_w, norm2_w, b_ada, pool_scale, conv_w, q_norm_w, k_norm_w, l0, L):
    s = np.zeros((128, NS), np.float32)
    for l in range(L):
        gl = l0 + l
        s[:, O_N1W + l * 8:O_N1W + (l + 1) * 8] = norm1_w[gl].reshape(8, 128).T
        s[:, O_N2W + l * 8:O_N2W + (l + 1) * 8] = norm2_w[gl].reshape(8, 128).T
        s[:, O_BADA + l * 48:O_BADA + (l + 1) * 48] = b_ada[gl].reshape(48, 128).T
        s[:, O_PSC + l * 2:O_PSC + (l + 1) * 2] = pool_scale[gl].reshape(2, 128).T
        s[:, O_CW + l * 6:O_CW + (l + 1) * 6] = conv_w[gl].reshape(3, 2, 128).transpose(2, 0, 1).reshape(128, 6)
        s[:, O_QKW + l * 128:O_QKW + l * 128 + 64] = q_norm_w[gl][None, :]
        s[:, O_QKW + l * 128 + 64:O_QKW + (l + 1) * 128] = k_norm_w[gl][None, :]
    s[:, O_CT:O_CT + 8] = c[b].reshape(8, 128).T
    return s


_NC_CACHE = {}


def _get_nc(L):
    if L not in _NC_CACHE:
        _NC_CACHE[L] = build(L)
    return _NC_CACHE[L]


def _run(x, c, positions, P, l0, L):
    nc = build(L)
    consts = _consts()
    in_maps = []
    wsl_ = slice(l0, l0 + L)
    for b in range(8):
        m = {
            "x": np.ascontiguousarray(x[b]),
            "small": _small(b, c, P["norm1_w"], P["norm2_w"], P["b_ada"], P["pool_scale"], P["conv_w"],
                            P["q_norm_w"], P["k_norm_w"], l0, L),
            "consts": consts,
            "pos": np.ascontiguousarray(positions[b].reshape(NT, 128).T.astype(np.int32)),
        }
        for k in ("w_ada", "w_in", "pool_w", "p_pool", "p_conv", "p_attn", "w_out", "w_gate", "w_up", "w_down"):
            m[k] = np.ascontiguousarray(P[k][wsl_])
        in_maps.append(m)
    res = run_bass_kernel_spmd(nc, in_maps, core_ids=list(range(8)))
    return np.stack([np.asarray(r["out"]) for r in res.results], axis=0)


def kernel(x, c, positions, norm1_w, norm2_w, w_ada, b_ada, w_in, pool_w, pool_scale, conv_w,
           q_norm_w, k_norm_w, p_pool, p_conv, p_attn, w_out, w_gate, w_up, w_down):
    P = dict(norm1_w=np.asarray(norm1_w), norm2_w=np.asarray(norm2_w), w_ada=np.asarray(w_ada), b_ada=np.asarray(b_ada),
             w_in=np.asarray(w_in), pool_w=np.asarray(pool_w), pool_scale=np.asarray(pool_scale), conv_w=np.asarray(conv_w),
             q_norm_w=np.asarray(q_norm_w), k_norm_w=np.asarray(k_norm_w), p_pool=np.asarray(p_pool), p_conv=np.asarray(p_conv),
             p_attn=np.asarray(p_attn), w_out=np.asarray(w_out), w_gate=np.asarray(w_gate), w_up=np.asarray(w_up),
             w_down=np.asarray(w_down))
    x = np.asarray(x, dtype=np.float32)
    c = np.asarray(c)
    positions = np.asarray(positions)
    if FUSED:
        out = _run(x, c, positions, P, 0, DEPTH)
    else:
        out = x
        for l in range(DEPTH):
            out = _run(out, c, positions, P, l, 1)
    return out.astype(np.float32)
```

```python
import math
import os as _os
from contextlib import ExitStack

import numpy as np
import concourse.bass as bass
import concourse.mybir as mybir
from concourse.bass_utils import run_bass_kernel_spmd

F32 = mybir.dt.float32
BF16 = mybir.dt.bfloat16
I32 = mybir.dt.int32
ALU = mybir.AluOpType
AF = mybir.ActivationFunctionType
AX = mybir.AxisListType

FUSED = True
DEPTH = 4
D = 1024
S_ = 2048
NT = 16
NTC = 4
DFF = 2816
EPS = 1e-6
NEGB = -30000.0
O_N1W, O_N2W, O_BADA, O_PSC, O_CW, O_QKW, O_CT, NS = 0, 32, 64, 256, 264, 288, 800, 808
O_IDF, O_TRI, O_FREQ, O_INVW, O_CORR, NCF = 0, 128, 256, 264, 266, 298
POOL_W = (2, 4, 8, 16)

ENGS = ["pe", "act", "dve", "pool", "sp"]


class Op:
    __slots__ = ("eng", "fn", "deps", "is_dma", "sig", "has_dep", "name", "pos")

    def __init__(self, eng, fn, is_dma, name=None):
        self.eng = eng
        self.fn = fn
        self.deps = []
        self.is_dma = is_dma
        self.sig = None
        self.has_dep = False
        self.name = name


class Sched:
    def __init__(self, nc, n_dma_sems=8, maxv=12000):
        self.nc = nc
        self.ops = {e: [] for e in ENGS}
        self.last_writer = {}
        self.readers = {}
        self.n_dma_sems = n_dma_sems
        self.maxv = maxv
        self.cow = {}
        self.old_r = {}
        self.old_w = {}

    def new_gen(self, k):
        self.old_r[k] = self.readers.get(k, [])
        self.old_w[k] = self.cow.get(k, [])
        self.readers[k] = []
        self.cow[k] = []

    def op(self, eng, fn, reads=(), writes=(), dma=False, name=None, joins=()):
        o = Op(eng, fn, dma, name)
        deps = {}
        lw = self.last_writer
        rd = self.readers
        if eng != "pe":
            extra = [k for k in reads if isinstance(k, tuple) and k[0] == "bank"]
            if extra:
                writes = list(writes) + extra
        for k in joins:
            for r in self.old_r.get(k, ()):
                deps[id(r)] = (r, "war")
            for w in self.old_w.get(k, ()):
                if id(w) not in deps:
                    deps[id(w)] = (w, "waw")
            self.cow.setdefault(k, []).append(o)
        for k in reads:
            w = lw.get(k)
            if w is not None:
                deps[id(w)] = (w, "raw")
            for w in self.cow.get(k, ()):
                deps[id(w)] = (w, "raw")
        for k in writes:
            w = lw.get(k)
            if w is not None and id(w) not in deps:
                deps[id(w)] = (w, "waw")
            for r in rd.get(k, ()):
                if id(r) not in deps:
                    deps[id(r)] = (r, "war")
        for k in reads:
            rd.setdefault(k, []).append(o)
        for k in writes:
            lw[k] = o
            rd[k] = []
        best = {}
        for d, kind in deps.values():
            if d.eng == eng and not d.is_dma and not dma:
                if eng == "pe":
                    continue
            if d.is_dma:
                o.deps.append(d)
                d.has_dep = True
            else:
                b = best.get(d.eng)
                if b is None or d.pos > b.pos:
                    best[d.eng] = d
        for d in best.values():
            o.deps.append(d)
            d.has_dep = True
        o.pos = len(self.ops[eng])
        self.ops[eng].append(o)
        return o

    def emit(self, stack):
        nc = self.nc
        for e in ENGS:
            cnt = 0
            nsem = 0
            cur = None
            for o in self.ops[e]:
                if o.is_dma or not o.has_dep:
                    continue
                if cur is None or cnt >= self.maxv:
                    cur = stack.enter_context(nc.semaphore(f"s_{e}_{nsem}"))
                    nsem += 1
                    cnt = 0
                cnt += 1
                o.sig = (cur, cnt, 1)
        for e in ENGS:
            dl = [o for o in self.ops[e] if o.is_dma]
            if not dl:
                continue
            sems = [stack.enter_context(nc.semaphore(f"d_{e}_{i}")) for i in range(min(self.n_dma_sems, len(dl)))]
            cnts = [0] * len(sems)
            for i, o in enumerate(dl):
                j = i % len(sems)
                cnts[j] += 1
                o.sig = (sems[j], 16 * cnts[j], 16)

        def run_engine(ename, eng):
            waited = {}
            for o in self.ops[ename]:
                need = {}
                for d in o.deps:
                    s, v = d.sig[0], d.sig[1]
                    if waited.get(s, 0) >= v:
                        continue
                    if need.get(s, 0) < v:
                        need[s] = v
                if o.is_dma:
                    s, v = o.sig[0], o.sig[1]
                    if v > 16 and waited.get(s, 0) < v - 16:
                        need[s] = max(need.get(s, 0), v - 16)
                for s, v in need.items():
                    eng.wait_ge(s, v)
                    waited[s] = v
                ins = o.fn(eng)
                if o.sig is not None:
                    ins.then_inc(o.sig[0], o.sig[2])
            last = {}
            for o in self.ops[ename]:
                if o.is_dma:
                    last[o.sig[0]] = o.sig[1]
            for s, v in last.items():
                if waited.get(s, 0) < v:
                    eng.wait_ge(s, v)

        with nc.Block() as block:
            @block.tensor
            def _(e):
                run_engine("pe", e)

            @block.scalar
            def _(e):
                run_engine("act", e)

            @block.vector
            def _(e):
                run_engine("dve", e)

            @block.gpsimd
            def _(e):
                run_engine("pool", e)

            @block.sync
            def _(e):
                run_engine("sp", e)


ARENA = 69632
BLK = 1024


def build(L):
    nc = bass.Bass("TRN2", target_bir_lowering=False)

    def din(name, shape, dt=F32):
        return nc.dram_tensor(name, shape, dt, kind="ExternalInput").ap()

    x_d = din("x", [S_, D])
    small_d = din("small", [128, NS])
    consts_d = din("consts", [128, NCF])
    pos_d = din("pos", [128, NT], I32)
    w_ada_d = din("w_ada", [L, D, 6 * D])
    w_in_d = din("w_in", [L, D, 5632])
    pool_w_d = din("pool_w", [L, 4, 64, 64])
    p_pool_d = din("p_pool", [L, 256, D])
    p_conv_d = din("p_conv", [L, 256, D])
    p_attn_d = din("p_attn", [L, 512, D])
    w_out_d = din("w_out", [L, D, D])
    w_gate_d = din("w_gate", [L, D, DFF])
    w_up_d = din("w_up", [L, D, DFF])
    w_down_d = din("w_down", [L, DFF, D])
    lidx_d = None
    out_d = nc.dram_tensor("out", [S_, D], F32, kind="ExternalOutput").ap()

    with ExitStack() as st:
        S = Sched(nc)

        def sb(name, shape, dt):
            return st.enter_context(nc.sbuf_tensor("sb_" + name, shape, dt))

        xT = sb("xT", [128, 8, S_], F32)
        hT = sb("hT", [128, 8, S_], BF16)
        wsl = [sb(f"ws{i}", [128, 4096], BF16) for i in range(4)]
        small = sb("small", [128, NS], F32)
        consts = sb("consts", [128, NCF], F32)
        identb = sb("identb", [128, 128], BF16)
        trib = sb("trib", [128, 128], BF16)
        onesb = sb("onesb", [128, 128], BF16)
        modT = sb("modT", [128, 48], F32)
        avec = sb("avec", [128, 32], F32)
        posi = sb("posi", [128, NT], I32)
        posf = sb("posf", [128, NT], F32)
        ang = sb("ang", [128, NT, 8], F32)
        cosT = sb("cosT", [128, NT, 8], F32)
        sinT = sb("sinT", [128, NT, 8], F32)
        cact = sb("cact", [128, 8], F32)
        cactb = sb("cactb", [128, 8], BF16)
        poolW = sb("poolW", [128, 2, 128], BF16)
        kmT = sb("kmT", [64, 4, 8], BF16)
        kmf = sb("kmf", [64, 4], F32)
        arena = sb("arena", [128, ARENA // 4], F32)
        arena_b = arena.bitcast(BF16)
        banks = [st.enter_context(nc.psum_tensor(f"bank{i}", [128, 512], F32)) for i in range(8)]
        banks_b = [b.bitcast(BF16) for b in banks]

        class AV:
            def __init__(self, base, P, fshape, dt):
                self.base = base
                self.P = P
                self.fshape = list(fshape)
                self.dt = dt
                self.esz = 2 if dt == BF16 else 4
                n = int(np.prod(fshape))
                self.n = n
                assert base % 32 == 0 and base + n * self.esz <= ARENA, (base, n, self.esz)
                src = arena_b if dt == BF16 else arena
                e0 = base // self.esz
                flat = src[0:P, e0:e0 + n]
                if len(fshape) == 1:
                    self.ap = flat
                elif len(fshape) == 2:
                    self.ap = flat.rearrange("p (a b) -> p a b", a=fshape[0])
                elif len(fshape) == 3:
                    self.ap = flat.rearrange("p (a b c) -> p a b c", a=fshape[0], b=fshape[1])
                else:
                    raise ValueError

            def keys(self, lo=0, n=None):
                if n is None:
                    n = self.n - lo
                b0 = (self.base + lo * self.esz) // BLK
                b1 = (self.base + (lo + n) * self.esz - 1) // BLK
                return [("A", b) for b in range(b0, b1 + 1)]

            def kidx(self, *idx):
                strides = []
                s = 1
                for d in reversed(self.fshape):
                    strides.append(s)
                    s *= d
                strides = strides[::-1]
                lo = sum(i * st_ for i, st_ in zip(idx, strides))
                n = strides[len(idx) - 1] if idx else self.n
                return self.keys(lo, n)

            def krange(self, idx, lo, n):
                strides = []
                s = 1
                for d in reversed(self.fshape):
                    strides.append(s)
                    s *= d
                strides = strides[::-1]
                base = sum(i * st_ for i, st_ in zip(idx, strides))
                return self.keys(base + lo, n)

        bank_rr = {"g": [0, [2, 3]], "s": [0, [4, 5]], "a": [0, [6, 7]], "G": [0, [0, 1, 2, 3, 4, 5, 6, 7]]}

        def nbank(pool):
            st_ = bank_rr[pool]
            b = st_[1][st_[0] % len(st_[1])]
            st_[0] += 1
            return b

        def bk(b):
            return ("bank", b)

        ws_rr = [0]

        def wslot():
            i = ws_rr[0] % 4
            ws_rr[0] += 1
            S.new_gen(("ws", i))
            return i

        def wload(slot, dst_ap, src_ap):
            S.op("pool", lambda e: e.dma_start(out=dst_ap, in_=src_ap), joins=[("ws", slot)], dma=True)

        def kx(c, tc):
            return ("xT", c, tc)

        def kh(c, tc):
            return ("hT", c, tc)

        def tcs(tc):
            return slice(tc * 512, (tc + 1) * 512)

        evac_rr = [0]

        def evac_eng():
            evac_rr[0] += 1
            return "act" if evac_rr[0] % 2 else "dve"

        def copy_op(eng, out_ap, in_ap, reads, writes):
            if eng == "act":
                S.op("act", lambda e: e.copy(out=out_ap, in_=in_ap), reads=reads, writes=writes)
            else:
                S.op(eng, lambda e: e.tensor_copy(out=out_ap, in_=in_ap), reads=reads, writes=writes)

        S.op("sp", lambda e: e.dma_start(out=small[:], in_=small_d), writes=["small"], dma=True)
        S.op("sp", lambda e: e.dma_start(out=consts[:], in_=consts_d), writes=["consts"], dma=True)
        S.op("sp", lambda e: e.dma_start(out=posi[:], in_=pos_d), writes=["posi"], dma=True)
        S.op("dve", lambda e: e.tensor_copy(out=identb[:], in_=consts[:, O_IDF:O_IDF + 128]), reads=["consts"], writes=["identb"])
        S.op("dve", lambda e: e.tensor_copy(out=trib[:], in_=consts[:, O_TRI:O_TRI + 128]), reads=["consts"], writes=["trib"])
        S.op("dve", lambda e: e.memset(onesb[:], 1.0), writes=["onesb"])
        S.op("dve", lambda e: e.tensor_copy(out=posf[:], in_=posi[:]), reads=["posi"], writes=["posf"])
        S.op("dve", lambda e: e.tensor_tensor(out=ang[:], in0=posf[:].unsqueeze(2).to_broadcast([128, NT, 8]),
                                              in1=consts[:, O_FREQ:O_FREQ + 8].unsqueeze(1).to_broadcast([128, NT, 8]),
                                              op=ALU.mult), reads=["posf", "consts"], writes=["ang"])
        TWO_PI = 2.0 * math.pi
        ki = sb("ki", [128, NT, 8], I32)
        kf = sb("kf", [128, NT, 8], F32)
        rr = sb("rr", [128, NT, 8], F32)
        mm_ = sb("mm_", [128, NT, 8], F32)

        def sincos(dst, shift, nm):
            S.op("dve", lambda e: e.tensor_scalar(out=rr[:], in0=ang[:], scalar1=shift, scalar2=1.0 / TWO_PI, op0=ALU.add, op1=ALU.mult),
                 reads=["ang"], writes=["rr"])
            S.op("dve", lambda e: e.tensor_copy(out=ki[:], in_=rr[:]), reads=["rr"], writes=["ki"])
            S.op("dve", lambda e: e.tensor_copy(out=kf[:], in_=ki[:]), reads=["ki"], writes=["kf"])
            S.op("dve", lambda e: e.tensor_scalar(out=rr[:], in0=ang[:], scalar1=shift, scalar2=None, op0=ALU.add),
                 reads=["ang", "kf"], writes=["rr"])
            S.op("dve", lambda e: e.scalar_tensor_tensor(out=rr[:], in0=kf[:], scalar=-TWO_PI, in1=rr[:], op0=ALU.mult, op1=ALU.add),
                 reads=["kf", "rr"], writes=["rr"])
            S.op("dve", lambda e: e.tensor_scalar(out=mm_[:], in0=rr[:], scalar1=math.pi, scalar2=-TWO_PI, op0=ALU.is_gt, op1=ALU.mult),
                 reads=["rr"], writes=["mm_"])
            S.op("dve", lambda e: e.tensor_tensor(out=rr[:], in0=rr[:], in1=mm_[:], op=ALU.add), reads=["rr", "mm_"], writes=["rr"])
            S.op("dve", lambda e: e.tensor_scalar(out=mm_[:], in0=rr[:], scalar1=-math.pi, scalar2=TWO_PI, op0=ALU.is_lt, op1=ALU.mult),
                 reads=["rr"], writes=["mm_"])
            S.op("dve", lambda e: e.tensor_tensor(out=rr[:], in0=rr[:], in1=mm_[:], op=ALU.add), reads=["rr", "mm_"], writes=["rr"])
            S.op("act", lambda e: e.activation(out=dst[:], in_=rr[:], func=AF.Sin, scale=1.0 - 1e-6), reads=["rr"], writes=[nm])

        sincos(sinT, 0.0, "sinT")
        sincos(cosT, 0.5 * math.pi, "cosT")
        S.op("act", lambda e: e.activation(out=cact[:], in_=small[:, O_CT:O_CT + 8], func=AF.Silu), reads=["small"], writes=["cact"])
        S.op("dve", lambda e: e.tensor_copy(out=cactb[:], in_=cact[:]), reads=["cact"], writes=["cactb"])

        xin = [AV(0, 128, [4, D], F32), AV(16384, 128, [4, D], F32)]
        for tc in range(NTC):
            xv = xin[tc % 2]
            S.op("sp", (lambda e, xv=xv, tc=tc: e.dma_start(
                out=xv.ap, in_=x_d[tc * 512:(tc + 1) * 512, :].rearrange("(t p) d -> p t d", p=128))),
                writes=xv.keys(), dma=True)
            for c in range(8):
                b = nbank("G")
                for t in range(4):
                    S.op("pe", (lambda e, b=b, xv=xv, t=t, c=c: e.transpose(
                        banks[b][:, t * 128:(t + 1) * 128], xv.ap[:, t, c * 128:(c + 1) * 128], consts[:, O_IDF:O_IDF + 128])),
                        reads=xv.keys() + ["consts"], writes=[bk(b)])
                copy_op(evac_eng(), xT[:, c, tcs(tc)], banks[b][:], [bk(b)], [kx(c, tc)])

        def modulation(l):
            pm = nbank("G")
            for j in range(12):
                s = wslot()
                wv = wsl[s][:, :].rearrange("p (k n) -> p k n", k=8)
                wload(s, wv, w_ada_d[l][:, j * 512:(j + 1) * 512].rearrange("(kc p) n -> p kc n", p=128))
                for oc in range(4):
                    col = j * 4 + oc
                    for kc in range(8):
                        S.op("pe", (lambda e, wv=wv, oc=oc, kc=kc, col=col, pm=pm: e.matmul(
                            banks[pm][:, col:col + 1], lhsT=wv[:, kc, oc * 128:(oc + 1) * 128], rhs=cactb[:, kc:kc + 1],
                            start=(kc == 0), stop=(kc == 7), skip_group_check=True)),
                            reads=[("ws", s), "cactb"], writes=[bk(pm)])
            S.op("dve", (lambda e, pm=pm, l=l: e.tensor_tensor(out=modT[:], in0=banks[pm][:, 0:48],
                                                               in1=small[:, O_BADA + l * 48:O_BADA + (l + 1) * 48], op=ALU.add)),
                 reads=[bk(pm), "small"], writes=["modT"])
            S.op("dve", (lambda e, l=l: e.scalar_tensor_tensor(out=avec[:, 0:8], in0=modT[:, 8:16], scalar=1.0,
                                                               in1=small[:, O_N1W + l * 8:O_N1W + (l + 1) * 8],
                                                               op0=ALU.add, op1=ALU.mult)),
                 reads=["modT", "small"], writes=["avec"])
            S.op("dve", (lambda e, l=l: e.scalar_tensor_tensor(out=avec[:, 8:16], in0=modT[:, 32:40], scalar=1.0,
                                                               in1=small[:, O_N2W + l * 8:O_N2W + (l + 1) * 8],
                                                               op0=ALU.add, op1=ALU.mult)),
                 reads=["modT", "small"], writes=["avec"])

        def norm_phase(a_off, sh_off):
            sqb = AV(0, 128, [8, 512], BF16)
            rst = [AV(8192, 128, [512], F32), AV(10240, 128, [512], F32)]
            tmp = [AV(12288, 128, [512], F32), AV(14336, 128, [512], F32), AV(16384, 128, [512], F32)]
            ti = 0
            for tc in range(NTC):
                for c in range(8):
                    if c % 2 == 0:
                        S.op("act", (lambda e, c=c, tc=tc: e.activation(out=sqb.ap[:, c, :], in_=xT[:, c, tcs(tc)], func=AF.Square)),
                             reads=[kx(c, tc)], writes=sqb.kidx(c))
                    else:
                        S.op("dve", (lambda e, c=c, tc=tc: e.tensor_tensor(out=sqb.ap[:, c, :], in0=xT[:, c, tcs(tc)],
                                                                           in1=xT[:, c, tcs(tc)], op=ALU.mult)),
                             reads=[kx(c, tc)], writes=sqb.kidx(c))
                b = nbank("G")
                for c in range(8):
                    S.op("pe", (lambda e, b=b, c=c: e.matmul(banks[b][:], lhsT=onesb[:], rhs=sqb.ap[:, c, :],
                                                            start=(c == 0), stop=(c == 7))),
                         reads=sqb.kidx(c) + ["onesb"], writes=[bk(b)])
                r = rst[tc % 2]
                S.op("act", (lambda e, b=b, r=r: e.activation(out=r.ap, in_=banks[b][:], func=AF.Sqrt, bias=epsb[:, 0:1], scale=1.0 / D)),
                     reads=[bk(b), "epsb"], writes=r.keys())
                S.op("dve", (lambda e, r=r: e.reciprocal(out=r.ap, in_=r.ap)), reads=r.keys(), writes=r.keys())
                for c in range(8):
                    t_ = tmp[ti % 3]
                    ti += 1
                    S.op("dve", (lambda e, t_=t_, r=r, c=c, tc=tc: e.tensor_tensor(out=t_.ap, in0=xT[:, c, tcs(tc)], in1=r.ap, op=ALU.mult)),
                         reads=[kx(c, tc)] + r.keys(), writes=t_.keys())
                    S.op("act", (lambda e, t_=t_, c=c, tc=tc: e.activation(
                        out=hT[:, c, tcs(tc)], in_=t_.ap, func=AF.Identity,
                        scale=avec[:, a_off + c:a_off + c + 1], bias=modT[:, sh_off + c:sh_off + c + 1])),
                        reads=t_.keys() + ["avec", "modT"], writes=[kh(c, tc)])

        epsb = sb("epsb", [128, 1], F32)
        S.op("dve", lambda e: e.memset(epsb[:], EPS), writes=["epsb"])

        yattnT = AV(0, 128, [4, S_], BF16)
        ypoolT = AV(16384, 128, [2, S_], BF16)
        yconvT = AV(24576, 128, [2, S_], BF16)
        A1 = 32768
        KaT = AV(A1, 72, [4, S_], BF16)
        Va = AV(A1 + 16384, 128, [NT, 4, 65], BF16)
        QaTs = [AV(A1 + 24736, 72, [4, 512], BF16), AV(65536, 72, [4, 512], BF16)]
        o_ = A1 + 24736 + 4096
        qk_tok = [AV(o_, 128, [8, 72], BF16), AV(o_ + 1152, 128, [8, 72], BF16)]
        o_ += 2304
        mb_tok = [AV(o_, 128, [4, 72], BF16), AV(o_ + 576, 128, [4, 72], BF16)]
        o_ += 1152
        qn_s = [AV(16384, 128, [8, 64], F32), AV(18432, 128, [8, 64], F32)]
        ytok = AV(20480, 128, [4, 256], BF16)
        pT = [AV(22528 + i * 1024, 128, [512], BF16) for i in range(4)]
        sq_s = AV(26624, 128, [512], F32)
        sm_par = []
        for i_ in range(2):
            o2 = 28672 + i_ * 1536
            sm_par.append(dict(
                ssq8=AV(o2, 128, [8], F32), vv8=AV(o2 + 32, 128, [8], F32), yy8=AV(o2 + 64, 128, [8], F32),
                aa8=AV(o2 + 96, 128, [8], F32), gsb=AV(o2 + 128, 128, [4, 8], F32), rank_s=AV(o2 + 256, 128, [4, 8], F32),
                rt=[AV(o2 + 384 + k_ * 256, 128, [8, 8], F32) for k_ in range(4)]))
        cmp_s = AV(31744, 128, [4, 8, 8], F32)
        rec_s = AV(65056, 128, [4], F32)

        def merge_steps(lists):
            idx = [0] * len(lists)
            tot = [max(1, len(x)) for x in lists]
            while True:
                best = None
                for i, x in enumerate(lists):
                    if idx[i] < len(x):
                        frac = idx[i] / tot[i]
                        if best is None or frac < best[0]:
                            best = (frac, i)
                if best is None:
                    break
                i = best[1]
                lists[i][idx[i]]()
                idx[i] += 1

        def tile_steps(l, hg, qc, j, wqk, wv, sqk, sv):
            t = qc * 4 + j
            blk = t // 2
            QaT = QaTs[qc % 2]
            qn = qn_s[t % 2]
            qt = qk_tok[t % 2]
            mb = mb_tok[t % 2]
            sp_ = sm_par[t % 2]
            ssq8, vv8, yy8, aa8, gsb, rank_s, rt = (sp_["ssq8"], sp_["vv8"], sp_["yy8"], sp_["aa8"], sp_["gsb"],
                                                    sp_["rank_s"], sp_["rt"])
            st_ = {}
            steps = []

            def s1():
                bq = st_["bq"] = t % 2
                for kc in range(8):
                    S.op("pe", (lambda e, kc=kc: e.matmul(
                        banks[bq][:], lhsT=hT[:, kc, t * 128:(t + 1) * 128], rhs=wqk[:, kc, :],
                        start=(kc == 0), stop=(kc == 7))),
                        reads=[kh(kc, t // 4), ("ws", sqk)], writes=[bk(bq)])
                bv = nbank("g")
                for kc in range(8):
                    S.op("pe", (lambda e, kc=kc: e.matmul(
                        banks[bv][:, 0:256], lhsT=hT[:, kc, t * 128:(t + 1) * 128], rhs=wv[:, kc, :],
                        start=(kc == 0), stop=(kc == 7))),
                        reads=[kh(kc, t // 4), ("ws", sv)], writes=[bk(bv)])
                S.op("act", (lambda e: e.activation(out=sq_s.ap, in_=banks[bq][:], func=AF.Square)),
                     reads=[bk(bq)], writes=sq_s.keys())
                S.op("act", (lambda e: e.copy(
                    out=Va.ap[:, t, :, 0:64], in_=banks[bv][:, 0:256].rearrange("p (h d) -> p h d", h=4))),
                    reads=[bk(bv)], writes=Va.kidx(t))
            steps.append(s1)

            def s2():
                S.op("dve", lambda e: e.tensor_reduce(out=ssq8.ap, in_=sq_s.ap.rearrange("p (h d) -> p h d", h=8),
                                                      axis=AX.X, op=ALU.add),
                     reads=sq_s.keys(), writes=ssq8.keys())
                S.op("dve", lambda e: e.tensor_scalar(out=vv8.ap, in0=ssq8.ap, scalar1=1.0 / 64, scalar2=EPS, op0=ALU.mult, op1=ALU.add),
                     reads=ssq8.keys(), writes=vv8.keys())
                S.op("dve", lambda e: e.tensor_scalar(out=yy8.ap.bitcast(I32), in0=vv8.ap.bitcast(I32), scalar1=-0.5, scalar2=1597463007.0,
                                                      op0=ALU.mult, op1=ALU.add),
                     reads=vv8.keys(), writes=yy8.keys())
                for _ in range(2):
                    S.op("dve", lambda e: e.tensor_tensor(out=aa8.ap, in0=yy8.ap, in1=yy8.ap, op=ALU.mult),
                         reads=yy8.keys(), writes=aa8.keys())
                    S.op("dve", lambda e: e.tensor_tensor(out=aa8.ap, in0=aa8.ap, in1=vv8.ap, op=ALU.mult),
                         reads=aa8.keys() + vv8.keys(), writes=aa8.keys())
                    S.op("dve", lambda e: e.tensor_scalar(out=aa8.ap, in0=aa8.ap, scalar1=-0.5, scalar2=1.5, op0=ALU.mult, op1=ALU.add),
                         reads=aa8.keys(), writes=aa8.keys())
                    S.op("dve", lambda e: e.tensor_tensor(out=yy8.ap, in0=yy8.ap, in1=aa8.ap, op=ALU.mult),
                         reads=yy8.keys() + aa8.keys(), writes=yy8.keys())
            steps.append(s2)

            def s3():
                bq = st_["bq"]
                S.op("dve", (lambda e: e.tensor_tensor(
                    out=qn.ap, in0=banks[bq][:].rearrange("p (h d) -> p h d", h=8),
                    in1=yy8.ap.unsqueeze(2).to_broadcast([128, 8, 64]), op=ALU.mult)),
                    reads=[bk(bq)] + yy8.keys(), writes=qn.keys())
                S.op("dve", (lambda e: e.tensor_tensor(
                    out=qn.ap.rearrange("p (a h) d -> p a h d", a=2), in0=qn.ap.rearrange("p (a h) d -> p a h d", a=2),
                    in1=small[:, O_QKW + l * 128:O_QKW + (l + 1) * 128].rearrange("p (a d) -> p a d", a=2)
                    .unsqueeze(2).to_broadcast([128, 2, 4, 64]), op=ALU.mult)),
                    reads=qn.keys() + ["small"], writes=qn.keys())
            steps.append(s3)

            def s4():
                S.op("pool", (lambda e: e.tensor_copy(out=qt.ap[:, :, 16:64], in_=qn.ap[:, :, 16:64])),
                     reads=qn.keys(), writes=qt.keys())
                cb = cosT[:, t, :].unsqueeze(1).to_broadcast([128, 8, 8])
                sbb = sinT[:, t, :].unsqueeze(1).to_broadcast([128, 8, 8])
                x1 = qn.ap[:, :, 0:8]
                x2 = qn.ap[:, :, 8:16]
                S.op("pool", (lambda e: e.tensor_tensor(out=rt[0].ap, in0=x1, in1=cb, op=ALU.mult)),
                     reads=qn.keys() + ["cosT"], writes=rt[0].keys())
                S.op("pool", (lambda e: e.tensor_tensor(out=rt[1].ap, in0=x2, in1=sbb, op=ALU.mult)),
                     reads=qn.keys() + ["sinT"], writes=rt[1].keys())
                S.op("pool", (lambda e: e.tensor_tensor(out=qt.ap[:, :, 0:8], in0=rt[0].ap, in1=rt[1].ap, op=ALU.subtract)),
                     reads=rt[0].keys() + rt[1].keys(), writes=qt.keys())
                S.op("pool", (lambda e: e.tensor_tensor(out=rt[2].ap, in0=x2, in1=cb, op=ALU.mult)),
                     reads=qn.keys() + ["cosT"], writes=rt[2].keys())
                S.op("pool", (lambda e: e.tensor_tensor(out=rt[3].ap, in0=x1, in1=sbb, op=ALU.mult)),
                     reads=qn.keys() + ["sinT"], writes=rt[3].keys())
                S.op("pool", (lambda e: e.tensor_tensor(out=qt.ap[:, :, 8:16], in0=rt[2].ap, in1=rt[3].ap, op=ALU.add)),
                     reads=rt[2].keys() + rt[3].keys(), writes=qt.keys())
                S.op("pool", (lambda e: e.memset(qt.ap[:, 4:8, 64:72], 0.0)), writes=qt.keys())
                S.op("pool", (lambda e: e.memset(qt.ap[:, 4:8, 64 + blk:65 + blk], 1.0)), writes=qt.keys())
                S.op("pool", (lambda e: e.memset(mb.ap[:, :, 64:72], NEGB)), writes=mb.keys())
                S.op("pool", (lambda e: e.memset(mb.ap[:, :, 64:65 + blk], 0.0)), writes=mb.keys())
            steps.append(s4)

            def s5():
                btr = nbank("g")
                for h in range(4):
                    S.op("pe", (lambda e, h=h: e.transpose(
                        banks_b[btr][0:72, h * 128:(h + 1) * 128], qt.ap[:, 4 + h, 0:72], identb[:])),
                        reads=qt.keys() + ["identb"], writes=[bk(btr)])
                    S.op("pe", (lambda e, h=h: e.transpose(
                        banks_b[btr][0:64, 512 + h * 128:512 + (h + 1) * 128], qt.ap[:, h, 0:64], identb[:])),
                        reads=qt.keys() + ["identb"], writes=[bk(btr)])
                S.op("act", (lambda e: e.copy(
                    out=KaT.ap[:, :, t * 128:(t + 1) * 128], in_=banks_b[btr][0:72, 0:512].rearrange("p (h n) -> p h n", h=4))),
                    reads=[bk(btr)], writes=[k_ for h in range(4) for k_ in KaT.krange((h,), t * 128, 128)])
                S.op("dve", (lambda e: e.tensor_copy(
                    out=QaT.ap[0:64, :, j * 128:(j + 1) * 128], in_=banks_b[btr][0:64, 512:1024].rearrange("p (h n) -> p h n", h=4))),
                    reads=[bk(btr)], writes=QaT.keys())
                if t % 2 == 1 and blk < 7:
                    S.op("dve", (lambda e: e.tensor_reduce(
                        out=kmf[:], in_=KaT.ap[0:64, :, blk * 256:(blk + 1) * 256], axis=AX.X, op=ALU.add)),
                        reads=[k_ for h in range(4) for k_ in KaT.krange((h,), blk * 256, 256)], writes=["kmf"])
                    S.op("dve", (lambda e: e.tensor_scalar(
                        out=kmT[:, :, blk:blk + 1], in0=kmf[:].unsqueeze(2), scalar1=1.0 / 256, scalar2=None, op0=ALU.mult)),
                        reads=["kmf"], writes=["kmT"])
            steps.append(s5)

            def s6():
                if blk >= 4:
                    bg = nbank("g")
                    for h in range(4):
                        S.op("pe", (lambda e, h=h: e.matmul(
                            banks[bg][:, h * 8:h * 8 + blk], lhsT=QaT.ap[0:64, h, j * 128:(j + 1) * 128],
                            rhs=kmT[:, h, 0:blk], start=True, stop=True, skip_group_check=True)),
                            reads=QaT.keys() + ["kmT"], writes=[bk(bg)])
                    S.op("act", (lambda e: e.copy(
                        out=gsb.ap[:, :, 0:blk], in_=banks[bg][:, 0:32].rearrange("p (h n) -> p h n", h=4)[:, :, 0:blk])),
                        reads=[bk(bg)], writes=gsb.keys())
                    S.op("dve", (lambda e: e.tensor_tensor(
                        out=cmp_s.ap[:, :, 0:blk, 0:blk],
                        in0=gsb.ap[:, :, 0:blk].unsqueeze(2).to_broadcast([128, 4, blk, blk]),
                        in1=gsb.ap[:, :, 0:blk].unsqueeze(3).to_broadcast([128, 4, blk, blk]), op=ALU.is_gt)),
                        reads=gsb.keys(), writes=cmp_s.keys())
                    S.op("dve", (lambda e: e.tensor_reduce(
                        out=rank_s.ap[:, :, 0:blk], in_=cmp_s.ap[:, :, 0:blk, 0:blk], axis=AX.X, op=ALU.add)),
                        reads=cmp_s.keys(), writes=rank_s.keys())
                    S.op("dve", (lambda e: e.tensor_scalar(
                        out=mb.ap[:, :, 64:64 + blk], in0=rank_s.ap[:, :, 0:blk], scalar1=2.5, scalar2=NEGB,
                        op0=ALU.is_gt, op1=ALU.mult)),
                        reads=rank_s.keys(), writes=mb.keys())
                bm = nbank("g")
                for h in range(4):
                    S.op("pe", (lambda e, h=h: e.transpose(
                        banks_b[bm][0:72, h * 128:(h + 1) * 128], mb.ap[:, h, 0:72], identb[:])),
                        reads=mb.keys() + ["identb"], writes=[bk(bm)])
                S.op("act", (lambda e: e.copy(
                    out=QaT.ap[64:72, :, j * 128:(j + 1) * 128], in_=banks_b[bm][64:72, 0:512].rearrange("p (h n) -> p h n", h=4))),
                    reads=[bk(bm)], writes=QaT.keys())
            steps.append(s6)
            return steps

        def attn_steps(hg, qc):
            QaT = QaTs[qc % 2]
            nk = 4 * (qc + 1)
            items = [(h, kt) for h in range(4) for kt in range(nk)]
            st_ = {}

            def stage1(i):
                h, kt = items[i]
                j0 = max(0, kt - 4 * qc)
                bs = nbank("s")
                p_ = pT[i % 4]
                S.op("pe", (lambda e: e.matmul(
                    banks[bs][:, j0 * 128:512], lhsT=KaT.ap[0:72, h, kt * 128:(kt + 1) * 128],
                    rhs=QaT.ap[0:72, h, j0 * 128:512], start=True, stop=True)),
                    reads=KaT.krange((h,), kt * 128, 128) + QaT.keys(), writes=[bk(bs)])
                S.op("act", (lambda e: e.activation(
                    out=p_.ap[:, j0 * 128:512], in_=banks[bs][:, j0 * 128:512], func=AF.Exp, scale=0.125)),
                    reads=[bk(bs)], writes=p_.keys())
                if kt >= 4 * qc:
                    S.op("pool", (lambda e: e.tensor_tensor(
                        out=p_.ap[:, j0 * 128:(j0 + 1) * 128], in0=p_.ap[:, j0 * 128:(j0 + 1) * 128], in1=trib[:], op=ALU.mult)),
                        reads=p_.keys() + ["trib"], writes=p_.keys())

            def stage2(i):
                h, kt = items[i]
                j0 = max(0, kt - 4 * qc)
                p_ = pT[i % 4]
                if kt == 0:
                    st_[h] = nbank("a")
                ba = st_[h]
                accv = banks[ba][:].rearrange("p (j n) -> p j n", j=4)
                for jj in range(j0, 4):
                    S.op("pe", (lambda e, jj=jj: e.matmul(
                        accv[:, jj, 0:65], lhsT=p_.ap[:, jj * 128:(jj + 1) * 128], rhs=Va.ap[:, kt, h, :],
                        start=(kt == 0 and jj == 0), stop=(kt == 4 * qc + jj), skip_group_check=True)),
                        reads=p_.keys() + Va.kidx(kt), writes=[bk(ba)])
                if kt == nk - 1:
                    S.op("dve", (lambda e: e.reciprocal(out=rec_s.ap, in_=accv[:, :, 64])),
                         reads=[bk(ba)], writes=rec_s.keys())
                    S.op("dve", (lambda e: e.tensor_tensor(
                        out=ytok.ap[:, :, h * 64:(h + 1) * 64], in0=accv[:, :, 0:64],
                        in1=rec_s.ap.unsqueeze(2).to_broadcast([128, 4, 64]), op=ALU.mult)),
                        reads=[bk(ba)] + rec_s.keys(), writes=ytok.keys())

            n = len(items)
            LA = int(_os.environ.get('LA', '3'))
            steps = [(lambda: [stage1(i_) for i_ in range(LA)])]
            for i in range(n):
                def sk(i=i):
                    stage2(i)
                    if i + LA < n:
                        stage1(i + LA)
                steps.append(sk)

            def sy():
                by = nbank("g")
                for jj in range(4):
                    for cc in range(2):
                        S.op("pe", (lambda e, jj=jj, cc=cc: e.transpose(
                            banks_b[by][:, (cc * 4 + jj) * 128:(cc * 4 + jj + 1) * 128], ytok.ap[:, jj, cc * 128:(cc + 1) * 128], identb[:])),
                            reads=ytok.keys() + ["identb"], writes=[bk(by)])
                for cc in range(2):
                    copy_op("act" if cc == 0 else "dve", yattnT.ap[:, hg * 2 + cc, qc * 512:(qc + 1) * 512],
                            banks_b[by][:, cc * 512:(cc + 1) * 512], [bk(by)], yattnT.krange((hg * 2 + cc,), qc * 512, 512))
            steps.append(sy)
            return steps

        def attention_phase(l):
            for hg in range(2):
                sqk = wslot()
                wqk = wsl[sqk][:, :].rearrange("p (k n) -> p k n", k=8)
                wload(sqk, wqk[:, :, 0:256], w_in_d[l][:, 1024 + hg * 256:1024 + (hg + 1) * 256].rearrange("(kc p) n -> p kc n", p=128))
                wload(sqk, wqk[:, :, 256:512], w_in_d[l][:, 1536 + hg * 256:1536 + (hg + 1) * 256].rearrange("(kc p) n -> p kc n", p=128))
                sv = wslot()
                wv = wsl[sv][:, 0:2048].rearrange("p (k n) -> p k n", k=8)
                wload(sv, wv, w_in_d[l][:, 2048 + hg * 256:2048 + (hg + 1) * 256].rearrange("(kc p) n -> p kc n", p=128))
                S.op("pool", lambda e: e.memset(Va.ap[:, :, :, 64:65], 1.0), writes=Va.keys())
                for m_ in mb_tok:
                    S.op("pool", (lambda e, m_=m_: e.memset(m_.ap[:, :, 0:64], 0.0)), writes=m_.keys())

                def A(qc):
                    tl = [tile_steps(l, hg, qc, j, wqk, wv, sqk, sv) for j in range(4)]
                    out = []
                    dsk = int(_os.environ.get('DSK', '2'))
                    ns = len(tl[0])
                    for k in range(ns + 3 * dsk):
                        for i in range(4):
                            ix = k - i * dsk
                            if 0 <= ix < ns:
                                out.append(tl[i][ix])
                    return out

                merge_steps([A(0)])
                for qc in range(NTC):
                    lists = [attn_steps(hg, qc)]
                    if qc + 1 < NTC:
                        lists.append(A(qc + 1))
                    merge_steps(lists)

        PW = S_ + 16
        up_s = AV(A1, 128, [PW], F32)
        sA_s = AV(A1 + 8256, 128, [PW], F32)
        sB_s = AV(A1 + 16512, 128, [PW], F32)

        def fm_proj(wv_of_kc, b, tc):
            for kc in range(8):
                lhsT, rk = wv_of_kc(kc)
                S.op("pe", (lambda e, lhsT=lhsT, kc=kc, b=b, tc=tc: e.matmul(
                    banks[b][:], lhsT=lhsT, rhs=hT[:, kc, tcs(tc)], start=(kc == 0), stop=(kc == 7))),
                    reads=[kh(kc, tc)] + rk, writes=[bk(b)])

        def pool_phase(l):
            S.op("dve", lambda e: e.memset(poolW[:], 0.0), writes=[("poolW", g) for g in range(4)])
            for g in range(4):
                r0 = (g % 2) * 64
                S.op("pool", (lambda e, g=g, r0=r0, l=l: e.dma_start(out=poolW[r0:r0 + 64, g // 2, r0:r0 + 64], in_=pool_w_d[l, g])),
                     writes=[("poolW", g)], dma=True)
            s = wslot()
            wv = wsl[s][:, 0:2048].rearrange("p (k n) -> p k n", k=8)
            wload(s, wv, w_in_d[l][:, 0:256].rearrange("(kc p) n -> p kc n", p=128))
            for v_ in (up_s, sA_s, sB_s):
                S.op("dve", (lambda e, v_=v_: e.memset(v_.ap[:, 0:16], 0.0)), writes=v_.keys(0, 16))
            for cc in range(2):
                for tc in range(NTC):
                    b = nbank("G")
                    fm_proj(lambda kc, cc=cc: (wv[:, kc, cc * 128:(cc + 1) * 128], [("ws", s)]), b, tc)
                    copy_op(evac_eng(), up_s.ap[:, 16 + tc * 512:16 + (tc + 1) * 512], banks[b][:], [bk(b)],
                            up_s.keys(16 + tc * 512, 512))
                n_lv = 2 if cc == 0 else 4
                cur = up_s
                nxt = [sA_s, sB_s]
                srcs = {}
                for lv in range(1, n_lv + 1):
                    sh = 1 << (lv - 1)
                    dst = nxt[(lv - 1) % 2]
                    p0 = 64 if lv == n_lv else 0
                    S.op("dve", (lambda e, dst=dst, cur=cur, sh=sh, p0=p0: e.tensor_tensor(
                        out=dst.ap[p0:128, 16:PW], in0=cur.ap[p0:128, 16:PW], in1=cur.ap[p0:128, 16 - sh:PW - sh], op=ALU.add)),
                        reads=cur.keys(), writes=dst.keys(16, S_))
                    srcs[lv] = dst
                    cur = dst
                lo_src = srcs[n_lv - 1]
                hi_src = srcs[n_lv]
                pooledT = yconvT
                for (p0, p1, src) in ((0, 64, lo_src), (64, 128, hi_src)):
                    S.op("dve", (lambda e, p0=p0, p1=p1, src=src, cc=cc: e.tensor_tensor(
                        out=src.ap[p0:p1, 16:32], in0=src.ap[p0:p1, 16:32],
                        in1=consts[p0:p1, O_CORR + cc * 16:O_CORR + (cc + 1) * 16], op=ALU.mult)),
                        reads=src.keys() + ["consts"], writes=src.keys(16, 16))
                    S.op("dve", (lambda e, p0=p0, p1=p1, src=src, cc=cc: e.tensor_tensor(
                        out=pooledT.ap[p0:p1, cc, 0:16], in0=src.ap[p0:p1, 16:32], in1=up_s.ap[p0:p1, 16:32], op=ALU.subtract)),
                        reads=src.keys() + up_s.keys(), writes=pooledT.krange((cc,), 0, 16))
                    S.op("dve", (lambda e, p0=p0, p1=p1, src=src, cc=cc: e.scalar_tensor_tensor(
                        out=pooledT.ap[p0:p1, cc, 16:S_], in0=src.ap[p0:p1, 32:PW], scalar=consts[p0:p1, O_INVW + cc:O_INVW + cc + 1],
                        in1=up_s.ap[p0:p1, 32:PW], op0=ALU.mult, op1=ALU.subtract)),
                        reads=src.keys() + up_s.keys() + ["consts"], writes=pooledT.kidx(cc))
                for tc in range(NTC):
                    b = nbank("G")
                    S.op("pe", (lambda e, b=b, cc=cc, tc=tc: e.matmul(banks[b][:], lhsT=poolW[:, cc, :], rhs=pooledT.ap[:, cc, tcs(tc)],
                                                                      start=True, stop=True)),
                         reads=[("poolW", 2 * cc), ("poolW", 2 * cc + 1)] + pooledT.krange((cc,), tc * 512, 512), writes=[bk(b)])
                    S.op("act", (lambda e, b=b, cc=cc, tc=tc, l=l: e.activation(
                        out=ypoolT.ap[:, cc, tcs(tc)], in_=banks[b][:], func=AF.Identity,
                        scale=small[:, O_PSC + l * 2 + cc:O_PSC + l * 2 + cc + 1])),
                        reads=[bk(b), "small"], writes=ypoolT.krange((cc,), tc * 512, 512))

        def conv_phase(l):
            cu_s, cc_s, ac_s = up_s, sA_s, sB_s
            for cc in range(2):
                s = wslot()
                wv = wsl[s][:, 0:3072].rearrange("p (k i n) -> p k i n", k=8, i=3)
                for i in range(3):
                    c0 = 256 + i * 256 + cc * 128
                    wload(s, wv[:, :, i, :], w_in_d[l][:, c0:c0 + 128].rearrange("(kc p) n -> p kc n", p=128))
                S.op("dve", lambda e: e.memset(cu_s.ap[:, 0:16], 0.0), writes=cu_s.keys(0, 16))
                for tc in range(NTC):
                    b = nbank("G")
                    fm_proj(lambda kc: (wv[:, kc, 0, :], [("ws", s)]), b, tc)
                    copy_op("act", cu_s.ap[:, 16 + tc * 512:16 + (tc + 1) * 512], banks[b][:], [bk(b)], cu_s.keys(16 + tc * 512, 512))
                    b2 = nbank("G")
                    fm_proj(lambda kc: (wv[:, kc, 2, :], [("ws", s)]), b2, tc)
                    S.op("dve", (lambda e, b2=b2, tc=tc: e.tensor_tensor(
                        out=cu_s.ap[:, 16 + tc * 512:16 + (tc + 1) * 512], in0=banks[b2][:],
                        in1=cu_s.ap[:, 16 + tc * 512:16 + (tc + 1) * 512], op=ALU.mult)),
                        reads=[bk(b2)] + cu_s.keys(16 + tc * 512, 512), writes=cu_s.keys(16 + tc * 512, 512))
                cw = lambda k, cc=cc, l=l: small[:, O_CW + l * 6 + k * 2 + cc:O_CW + l * 6 + k * 2 + cc + 1]
                S.op("dve", (lambda e, cw=cw: e.tensor_scalar(out=ac_s.ap[:, 16:PW], in0=cu_s.ap[:, 16:PW], scalar1=cw(2), scalar2=None, op0=ALU.mult)),
                     reads=cu_s.keys() + ["small"], writes=ac_s.keys(16, S_))
                S.op("dve", (lambda e, cw=cw: e.scalar_tensor_tensor(out=ac_s.ap[:, 16:PW], in0=cu_s.ap[:, 15:PW - 1], scalar=cw(1),
                                                                     in1=ac_s.ap[:, 16:PW], op0=ALU.mult, op1=ALU.add)),
                     reads=cu_s.keys() + ac_s.keys() + ["small"], writes=ac_s.keys(16, S_))
                S.op("dve", (lambda e, cw=cw: e.scalar_tensor_tensor(out=ac_s.ap[:, 16:PW], in0=cu_s.ap[:, 14:PW - 2], scalar=cw(0),
                                                                     in1=ac_s.ap[:, 16:PW], op0=ALU.mult, op1=ALU.add)),
                     reads=cu_s.keys() + ac_s.keys() + ["small"], writes=ac_s.keys(16, S_))
                for tc in range(NTC):
                    b = nbank("G")
                    fm_proj(lambda kc: (wv[:, kc, 1, :], [("ws", s)]), b, tc)
                    S.op("dve", (lambda e, b=b, cc=cc, tc=tc: e.tensor_tensor(
                        out=yconvT.ap[:, cc, tcs(tc)], in0=banks[b][:], in1=ac_s.ap[:, 16 + tc * 512:16 + (tc + 1) * 512], op=ALU.mult)),
                        reads=[bk(b)] + ac_s.keys(16 + tc * 512, 512), writes=yconvT.krange((cc,), tc * 512, 512))

        merged = AV(A1, 128, [8, 1024], BF16)
        mo = A1 + 16384
        s_t = [AV(mo, 128, [512], F32), AV(mo + 2048, 128, [512], F32)]
        m_t = [AV(mo + 4096, 128, [512], F32), AV(mo + 6144, 128, [512], F32)]
        t_t = [AV(mo + 8192, 128, [512], F32), AV(mo + 10240, 128, [512], F32)]

        def merge_phase(l):
            ysrc = [(ypoolT, 2, 0), (yconvT, 2, 2), (yattnT, 4, 4)]
            pds = [p_pool_d, p_conv_d, p_attn_d]
            cnt = 0
            for hh in range(2):
                for c in range(8):
                    s = wslot()
                    wg = wsl[s][:, 0:3072].rearrange("p (k i n) -> p k i n", k=8, i=3)
                    wp = wsl[s][:, 3072:4096].rearrange("p (k n) -> p k n", k=8)
                    for i in range(3):
                        c0 = 2560 + i * 1024 + c * 128
                        wload(s, wg[:, :, i, :], w_in_d[l][:, c0:c0 + 128].rearrange("(kc p) n -> p kc n", p=128))
                    wload(s, wp[:, 0:2, :], p_pool_d[l][:, c * 128:(c + 1) * 128].rearrange("(kc p) n -> p kc n", p=128))
                    wload(s, wp[:, 2:4, :], p_conv_d[l][:, c * 128:(c + 1) * 128].rearrange("(kc p) n -> p kc n", p=128))
                    wload(s, wp[:, 4:8, :], p_attn_d[l][:, c * 128:(c + 1) * 128].rearrange("(kc p) n -> p kc n", p=128))
                    for th in range(2):
                        tc = hh * 2 + th
                        mt = m_t[cnt % 2]
                        for i in range(3):
                            bgt = nbank("G")
                            fm_proj(lambda kc, i=i: (wg[:, kc, i, :], [("ws", s)]), bgt, tc)
                            st_ = s_t[(cnt * 3 + i) % 2]
                            S.op("act", (lambda e, bgt=bgt, st_=st_: e.activation(out=st_.ap, in_=banks[bgt][:], func=AF.Tanh, scale=0.5)),
                                 reads=[bk(bgt)], writes=st_.keys())
                            ysv, nkk, koff = ysrc[i]
                            bp = nbank("G")
                            for kk in range(nkk):
                                S.op("pe", (lambda e, bp=bp, kk=kk, koff=koff, ysv=ysv, tc=tc, nkk=nkk, wp=wp: e.matmul(
                                    banks[bp][:], lhsT=wp[:, koff + kk, :], rhs=ysv.ap[:, kk, tcs(tc)], start=(kk == 0), stop=(kk == nkk - 1))),
                                    reads=[("ws", s)] + ysv.krange((kk,), tc * 512, 512), writes=[bk(bp)])
                            if i == 0:
                                S.op("dve", (lambda e, bp=bp, st_=st_, mt=mt: e.scalar_tensor_tensor(
                                    out=mt.ap, in0=st_.ap, scalar=1.0, in1=banks[bp][:], op0=ALU.add, op1=ALU.mult)),
                                    reads=[bk(bp)] + st_.keys(), writes=mt.keys())
                            else:
                                tt_ = t_t[(cnt * 3 + i) % 2]
                                S.op("dve", (lambda e, bp=bp, st_=st_, tt_=tt_: e.scalar_tensor_tensor(
                                    out=tt_.ap, in0=st_.ap, scalar=1.0, in1=banks[bp][:], op0=ALU.add, op1=ALU.mult)),
                                    reads=[bk(bp)] + st_.keys(), writes=tt_.keys())
                                if i == 1:
                                    S.op("dve", (lambda e, tt_=tt_, mt=mt: e.tensor_tensor(out=mt.ap, in0=mt.ap, in1=tt_.ap, op=ALU.add)),
                                         reads=mt.keys() + tt_.keys(), writes=mt.keys())
                                else:
                                    S.op("dve", (lambda e, tt_=tt_, mt=mt, c=c, th=th: e.tensor_tensor(
                                        out=merged.ap[:, c, th * 512:(th + 1) * 512], in0=mt.ap, in1=tt_.ap, op=ALU.add)),
                                        reads=mt.keys() + tt_.keys(), writes=merged.krange((c,), th * 512, 512))
                        cnt += 1
                for ob in range(2):
                    s = wslot()
                    wo = wsl[s][:, :].rearrange("p (k n) -> p k n", k=8)
                    wload(s, wo, w_out_d[l][:, ob * 512:(ob + 1) * 512].rearrange("(kc p) n -> p kc n", p=128))
                    for oo in range(4):
                        o = ob * 4 + oo
                        for th in range(2):
                            tc = hh * 2 + th
                            b = nbank("G")
                            for c in range(8):
                                S.op("pe", (lambda e, b=b, c=c, oo=oo, th=th, wo=wo: e.matmul(
                                    banks[b][:], lhsT=wo[:, c, oo * 128:(oo + 1) * 128], rhs=merged.ap[:, c, th * 512:(th + 1) * 512],
                                    start=(c == 0), stop=(c == 7))),
                                    reads=[("ws", s)] + merged.krange((c,), th * 512, 512), writes=[bk(b)])
                            S.op("dve", (lambda e, b=b, o=o, tc=tc: e.scalar_tensor_tensor(
                                out=xT[:, o, tcs(tc)], in0=banks[b][:], scalar=avec[:, 16 + o:17 + o], in1=xT[:, o, tcs(tc)],
                                op0=ALU.mult, op1=ALU.add)),
                                reads=[bk(b), kx(o, tc), "avec"], writes=[kx(o, tc)])

        actT = [AV(0, 128, [4, S_], BF16), AV(16384, 128, [4, S_], BF16)]
        sl_t = [AV(32768, 128, [512], F32), AV(32768 + 2048, 128, [512], F32)]

        def ffn_phase(l):
            nfb = 6
            cnt = 0
            for fb in range(nfb):
                nch = 4 if fb < 5 else 2
                f0 = fb * 512
                ncol = nch * 128
                sg = wslot()
                wg = wsl[sg][:, 0:8 * ncol].rearrange("p (k n) -> p k n", k=8)
                wload(sg, wg, w_gate_d[l][:, f0:f0 + ncol].rearrange("(kc p) n -> p kc n", p=128))
                su = wslot()
                wu = wsl[su][:, 0:8 * ncol].rearrange("p (k n) -> p k n", k=8)
                wload(su, wu, w_up_d[l][:, f0:f0 + ncol].rearrange("(kc p) n -> p kc n", p=128))
                sd = wslot()
                wd = wsl[sd][:, 0:nch * 1024].rearrange("p (j n) -> p j n", j=nch)
                wload(sd, wd, w_down_d[l][f0:f0 + ncol, :].rearrange("(j p) n -> p j n", p=128))
                at = actT[fb % 2]
                for j in range(nch):
                    for tc in range(NTC):
                        bg_ = nbank("G")
                        fm_proj(lambda kc, j=j: (wg[:, kc, j * 128:(j + 1) * 128], [("ws", sg)]), bg_, tc)
                        bu_ = nbank("G")
                        fm_proj(lambda kc, j=j: (wu[:, kc, j * 128:(j + 1) * 128], [("ws", su)]), bu_, tc)
                        sl = sl_t[cnt % 2]
                        cnt += 1
                        S.op("act", (lambda e, bg_=bg_, sl=sl: e.activation(out=sl.ap, in_=banks[bg_][:], func=AF.Silu)),
                             reads=[bk(bg_)], writes=sl.keys())
                        S.op("dve", (lambda e, bu_=bu_, sl=sl, at=at, j=j, tc=tc: e.tensor_tensor(
                            out=at.ap[:, j, tcs(tc)], in0=banks[bu_][:], in1=sl.ap, op=ALU.mult)),
                            reads=[bk(bu_)] + sl.keys(), writes=at.krange((j,), tc * 512, 512))
                for o in range(8):
                    for tc in range(NTC):
                        b = nbank("G")
                        for j in range(nch):
                            S.op("pe", (lambda e, b=b, j=j, o=o, tc=tc, at=at, nch=nch, wd=wd: e.matmul(
                                banks[b][:], lhsT=wd[:, j, o * 128:(o + 1) * 128], rhs=at.ap[:, j, tcs(tc)],
                                start=(j == 0), stop=(j == nch - 1))),
                                reads=[("ws", sd)] + at.krange((j,), tc * 512, 512), writes=[bk(b)])
                        S.op("dve", (lambda e, b=b, o=o, tc=tc: e.scalar_tensor_tensor(
                            out=xT[:, o, tcs(tc)], in0=banks[b][:], scalar=modT[:, 40 + o:41 + o], in1=xT[:, o, tcs(tc)],
                            op0=ALU.mult, op1=ALU.add)),
                            reads=[bk(b), kx(o, tc), "modT"], writes=[kx(o, tc)])

        import os as _os
        KSTOP = int(_os.environ.get("KSTOP", "99"))
        for l in range(L):
            if KSTOP < 1:
                break
            modulation(l)
            S.op("dve", lambda e: e.tensor_scalar(out=avec[:, 16:24], in0=modT[:, 16:24], scalar1=0.5, scalar2=None, op0=ALU.mult),
                 reads=["modT"], writes=["avec"])
            if KSTOP >= 2:
                norm_phase(0, 0)
            if KSTOP >= 3:
                attention_phase(l)
            if KSTOP >= 4:
                pool_phase(l)
            if KSTOP >= 5:
                conv_phase(l)
            if KSTOP >= 6:
                merge_phase(l)
            if KSTOP >= 7:
                norm_phase(8, 24)
            if KSTOP >= 8:
                ffn_phase(l)

        xo = [AV(0, 128, [D], F32), AV(4096, 128, [D], F32)]
        for t in range(NT):
            xv = xo[t % 2]
            for half in range(2):
                b = nbank("G")
                for cq in range(4):
                    c = half * 4 + cq
                    S.op("pe", (lambda e, b=b, cq=cq, c=c, t=t: e.transpose(
                        banks[b][:, cq * 128:(cq + 1) * 128], xT[:, c, t * 128:(t + 1) * 128], consts[:, O_IDF:O_IDF + 128])),
                        reads=[kx(c, t // 4), "consts"], writes=[bk(b)])
                copy_op(evac_eng(), xv.ap[:, half * 512:(half + 1) * 512], banks[b][:], [bk(b)], xv.keys(half * 512, 512))
            S.op("sp", (lambda e, xv=xv, t=t: e.dma_start(out=out_d[t * 128:(t + 1) * 128, :], in_=xv.ap)),
                 reads=xv.keys(), dma=True)
        S.emit(st)
    return nc


def _consts():
    c = np.zeros((128, NCF), np.float32)
    c[:, O_IDF:O_IDF + 128] = np.eye(128, dtype=np.float32)
    kk = np.arange(128)[:, None]
    qq = np.arange(128)[None, :]
    c[:, O_TRI:O_TRI + 128] = (kk <= qq).astype(np.float32)
    c[:, O_FREQ:O_FREQ + 8] = (500000.0 ** (-np.arange(0, 16, 2, dtype=np.float32) / 16.0)).astype(np.float32)[None, :]
    for p in range(128):
        for cc in range(2):
            w = POOL_W[cc * 2 + (1 if p >= 64 else 0)]
            c[p, O_INVW + cc] = 1.0 / w
            for t in range(16):
                c[p, O_CORR + cc * 16 + t] = 1.0 / min(t + 1, w)
    return c


def _small(b, c, norm1_w, norm2_w, b_ada, pool_scale, conv_w, q_norm_w, k_norm_w, l0, L):
    s = np.zeros((128, NS), np.float32)
    for l in range(L):
        gl = l0 + l
        s[:, O_N1W + l * 8:O_N1W + (l + 1) * 8] = norm1_w[gl].reshape(8, 128).T
        s[:, O_N2W + l * 8:O_N2W + (l + 1) * 8] = norm2_w[gl].reshape(8, 128).T
        s[:, O_BADA + l * 48:O_BADA + (l + 1) * 48] = b_ada[gl].reshape(48, 128).T
        s[:, O_PSC + l * 2:O_PSC + (l + 1) * 2] = pool_scale[gl].reshape(2, 128).T
        s[:, O_CW + l * 6:O_CW + (l + 1) * 6] = conv_w[gl].reshape(3, 2, 128).transpose(2, 0, 1).reshape(128, 6)
        s[:, O_QKW + l * 128:O_QKW + l * 128 + 64] = q_norm_w[gl][None, :]
        s[:, O_QKW + l * 128 + 64:O_QKW + (l + 1) * 128] = k_norm_w[gl][None, :]
    s[:, O_CT:O_CT + 8] = c[b].reshape(8, 128).T
    return s


_NC_CACHE = {}


def _get_nc(L):
    if L not in _NC_CACHE:
        _NC_CACHE[L] = build(L)
    return _NC_CACHE[L]


def _run(x, c, positions, P, l0, L):
    nc = build(L)
    consts = _consts()
    in_maps = []
    wsl_ = slice(l0, l0 + L)
    for b in range(8):
        m = {
            "x": np.ascontiguousarray(x[b]),
            "small": _small(b, c, P["norm1_w"], P["norm2_w"], P["b_ada"], P["pool_scale"], P["conv_w"],
                            P["q_norm_w"], P["k_norm_w"], l0, L),
            "consts": consts,
            "pos": np.ascontiguousarray(positions[b].reshape(NT, 128).T.astype(np.int32)),
        }
        for k in ("w_ada", "w_in", "pool_w", "p_pool", "p_conv", "p_attn", "w_out", "w_gate", "w_up", "w_down"):
            m[k] = np.ascontiguousarray(P[k][wsl_])
        in_maps.append(m)
    res = run_bass_kernel_spmd(nc, in_maps, core_ids=list(range(8)))
    return np.stack([np.asarray(r["out"]) for r in res.results], axis=0)


def kernel(x, c, positions, norm1_w, norm2_w, w_ada, b_ada, w_in, pool_w, pool_scale, conv_w,
           q_norm_w, k_norm_w, p_pool, p_conv, p_attn, w_out, w_gate, w_up, w_down):
    P = dict(norm1_w=np.asarray(norm1_w), norm2_w=np.asarray(norm2_w), w_ada=np.asarray(w_ada), b_ada=np.asarray(b_ada),
             w_in=np.asarray(w_in), pool_w=np.asarray(pool_w), pool_scale=np.asarray(pool_scale), conv_w=np.asarray(conv_w),
             q_norm_w=np.asarray(q_norm_w), k_norm_w=np.asarray(k_norm_w), p_pool=np.asarray(p_pool), p_conv=np.asarray(p_conv),
             p_attn=np.asarray(p_attn), w_out=np.asarray(w_out), w_gate=np.asarray(w_gate), w_up=np.asarray(w_up),
             w_down=np.asarray(w_down))
    x = np.asarray(x, dtype=np.float32)
    c = np.asarray(c)
    positions = np.asarray(positions)
    if FUSED:
        out = _run(x, c, positions, P, 0, DEPTH)
    else:
        out = x
        for l in range(DEPTH):
            out = _run(out, c, positions, P, l, 1)
    return out.astype(np.float32)
```

```python
import math
import os as _os
from contextlib import ExitStack

import numpy as np
import concourse.bass as bass
import concourse.mybir as mybir
from concourse.bass_utils import run_bass_kernel_spmd

F32 = mybir.dt.float32
BF16 = mybir.dt.bfloat16
I32 = mybir.dt.int32
ALU = mybir.AluOpType
AF = mybir.ActivationFunctionType
AX = mybir.AxisListType

FUSED = True
DEPTH = 4
D = 1024
S_ = 2048
NT = 16
NTC = 4
DFF = 2816
EPS = 1e-6
NEGB = -30000.0
O_N1W, O_N2W, O_BADA, O_PSC, O_CW, O_QKW, O_CT, NS = 0, 32, 64, 256, 264, 288, 800, 808
O_IDF, O_TRI, O_FREQ, O_INVW, O_CORR, NCF = 0, 128, 256, 264, 266, 298
POOL_W = (2, 4, 8, 16)

ENGS = ["pe", "act", "dve", "pool", "sp"]


class Op:
    __slots__ = ("eng", "fn", "deps", "is_dma", "sig", "has_dep", "name", "pos")

    def __init__(self, eng, fn, is_dma, name=None):
        self.eng = eng
        self.fn = fn
        self.deps = []
        self.is_dma = is_dma
        self.sig = None
        self.has_dep = False
        self.name = name


class Sched:
    def __init__(self, nc, n_dma_sems=8, maxv=12000):
        self.nc = nc
        self.ops = {e: [] for e in ENGS}
        self.last_writer = {}
        self.readers = {}
        self.n_dma_sems = n_dma_sems
        self.maxv = maxv
        self.cow = {}
        self.old_r = {}
        self.old_w = {}

    def new_gen(self, k):
        self.old_r[k] = self.readers.get(k, [])
        self.old_w[k] = self.cow.get(k, [])
        self.readers[k] = []
        self.cow[k] = []

    def op(self, eng, fn, reads=(), writes=(), dma=False, name=None, joins=()):
        o = Op(eng, fn, dma, name)
        deps = {}
        lw = self.last_writer
        rd = self.readers
        if eng != "pe":
            extra = [k for k in reads if isinstance(k, tuple) and k[0] == "bank"]
            if extra:
                writes = list(writes) + extra
        for k in joins:
            for r in self.old_r.get(k, ()):
                deps[id(r)] = (r, "war")
            for w in self.old_w.get(k, ()):
                if id(w) not in deps:
                    deps[id(w)] = (w, "waw")
            self.cow.setdefault(k, []).append(o)
        for k in reads:
            w = lw.get(k)
            if w is not None:
                deps[id(w)] = (w, "raw")
            for w in self.cow.get(k, ()):
                deps[id(w)] = (w, "raw")
        for k in writes:
            w = lw.get(k)
            if w is not None and id(w) not in deps:
                deps[id(w)] = (w, "waw")
            for r in rd.get(k, ()):
                if id(r) not in deps:
                    deps[id(r)] = (r, "war")
        for k in reads:
            rd.setdefault(k, []).append(o)
        for k in writes:
            lw[k] = o
            rd[k] = []
        best = {}
        for d, kind in deps.values():
            if d.eng == eng and not d.is_dma and not dma:
                if eng == "pe":
                    continue
            if d.is_dma:
                o.deps.append(d)
                d.has_dep = True
            else:
                b = best.get(d.eng)
                if b is None or d.pos > b.pos:
                    best[d.eng] = d
        for d in best.values():
            o.deps.append(d)
            d.has_dep = True
        o.pos = len(self.ops[eng])
        self.ops[eng].append(o)
        return o

    def emit(self, stack):
        nc = self.nc
        for e in ENGS:
            cnt = 0
            nsem = 0
            cur = None
            for o in self.ops[e]:
                if o.is_dma or not o.has_dep:
                    continue
                if cur is None or cnt >= self.maxv:
                    cur = stack.enter_context(nc.semaphore(f"s_{e}_{nsem}"))
                    nsem += 1
                    cnt = 0
                cnt += 1
                o.sig = (cur, cnt, 1)
        for e in ENGS:
            dl = [o for o in self.ops[e] if o.is_dma]
            if not dl:
                continue
            sems = [stack.enter_context(nc.semaphore(f"d_{e}_{i}")) for i in range(min(self.n_dma_sems, len(dl)))]
            cnts = [0] * len(sems)
            for i, o in enumerate(dl):
                j = i % len(sems)
                cnts[j] += 1
                o.sig = (sems[j], 16 * cnts[j], 16)

        def run_engine(ename, eng):
            waited = {}
            for o in self.ops[ename]:
                need = {}
                for d in o.deps:
                    s, v = d.sig[0], d.sig[1]
                    if waited.get(s, 0) >= v:
                        continue
                    if need.get(s, 0) < v:
                        need[s] = v
                if o.is_dma:
                    s, v = o.sig[0], o.sig[1]
                    if v > 16 and waited.get(s, 0) < v - 16:
                        need[s] = max(need.get(s, 0), v - 16)
                for s, v in need.items():
                    eng.wait_ge(s, v)
                    waited[s] = v
                ins = o.fn(eng)
                if o.sig is not None:
                    ins.then_inc(o.sig[0], o.sig[2])
            last = {}
            for o in self.ops[ename]:
                if o.is_dma:
                    last[o.sig[0]] = o.sig[1]
            for s, v in last.items():
                if waited.get(s, 0) < v:
                    eng.wait_ge(s, v)

        with nc.Block() as block:
            @block.tensor
            def _(e):
                run_engine("pe", e)

            @block.scalar
            def _(e):
                run_engine("act", e)

            @block.vector
            def _(e):
                run_engine("dve", e)

            @block.gpsimd
            def _(e):
                run_engine("pool", e)

            @block.sync
            def _(e):
                run_engine("sp", e)


ARENA = 69632
BLK = 1024


def build(L):
    nc = bass.Bass("TRN2", target_bir_lowering=False)

    def din(name, shape, dt=F32):
        return nc.dram_tensor(name, shape, dt, kind="ExternalInput").ap()

    x_d = din("x", [S_, D])
    small_d = din("small", [128, NS])
    consts_d = din("consts", [128, NCF])
    pos_d = din("pos", [128, NT], I32)
    w_ada_d = din("w_ada", [L, D, 6 * D])
    w_in_d = din("w_in", [L, D, 5632])
    pool_w_d = din("pool_w", [L, 4, 64, 64])
    p_pool_d = din("p_pool", [L, 256, D])
    p_conv_d = din("p_conv", [L, 256, D])
    p_attn_d = din("p_attn", [L, 512, D])
    w_out_d = din("w_out", [L, D, D])
    w_gate_d = din("w_gate", [L, D, DFF])
    w_up_d = din("w_up", [L, D, DFF])
    w_down_d = din("w_down", [L, DFF, D])
    lidx_d = None
    out_d = nc.dram_tensor("out", [S_, D], F32, kind="ExternalOutput").ap()

    with ExitStack() as st:
        S = Sched(nc)

        def sb(name, shape, dt):
            return st.enter_context(nc.sbuf_tensor("sb_" + name, shape, dt))

        xT = sb("xT", [128, 8, S_], F32)
        hT = sb("hT", [128, 8, S_], BF16)
        wsl = [sb(f"ws{i}", [128, 4096], BF16) for i in range(4)]
        small = sb("small", [128, NS], F32)
        consts = sb("consts", [128, NCF], F32)
        identb = sb("identb", [128, 128], BF16)
        trib = sb("trib", [128, 128], BF16)
        onesb = sb("onesb", [128, 128], BF16)
        modTs = [sb("modT0", [128, 48], F32), sb("modT1", [128, 48], F32)]
        cur = {"lp": 0}
        avecs = [sb("avec0", [128, 32], F32), sb("avec1", [128, 32], F32)]
        posi = sb("posi", [128, NT], I32)
        posf = sb("posf", [128, NT], F32)
        ang = sb("ang", [128, NT, 8], F32)
        cosT = sb("cosT", [128, NT, 8], F32)
        sinT = sb("sinT", [128, NT, 8], F32)
        cact = sb("cact", [128, 8], F32)
        cactb = sb("cactb", [128, 8], BF16)
        poolW = sb("poolW", [128, 2, 128], BF16)
        kmT = sb("kmT", [64, 4, 8], BF16)
        kmf = sb("kmf", [64, 4], F32)
        arena = sb("arena", [128, ARENA // 4], F32)
        arena_b = arena.bitcast(BF16)
        banks = [st.enter_context(nc.psum_tensor(f"bank{i}", [128, 512], F32)) for i in range(8)]
        banks_b = [b.bitcast(BF16) for b in banks]

        class AV:
            def __init__(self, base, P, fshape, dt):
                self.base = base
                self.P = P
                self.fshape = list(fshape)
                self.dt = dt
                self.esz = 2 if dt == BF16 else 4
                n = int(np.prod(fshape))
                self.n = n
                assert base % 32 == 0 and base + n * self.esz <= ARENA, (base, n, self.esz)
                src = arena_b if dt == BF16 else arena
                e0 = base // self.esz
                flat = src[0:P, e0:e0 + n]
                if len(fshape) == 1:
                    self.ap = flat
                elif len(fshape) == 2:
                    self.ap = flat.rearrange("p (a b) -> p a b", a=fshape[0])
                elif len(fshape) == 3:
                    self.ap = flat.rearrange("p (a b c) -> p a b c", a=fshape[0], b=fshape[1])
                else:
                    raise ValueError

            def keys(self, lo=0, n=None):
                if n is None:
                    n = self.n - lo
                b0 = (self.base + lo * self.esz) // BLK
                b1 = (self.base + (lo + n) * self.esz - 1) // BLK
                return [("A", b) for b in range(b0, b1 + 1)]

            def kidx(self, *idx):
                strides = []
                s = 1
                for d in reversed(self.fshape):
                    strides.append(s)
                    s *= d
                strides = strides[::-1]
                lo = sum(i * st_ for i, st_ in zip(idx, strides))
                n = strides[len(idx) - 1] if idx else self.n
                return self.keys(lo, n)

            def krange(self, idx, lo, n):
                strides = []
                s = 1
                for d in reversed(self.fshape):
                    strides.append(s)
                    s *= d
                strides = strides[::-1]
                base = sum(i * st_ for i, st_ in zip(idx, strides))
                return self.keys(base + lo, n)

        bank_rr = {"F": [0, [0, 1, 2, 3, 4, 5]], "n": [0, [6]], "g": [0, [2, 3]], "s": [0, [4, 5]], "a": [0, [6, 7]], "G": [0, [0, 1, 2, 3, 4, 5, 6, 7]]}

        def nbank(pool):
            st_ = bank_rr[pool]
            b = st_[1][st_[0] % len(st_[1])]
            st_[0] += 1
            return b

        def bk(b):
            return ("bank", b)

        ws_rr = [0]

        ws_grp = {"ffn": [0, [0, 1, 2]], "mod": [0, [3]]}

        def wslot(grp=None):
            if grp is None:
                i = ws_rr[0] % 4
                ws_rr[0] += 1
            else:
                g_ = ws_grp[grp]
                i = g_[1][g_[0] % len(g_[1])]
                g_[0] += 1
            S.new_gen(("ws", i))
            return i

        def wload(slot, dst_ap, src_ap):
            S.op("pool", lambda e: e.dma_start(out=dst_ap, in_=src_ap), joins=[("ws", slot)], dma=True)

        def kx(c, tc):
            return ("xT", c, tc)

        def kh(c, tc):
            return ("hT", c, tc)

        def tcs(tc):
            return slice(tc * 512, (tc + 1) * 512)

        evac_rr = [0]

        def evac_eng():
            evac_rr[0] += 1
            return "act" if evac_rr[0] % 2 else "dve"

        def copy_op(eng, out_ap, in_ap, reads, writes):
            if eng == "act":
                S.op("act", lambda e: e.copy(out=out_ap, in_=in_ap), reads=reads, writes=writes)
            else:
                S.op(eng, lambda e: e.tensor_copy(out=out_ap, in_=in_ap), reads=reads, writes=writes)

        S.op("sp", lambda e: e.dma_start(out=small[:], in_=small_d), writes=["small"], dma=True)
        S.op("sp", lambda e: e.dma_start(out=consts[:], in_=consts_d), writes=["consts"], dma=True)
        S.op("sp", lambda e: e.dma_start(out=posi[:], in_=pos_d), writes=["posi"], dma=True)
        S.op("dve", lambda e: e.tensor_copy(out=identb[:], in_=consts[:, O_IDF:O_IDF + 128]), reads=["consts"], writes=["identb"])
        S.op("dve", lambda e: e.tensor_copy(out=trib[:], in_=consts[:, O_TRI:O_TRI + 128]), reads=["consts"], writes=["trib"])
        S.op("dve", lambda e: e.memset(onesb[:], 1.0), writes=["onesb"])
        S.op("dve", lambda e: e.tensor_copy(out=posf[:], in_=posi[:]), reads=["posi"], writes=["posf"])
        S.op("dve", lambda e: e.tensor_tensor(out=ang[:], in0=posf[:].unsqueeze(2).to_broadcast([128, NT, 8]),
                                              in1=consts[:, O_FREQ:O_FREQ + 8].unsqueeze(1).to_broadcast([128, NT, 8]),
                                              op=ALU.mult), reads=["posf", "consts"], writes=["ang"])
        TWO_PI = 2.0 * math.pi
        ki = sb("ki", [128, NT, 8], I32)
        kf = sb("kf", [128, NT, 8], F32)
        rr = sb("rr", [128, NT, 8], F32)
        mm_ = sb("mm_", [128, NT, 8], F32)

        def sincos(dst, shift, nm):
            S.op("dve", lambda e: e.tensor_scalar(out=rr[:], in0=ang[:], scalar1=shift, scalar2=1.0 / TWO_PI, op0=ALU.add, op1=ALU.mult),
                 reads=["ang"], writes=["rr"])
            S.op("dve", lambda e: e.tensor_copy(out=ki[:], in_=rr[:]), reads=["rr"], writes=["ki"])
            S.op("dve", lambda e: e.tensor_copy(out=kf[:], in_=ki[:]), reads=["ki"], writes=["kf"])
            S.op("dve", lambda e: e.tensor_scalar(out=rr[:], in0=ang[:], scalar1=shift, scalar2=None, op0=ALU.add),
                 reads=["ang", "kf"], writes=["rr"])
            S.op("dve", lambda e: e.scalar_tensor_tensor(out=rr[:], in0=kf[:], scalar=-TWO_PI, in1=rr[:], op0=ALU.mult, op1=ALU.add),
                 reads=["kf", "rr"], writes=["rr"])
            S.op("dve", lambda e: e.tensor_scalar(out=mm_[:], in0=rr[:], scalar1=math.pi, scalar2=-TWO_PI, op0=ALU.is_gt, op1=ALU.mult),
                 reads=["rr"], writes=["mm_"])
            S.op("dve", lambda e: e.tensor_tensor(out=rr[:], in0=rr[:], in1=mm_[:], op=ALU.add), reads=["rr", "mm_"], writes=["rr"])
            S.op("dve", lambda e: e.tensor_scalar(out=mm_[:], in0=rr[:], scalar1=-math.pi, scalar2=TWO_PI, op0=ALU.is_lt, op1=ALU.mult),
                 reads=["rr"], writes=["mm_"])
            S.op("dve", lambda e: e.tensor_tensor(out=rr[:], in0=rr[:], in1=mm_[:], op=ALU.add), reads=["rr", "mm_"], writes=["rr"])
            S.op("act", lambda e: e.activation(out=dst[:], in_=rr[:], func=AF.Sin, scale=1.0 - 1e-6), reads=["rr"], writes=[nm])

        sincos(sinT, 0.0, "sinT")
        sincos(cosT, 0.5 * math.pi, "cosT")
        S.op("act", lambda e: e.activation(out=cact[:], in_=small[:, O_CT:O_CT + 8], func=AF.Silu), reads=["small"], writes=["cact"])
        S.op("dve", lambda e: e.tensor_copy(out=cactb[:], in_=cact[:]), reads=["cact"], writes=["cactb"])

        xin = [AV(0, 128, [4, D], F32), AV(16384, 128, [4, D], F32)]
        for tc in range(NTC):
            xv = xin[tc % 2]
            S.op("sp", (lambda e, xv=xv, tc=tc: e.dma_start(
                out=xv.ap, in_=x_d[tc * 512:(tc + 1) * 512, :].rearrange("(t p) d -> p t d", p=128))),
                writes=xv.keys(), dma=True)
            for c in range(8):
                b = nbank("G")
                for t in range(4):
                    S.op("pe", (lambda e, b=b, xv=xv, t=t, c=c: e.transpose(
                        banks[b][:, t * 128:(t + 1) * 128], xv.ap[:, t, c * 128:(c + 1) * 128], consts[:, O_IDF:O_IDF + 128])),
                        reads=xv.keys() + ["consts"], writes=[bk(b)])
                copy_op(evac_eng(), xT[:, c, tcs(tc)], banks[b][:], [bk(b)], [kx(c, tc)])

        def mod_steps(l, grp=None):
            lp = l % 2
            modT, avec = modTs[lp], avecs[lp]
            kM, kA = ("modT", lp), ("avec", lp)
            pm = 7
            steps = []
            for j in range(12):
                def sj(j=j):
                    s = wslot(grp)
                    wv = wsl[s][:, :].rearrange("p (k n) -> p k n", k=8)
                    wload(s, wv, w_ada_d[l][:, j * 512:(j + 1) * 512].rearrange("(kc p) n -> p kc n", p=128))
                    for oc in range(4):
                        col = j * 4 + oc
                        for kc in range(8):
                            S.op("pe", (lambda e, oc=oc, kc=kc, col=col: e.matmul(
                                banks[pm][:, col:col + 1], lhsT=wv[:, kc, oc * 128:(oc + 1) * 128], rhs=cactb[:, kc:kc + 1],
                                start=(kc == 0), stop=(kc == 7), skip_group_check=True)),
                                reads=[("ws", s), "cactb"], writes=[bk(pm)])
                steps.append(sj)

            def fin():
                S.op("dve", (lambda e: e.tensor_tensor(out=modT[:], in0=banks[pm][:, 0:48],
                                                       in1=small[:, O_BADA + l * 48:O_BADA + (l + 1) * 48], op=ALU.add)),
                     reads=[bk(pm), "small"], writes=[kM])
                S.op("dve", (lambda e: e.scalar_tensor_tensor(out=avec[:, 0:8], in0=modT[:, 8:16], scalar=1.0,
                                                              in1=small[:, O_N1W + l * 8:O_N1W + (l + 1) * 8],
                                                              op0=ALU.add, op1=ALU.mult)),
                     reads=[kM, "small"], writes=[kA])
                S.op("dve", (lambda e: e.scalar_tensor_tensor(out=avec[:, 8:16], in0=modT[:, 32:40], scalar=1.0,
                                                              in1=small[:, O_N2W + l * 8:O_N2W + (l + 1) * 8],
                                                              op0=ALU.add, op1=ALU.mult)),
                     reads=[kM, "small"], writes=[kA])
                S.op("dve", lambda e: e.tensor_scalar(out=avec[:, 16:24], in0=modT[:, 16:24], scalar1=0.5, scalar2=None, op0=ALU.mult),
                     reads=[kM], writes=[kA])
            steps.append(fin)
            return steps

        def norm_tc_steps(a_off, sh_off, lp, base=0):
            modT, avec = modTs[lp], avecs[lp]
            sqb = AV(base, 128, [8, 512], BF16)
            rst = [AV(base + 8192, 128, [512], F32), AV(base + 10240, 128, [512], F32)]
            tmp = [AV(base + 12288, 128, [512], F32), AV(base + 14336, 128, [512], F32), AV(base + 16384, 128, [512], F32)]
            st_ = {"ti": 0}

            def mk(tc):
                def f():
                    for c in range(8):
                        if c % 2 == 0:
                            S.op("act", (lambda e, c=c: e.activation(out=sqb.ap[:, c, :], in_=xT[:, c, tcs(tc)], func=AF.Square)),
                                 reads=[kx(c, tc)], writes=sqb.kidx(c))
                        else:
                            S.op("dve", (lambda e, c=c: e.tensor_tensor(out=sqb.ap[:, c, :], in0=xT[:, c, tcs(tc)],
                                                                        in1=xT[:, c, tcs(tc)], op=ALU.mult)),
                                 reads=[kx(c, tc)], writes=sqb.kidx(c))
                    b = nbank("n")
                    for c in range(8):
                        S.op("pe", (lambda e, c=c: e.matmul(banks[b][:], lhsT=onesb[:], rhs=sqb.ap[:, c, :],
                                                           start=(c == 0), stop=(c == 7))),
                             reads=sqb.kidx(c) + ["onesb"], writes=[bk(b)])
                    r = rst[tc % 2]
                    S.op("act", (lambda e: e.activation(out=r.ap, in_=banks[b][:], func=AF.Sqrt, bias=epsb[:, 0:1], scale=1.0 / D)),
                         reads=[bk(b), "epsb"], writes=r.keys())
                    S.op("dve", (lambda e: e.reciprocal(out=r.ap, in_=r.ap)), reads=r.keys(), writes=r.keys())
                    for c in range(8):
                        t_ = tmp[st_["ti"] % 3]
                        st_["ti"] += 1
                        S.op("dve", (lambda e, t_=t_, c=c: e.tensor_tensor(out=t_.ap, in0=xT[:, c, tcs(tc)], in1=r.ap, op=ALU.mult)),
                             reads=[kx(c, tc)] + r.keys(), writes=t_.keys())
                        S.op("act", (lambda e, t_=t_, c=c: e.activation(
                            out=hT[:, c, tcs(tc)], in_=t_.ap, func=AF.Identity,
                            scale=avec[:, a_off + c:a_off + c + 1], bias=modT[:, sh_off + c:sh_off + c + 1])),
                            reads=t_.keys() + [("avec", lp), ("modT", lp)], writes=[kh(c, tc)])
                return f
            return [mk(tc) for tc in range(NTC)]

        epsb = sb("epsb", [128, 1], F32)
        S.op("dve", lambda e: e.memset(epsb[:], EPS), writes=["epsb"])

        yattnT = AV(0, 128, [4, S_], BF16)
        ypoolT = AV(16384, 128, [2, S_], BF16)
        yconvT = AV(24576, 128, [2, S_], BF16)
        A1 = 32768
        KaT = AV(A1, 72, [4, S_], BF16)
        Va = AV(A1 + 16384, 128, [NT, 4, 65], BF16)
        QaTs = [AV(A1 + 24736, 72, [4, 512], BF16), AV(65536, 72, [4, 512], BF16)]
        o_ = A1 + 24736 + 4096
        qk_tok = [AV(o_, 128, [8, 72], BF16), AV(o_ + 1152, 128, [8, 72], BF16)]
        o_ += 2304
        mb_tok = [AV(o_, 128, [4, 72], BF16), AV(o_ + 576, 128, [4, 72], BF16)]
        o_ += 1152
        qn_s = [AV(16384, 128, [8, 64], F32), AV(18432, 128, [8, 64], F32)]
        ytok = AV(20480, 128, [4, 256], BF16)
        pT = [AV(22528 + i * 1024, 128, [512], BF16) for i in range(4)]
        sq_s = AV(26624, 128, [512], BF16)
        cmp_s = AV(27648, 128, [4, 8, 8], F32)
        sm_par = []
        for i_ in range(2):
            o2 = 28672 + i_ * 1920
            sm_par.append(dict(
                ssq8=AV(o2, 128, [8], F32), vv8=AV(o2 + 32, 128, [8], F32), yy8=AV(o2 + 64, 128, [8], F32),
                aa8=AV(o2 + 96, 128, [8], F32), gsb=AV(o2 + 128, 128, [4, 8], F32), rank_s=AV(o2 + 256, 128, [4, 8], F32),
                rt=[AV(o2 + 384 + k_ * 256, 128, [8, 8], F32) for k_ in range(4)],
                qr=AV(o2 + 1408, 128, [8, 16], F32)))
        rec_s = AV(65056, 128, [4], F32)

        def merge_steps(lists):
            idx = [0] * len(lists)
            tot = [max(1, len(x)) for x in lists]
            while True:
                best = None
                for i, x in enumerate(lists):
                    if idx[i] < len(x):
                        frac = idx[i] / tot[i]
                        if best is None or frac < best[0]:
                            best = (frac, i)
                if best is None:
                    break
                i = best[1]
                lists[i][idx[i]]()
                idx[i] += 1

        def tile_steps(l, hg, qc, j, wqk, wv, sqk, sv):
            t = qc * 4 + j
            blk = t // 2
            QaT = QaTs[qc % 2]
            qn = qn_s[t % 2]
            qt = qk_tok[t % 2]
            mb = mb_tok[t % 2]
            sp_ = sm_par[t % 2]
            ssq8, vv8, yy8, aa8, gsb, rank_s, rt, qr = (sp_["ssq8"], sp_["vv8"], sp_["yy8"], sp_["aa8"], sp_["gsb"],
                                                        sp_["rank_s"], sp_["rt"], sp_["qr"])
            st_ = {}
            steps = []

            def s1():
                bq = st_["bq"] = t % 2
                S.op("pool", (lambda e: e.memset(qt.ap[:, 4:8, 64:72], 0.0)), writes=qt.keys())
                S.op("pool", (lambda e: e.memset(qt.ap[:, 4:8, 64 + blk:65 + blk], 1.0)), writes=qt.keys())
                S.op("pool", (lambda e: e.memset(mb.ap[:, :, 64:72], NEGB)), writes=mb.keys())
                S.op("pool", (lambda e: e.memset(mb.ap[:, :, 64:65 + blk], 0.0)), writes=mb.keys())
                for kc in range(8):
                    S.op("pe", (lambda e, kc=kc: e.matmul(
                        banks[bq][:], lhsT=hT[:, kc, t * 128:(t + 1) * 128], rhs=wqk[:, kc, :],
                        start=(kc == 0), stop=(kc == 7))),
                        reads=[kh(kc, t // 4), ("ws", sqk)], writes=[bk(bq)])
                bv = nbank("g")
                for kc in range(8):
                    S.op("pe", (lambda e, kc=kc: e.matmul(
                        banks[bv][:, 0:256], lhsT=hT[:, kc, t * 128:(t + 1) * 128], rhs=wv[:, kc, :],
                        start=(kc == 0), stop=(kc == 7))),
                        reads=[kh(kc, t // 4), ("ws", sv)], writes=[bk(bv)])
                S.op("act", (lambda e: e.activation(out=sq_s.ap, in_=banks[bq][:], func=AF.Square)),
                     reads=[bk(bq)], writes=sq_s.keys())
                S.op("act", (lambda e: e.copy(
                    out=Va.ap[:, t, :, 0:64], in_=banks[bv][:, 0:256].rearrange("p (h d) -> p h d", h=4))),
                    reads=[bk(bv)], writes=Va.kidx(t))
            steps.append(s1)

            def s2():
                S.op("dve", lambda e: e.tensor_reduce(out=ssq8.ap, in_=sq_s.ap.rearrange("p (h d) -> p h d", h=8),
                                                      axis=AX.X, op=ALU.add),
                     reads=sq_s.keys(), writes=ssq8.keys())
                S.op("dve", lambda e: e.tensor_scalar(out=vv8.ap, in0=ssq8.ap, scalar1=1.0 / 64, scalar2=EPS, op0=ALU.mult, op1=ALU.add),
                     reads=ssq8.keys(), writes=vv8.keys())
                S.op("dve", lambda e: e.tensor_scalar(out=yy8.ap.bitcast(I32), in0=vv8.ap.bitcast(I32), scalar1=-0.5, scalar2=1597463007.0,
                                                      op0=ALU.mult, op1=ALU.add),
                     reads=vv8.keys(), writes=yy8.keys())
                for _ in range(2):
                    S.op("dve", lambda e: e.tensor_tensor(out=aa8.ap, in0=yy8.ap, in1=yy8.ap, op=ALU.mult),
                         reads=yy8.keys(), writes=aa8.keys())
                    S.op("dve", lambda e: e.scalar_tensor_tensor(out=aa8.ap, in0=aa8.ap, scalar=-0.5, in1=vv8.ap, op0=ALU.mult, op1=ALU.mult),
                         reads=aa8.keys() + vv8.keys(), writes=aa8.keys())
                    S.op("dve", lambda e: e.scalar_tensor_tensor(out=yy8.ap, in0=aa8.ap, scalar=1.5, in1=yy8.ap, op0=ALU.add, op1=ALU.mult),
                         reads=yy8.keys() + aa8.keys(), writes=yy8.keys())
            steps.append(s2)

            def s3():
                bq = st_["bq"]
                wq = small[:, O_QKW + l * 128:O_QKW + (l + 1) * 128].rearrange("p (a d) -> p a d", a=2)
                S.op("dve", (lambda e: e.tensor_tensor(
                    out=qn.ap, in0=banks[bq][:].rearrange("p (h d) -> p h d", h=8),
                    in1=yy8.ap.unsqueeze(2).to_broadcast([128, 8, 64]), op=ALU.mult)),
                    reads=[bk(bq)] + yy8.keys(), writes=qn.keys())
                S.op("dve", (lambda e: e.tensor_tensor(
                    out=qr.ap.rearrange("p (a h) d -> p a h d", a=2), in0=qn.ap[:, :, 0:16].rearrange("p (a h) d -> p a h d", a=2),
                    in1=wq[:, :, 0:16].unsqueeze(2).to_broadcast([128, 2, 4, 16]), op=ALU.mult)),
                    reads=qn.keys() + ["small"], writes=qr.keys())
                S.op("dve", (lambda e: e.tensor_tensor(
                    out=qt.ap[:, :, 16:64].rearrange("p (a h) d -> p a h d", a=2), in0=qn.ap[:, :, 16:64].rearrange("p (a h) d -> p a h d", a=2),
                    in1=wq[:, :, 16:64].unsqueeze(2).to_broadcast([128, 2, 4, 48]), op=ALU.mult)),
                    reads=qn.keys() + ["small"], writes=qt.keys())
            steps.append(s3)

            def s4():
                cb = cosT[:, t, :].unsqueeze(1).to_broadcast([128, 8, 8])
                sbb = sinT[:, t, :].unsqueeze(1).to_broadcast([128, 8, 8])
                x1 = qr.ap[:, :, 0:8]
                x2 = qr.ap[:, :, 8:16]
                S.op("dve", (lambda e: e.tensor_tensor(out=rt[0].ap, in0=x1, in1=cb, op=ALU.mult)),
                     reads=qr.keys() + ["cosT"], writes=rt[0].keys())
                S.op("dve", (lambda e: e.tensor_tensor(out=rt[1].ap, in0=x2, in1=sbb, op=ALU.mult)),
                     reads=qr.keys() + ["sinT"], writes=rt[1].keys())
                S.op("dve", (lambda e: e.tensor_tensor(out=rt[2].ap, in0=x2, in1=cb, op=ALU.mult)),
                     reads=qr.keys() + ["cosT"], writes=rt[2].keys())
                S.op("dve", (lambda e: e.tensor_tensor(out=rt[3].ap, in0=x1, in1=sbb, op=ALU.mult)),
                     reads=qr.keys() + ["sinT"], writes=rt[3].keys())
                S.op("dve", (lambda e: e.tensor_tensor(out=qt.ap[:, :, 0:8], in0=rt[0].ap, in1=rt[1].ap, op=ALU.subtract)),
                     reads=rt[0].keys() + rt[1].keys(), writes=qt.keys())
                S.op("dve", (lambda e: e.tensor_tensor(out=qt.ap[:, :, 8:16], in0=rt[2].ap, in1=rt[3].ap, op=ALU.add)),
                     reads=rt[2].keys() + rt[3].keys(), writes=qt.keys())
            steps.append(s4)

            def s5():
                btr = nbank("g")
                for h in range(4):
                    S.op("pe", (lambda e, h=h: e.transpose(
                        banks_b[btr][0:72, h * 128:(h + 1) * 128], qt.ap[:, 4 + h, 0:72], identb[:])),
                        reads=qt.keys() + ["identb"], writes=[bk(btr)])
                    S.op("pe", (lambda e, h=h: e.transpose(
                        banks_b[btr][0:64, 512 + h * 128:512 + (h + 1) * 128], qt.ap[:, h, 0:64], identb[:])),
                        reads=qt.keys() + ["identb"], writes=[bk(btr)])
                S.op("act", (lambda e: e.copy(
                    out=KaT.ap[:, :, t * 128:(t + 1) * 128], in_=banks_b[btr][0:72, 0:512].rearrange("p (h n) -> p h n", h=4))),
                    reads=[bk(btr)], writes=[k_ for h in range(4) for k_ in KaT.krange((h,), t * 128, 128)])
                S.op("dve", (lambda e: e.tensor_copy(
                    out=QaT.ap[0:64, :, j * 128:(j + 1) * 128], in_=banks_b[btr][0:64, 512:1024].rearrange("p (h n) -> p h n", h=4))),
                    reads=[bk(btr)], writes=QaT.keys())
                if t % 2 == 1 and blk < 7:
                    S.op("dve", (lambda e: e.tensor_reduce(
                        out=kmf[:], in_=KaT.ap[0:64, :, blk * 256:(blk + 1) * 256], axis=AX.X, op=ALU.add)),
                        reads=[k_ for h in range(4) for k_ in KaT.krange((h,), blk * 256, 256)], writes=["kmf"])
                    S.op("dve", (lambda e: e.tensor_scalar(
                        out=kmT[:, :, blk:blk + 1], in0=kmf[:].unsqueeze(2), scalar1=1.0 / 256, scalar2=None, op0=ALU.mult)),
                        reads=["kmf"], writes=["kmT"])
            steps.append(s5)

            def s6():
                if blk >= 4:
                    bg = nbank("g")
                    for h in range(4):
                        S.op("pe", (lambda e, h=h: e.matmul(
                            banks[bg][:, h * 8:h * 8 + blk], lhsT=QaT.ap[0:64, h, j * 128:(j + 1) * 128],
                            rhs=kmT[:, h, 0:blk], start=True, stop=True, skip_group_check=True)),
                            reads=QaT.keys() + ["kmT"], writes=[bk(bg)])
                    S.op("act", (lambda e: e.copy(
                        out=gsb.ap[:, :, 0:blk], in_=banks[bg][:, 0:32].rearrange("p (h n) -> p h n", h=4)[:, :, 0:blk])),
                        reads=[bk(bg)], writes=gsb.keys())
                    S.op("dve", (lambda e: e.tensor_tensor(
                        out=cmp_s.ap[:, :, 0:blk, 0:blk],
                        in0=gsb.ap[:, :, 0:blk].unsqueeze(2).to_broadcast([128, 4, blk, blk]),
                        in1=gsb.ap[:, :, 0:blk].unsqueeze(3).to_broadcast([128, 4, blk, blk]), op=ALU.is_gt)),
                        reads=gsb.keys(), writes=cmp_s.keys())
                    S.op("dve", (lambda e: e.tensor_reduce(
                        out=rank_s.ap[:, :, 0:blk], in_=cmp_s.ap[:, :, 0:blk, 0:blk], axis=AX.X, op=ALU.add)),
                        reads=cmp_s.keys(), writes=rank_s.keys())
                    S.op("dve", (lambda e: e.tensor_scalar(
                        out=mb.ap[:, :, 64:64 + blk], in0=rank_s.ap[:, :, 0:blk], scalar1=2.5, scalar2=NEGB,
                        op0=ALU.is_gt, op1=ALU.mult)),
                        reads=rank_s.keys(), writes=mb.keys())
                bm = nbank("g")
                for h in range(4):
                    S.op("pe", (lambda e, h=h: e.transpose(
                        banks_b[bm][0:72, h * 128:(h + 1) * 128], mb.ap[:, h, 0:72], identb[:])),
                        reads=mb.keys() + ["identb"], writes=[bk(bm)])
                S.op("act", (lambda e: e.copy(
                    out=QaT.ap[64:72, :, j * 128:(j + 1) * 128], in_=banks_b[bm][64:72, 0:512].rearrange("p (h n) -> p h n", h=4))),
                    reads=[bk(bm)], writes=QaT.keys())
            steps.append(s6)
            return steps

        def attn_steps(hg, qc):
            QaT = QaTs[qc % 2]
            nk = 4 * (qc + 1)
            items = [(h, kt) for h in range(4) for kt in range(nk)]
            st_ = {}

            def stage1(i):
                h, kt = items[i]
                j0 = max(0, kt - 4 * qc)
                bs = nbank("s")
                p_ = pT[i % 4]
                S.op("pe", (lambda e: e.matmul(
                    banks[bs][:, j0 * 128:512], lhsT=KaT.ap[0:72, h, kt * 128:(kt + 1) * 128],
                    rhs=QaT.ap[0:72, h, j0 * 128:512], start=True, stop=True)),
                    reads=KaT.krange((h,), kt * 128, 128) + QaT.keys(), writes=[bk(bs)])
                S.op("act", (lambda e: e.activation(
                    out=p_.ap[:, j0 * 128:512], in_=banks[bs][:, j0 * 128:512], func=AF.Exp, scale=0.125)),
                    reads=[bk(bs)], writes=p_.keys())
                if kt >= 4 * qc:
                    S.op("pool", (lambda e: e.tensor_tensor(
                        out=p_.ap[:, j0 * 128:(j0 + 1) * 128], in0=p_.ap[:, j0 * 128:(j0 + 1) * 128], in1=trib[:], op=ALU.mult)),
                        reads=p_.keys() + ["trib"], writes=p_.keys())

            def stage2(i):
                h, kt = items[i]
                j0 = max(0, kt - 4 * qc)
                p_ = pT[i % 4]
                if kt == 0:
                    st_[h] = nbank("a")
                ba = st_[h]
                accv = banks[ba][:].rearrange("p (j n) -> p j n", j=4)
                for jj in range(j0, 4):
                    S.op("pe", (lambda e, jj=jj: e.matmul(
                        accv[:, jj, 0:65], lhsT=p_.ap[:, jj * 128:(jj + 1) * 128], rhs=Va.ap[:, kt, h, :],
                        start=(kt == 0 and jj == 0), stop=(kt == 4 * qc + jj), skip_group_check=True)),
                        reads=p_.keys() + Va.kidx(kt), writes=[bk(ba)])
                if kt == nk - 1:
                    S.op("dve", (lambda e: e.reciprocal(out=rec_s.ap, in_=accv[:, :, 64])),
                         reads=[bk(ba)], writes=rec_s.keys())
                    S.op("dve", (lambda e: e.tensor_tensor(
                        out=ytok.ap[:, :, h * 64:(h + 1) * 64], in0=accv[:, :, 0:64],
                        in1=rec_s.ap.unsqueeze(2).to_broadcast([128, 4, 64]), op=ALU.mult)),
                        reads=[bk(ba)] + rec_s.keys(), writes=ytok.keys())

            n = len(items)
            LA = int(_os.environ.get('LA', '3'))
            steps = [(lambda: [stage1(i_) for i_ in range(LA)])]
            for i in range(n):
                def sk(i=i):
                    stage2(i)
                    if i + LA < n:
                        stage1(i + LA)
                steps.append(sk)

            def sy():
                by = nbank("g")
                for jj in range(4):
                    for cc in range(2):
                        S.op("pe", (lambda e, jj=jj, cc=cc: e.transpose(
                            banks_b[by][:, (cc * 4 + jj) * 128:(cc * 4 + jj + 1) * 128], ytok.ap[:, jj, cc * 128:(cc + 1) * 128], identb[:])),
                            reads=ytok.keys() + ["identb"], writes=[bk(by)])
                for cc in range(2):
                    copy_op("act" if cc == 0 else "dve", yattnT.ap[:, hg * 2 + cc, qc * 512:(qc + 1) * 512],
                            banks_b[by][:, cc * 512:(cc + 1) * 512], [bk(by)], yattnT.krange((hg * 2 + cc,), qc * 512, 512))
            steps.append(sy)
            return steps

        def attention_phase(l):
            for hg in range(2):
                sqk = wslot()
                wqk = wsl[sqk][:, :].rearrange("p (k n) -> p k n", k=8)
                wload(sqk, wqk[:, :, 0:256], w_in_d[l][:, 1024 + hg * 256:1024 + (hg + 1) * 256].rearrange("(kc p) n -> p kc n", p=128))
                wload(sqk, wqk[:, :, 256:512], w_in_d[l][:, 1536 + hg * 256:1536 + (hg + 1) * 256].rearrange("(kc p) n -> p kc n", p=128))
                sv = wslot()
                wv = wsl[sv][:, 0:2048].rearrange("p (k n) -> p k n", k=8)
                wload(sv, wv, w_in_d[l][:, 2048 + hg * 256:2048 + (hg + 1) * 256].rearrange("(kc p) n -> p kc n", p=128))
                S.op("pool", lambda e: e.memset(Va.ap[:, :, :, 64:65], 1.0), writes=Va.keys())
                for m_ in mb_tok:
                    S.op("pool", (lambda e, m_=m_: e.memset(m_.ap[:, :, 0:64], 0.0)), writes=m_.keys())

                def A(qc):
                    tl = [tile_steps(l, hg, qc, j, wqk, wv, sqk, sv) for j in range(4)]
                    out = []
                    dsk = int(_os.environ.get('DSK', '2'))
                    ns = len(tl[0])
                    for k in range(ns + 3 * dsk):
                        for i in range(4):
                            ix = k - i * dsk
                            if 0 <= ix < ns:
                                out.append(tl[i][ix])
                    return out

                merge_steps([A(0)])
                for qc in range(NTC):
                    lists = [attn_steps(hg, qc)]
                    if qc + 1 < NTC:
                        lists.append(A(qc + 1))
                    merge_steps(lists)

        PW = S_ + 16
        up_s = AV(A1, 128, [PW], F32)
        sA_s = AV(A1 + 8256, 128, [PW], F32)
        sB_s = AV(A1 + 16512, 128, [PW], F32)

        def fm_proj(wv_of_kc, b, tc):
            for kc in range(8):
                lhsT, rk = wv_of_kc(kc)
                S.op("pe", (lambda e, lhsT=lhsT, kc=kc, b=b, tc=tc: e.matmul(
                    banks[b][:], lhsT=lhsT, rhs=hT[:, kc, tcs(tc)], start=(kc == 0), stop=(kc == 7))),
                    reads=[kh(kc, tc)] + rk, writes=[bk(b)])

        def pool_phase(l):
            S.op("dve", lambda e: e.memset(poolW[:], 0.0), writes=[("poolW", g) for g in range(4)])
            for g in range(4):
                r0 = (g % 2) * 64
                S.op("pool", (lambda e, g=g, r0=r0, l=l: e.dma_start(out=poolW[r0:r0 + 64, g // 2, r0:r0 + 64], in_=pool_w_d[l, g])),
                     writes=[("poolW", g)], dma=True)
            s = wslot()
            wv = wsl[s][:, 0:2048].rearrange("p (k n) -> p k n", k=8)
            wload(s, wv, w_in_d[l][:, 0:256].rearrange("(kc p) n -> p kc n", p=128))
            for v_ in (up_s, sA_s, sB_s):
                S.op("dve", (lambda e, v_=v_: e.memset(v_.ap[:, 0:16], 0.0)), writes=v_.keys(0, 16))
            for cc in range(2):
                for tc in range(NTC):
                    b = nbank("G")
                    fm_proj(lambda kc, cc=cc: (wv[:, kc, cc * 128:(cc + 1) * 128], [("ws", s)]), b, tc)
                    copy_op(evac_eng(), up_s.ap[:, 16 + tc * 512:16 + (tc + 1) * 512], banks[b][:], [bk(b)],
                            up_s.keys(16 + tc * 512, 512))
                n_lv = 2 if cc == 0 else 4
                cur = up_s
                nxt = [sA_s, sB_s]
                srcs = {}
                for lv in range(1, n_lv + 1):
                    sh = 1 << (lv - 1)
                    dst = nxt[(lv - 1) % 2]
                    p0 = 64 if lv == n_lv else 0
                    S.op("dve", (lambda e, dst=dst, cur=cur, sh=sh, p0=p0: e.tensor_tensor(
                        out=dst.ap[p0:128, 16:PW], in0=cur.ap[p0:128, 16:PW], in1=cur.ap[p0:128, 16 - sh:PW - sh], op=ALU.add)),
                        reads=cur.keys(), writes=dst.keys(16, S_))
                    srcs[lv] = dst
                    cur = dst
                lo_src = srcs[n_lv - 1]
                hi_src = srcs[n_lv]
                pooledT = yconvT
                for (p0, p1, src) in ((0, 64, lo_src), (64, 128, hi_src)):
                    S.op("dve", (lambda e, p0=p0, p1=p1, src=src, cc=cc: e.tensor_tensor(
                        out=src.ap[p0:p1, 16:32], in0=src.ap[p0:p1, 16:32],
                        in1=consts[p0:p1, O_CORR + cc * 16:O_CORR + (cc + 1) * 16], op=ALU.mult)),
                        reads=src.keys() + ["consts"], writes=src.keys(16, 16))
                    S.op("dve", (lambda e, p0=p0, p1=p1, src=src, cc=cc: e.tensor_tensor(
                        out=pooledT.ap[p0:p1, cc, 0:16], in0=src.ap[p0:p1, 16:32], in1=up_s.ap[p0:p1, 16:32], op=ALU.subtract)),
                        reads=src.keys() + up_s.keys(), writes=pooledT.krange((cc,), 0, 16))
                    S.op("dve", (lambda e, p0=p0, p1=p1, src=src, cc=cc: e.scalar_tensor_tensor(
                        out=pooledT.ap[p0:p1, cc, 16:S_], in0=src.ap[p0:p1, 32:PW], scalar=consts[p0:p1, O_INVW + cc:O_INVW + cc + 1],
                        in1=up_s.ap[p0:p1, 32:PW], op0=ALU.mult, op1=ALU.subtract)),
                        reads=src.keys() + up_s.keys() + ["consts"], writes=pooledT.kidx(cc))
                for tc in range(NTC):
                    b = nbank("G")
                    S.op("pe", (lambda e, b=b, cc=cc, tc=tc: e.matmul(banks[b][:], lhsT=poolW[:, cc, :], rhs=pooledT.ap[:, cc, tcs(tc)],
                                                                      start=True, stop=True)),
                         reads=[("poolW", 2 * cc), ("poolW", 2 * cc + 1)] + pooledT.krange((cc,), tc * 512, 512), writes=[bk(b)])
                    S.op("act", (lambda e, b=b, cc=cc, tc=tc, l=l: e.activation(
                        out=ypoolT.ap[:, cc, tcs(tc)], in_=banks[b][:], func=AF.Identity,
                        scale=small[:, O_PSC + l * 2 + cc:O_PSC + l * 2 + cc + 1])),
                        reads=[bk(b), "small"], writes=ypoolT.krange((cc,), tc * 512, 512))

        def conv_phase(l):
            cu_s, cc_s, ac_s = up_s, sA_s, sB_s
            for cc in range(2):
                s = wslot()
                wv = wsl[s][:, 0:3072].rearrange("p (k i n) -> p k i n", k=8, i=3)
                for i in range(3):
                    c0 = 256 + i * 256 + cc * 128
                    wload(s, wv[:, :, i, :], w_in_d[l][:, c0:c0 + 128].rearrange("(kc p) n -> p kc n", p=128))
                S.op("dve", lambda e: e.memset(cu_s.ap[:, 0:16], 0.0), writes=cu_s.keys(0, 16))
                for tc in range(NTC):
                    b = nbank("G")
                    fm_proj(lambda kc: (wv[:, kc, 0, :], [("ws", s)]), b, tc)
                    copy_op("act", cu_s.ap[:, 16 + tc * 512:16 + (tc + 1) * 512], banks[b][:], [bk(b)], cu_s.keys(16 + tc * 512, 512))
                    b2 = nbank("G")
                    fm_proj(lambda kc: (wv[:, kc, 2, :], [("ws", s)]), b2, tc)
                    S.op("dve", (lambda e, b2=b2, tc=tc: e.tensor_tensor(
                        out=cu_s.ap[:, 16 + tc * 512:16 + (tc + 1) * 512], in0=banks[b2][:],
                        in1=cu_s.ap[:, 16 + tc * 512:16 + (tc + 1) * 512], op=ALU.mult)),
                        reads=[bk(b2)] + cu_s.keys(16 + tc * 512, 512), writes=cu_s.keys(16 + tc * 512, 512))
                cw = lambda k, cc=cc, l=l: small[:, O_CW + l * 6 + k * 2 + cc:O_CW + l * 6 + k * 2 + cc + 1]
                S.op("dve", (lambda e, cw=cw: e.tensor_scalar(out=ac_s.ap[:, 16:PW], in0=cu_s.ap[:, 16:PW], scalar1=cw(2), scalar2=None, op0=ALU.mult)),
                     reads=cu_s.keys() + ["small"], writes=ac_s.keys(16, S_))
                S.op("dve", (lambda e, cw=cw: e.scalar_tensor_tensor(out=ac_s.ap[:, 16:PW], in0=cu_s.ap[:, 15:PW - 1], scalar=cw(1),
                                                                     in1=ac_s.ap[:, 16:PW], op0=ALU.mult, op1=ALU.add)),
                     reads=cu_s.keys() + ac_s.keys() + ["small"], writes=ac_s.keys(16, S_))
                S.op("dve", (lambda e, cw=cw: e.scalar_tensor_tensor(out=ac_s.ap[:, 16:PW], in0=cu_s.ap[:, 14:PW - 2], scalar=cw(0),
                                                                     in1=ac_s.ap[:, 16:PW], op0=ALU.mult, op1=ALU.add)),
                     reads=cu_s.keys() + ac_s.keys() + ["small"], writes=ac_s.keys(16, S_))
                for tc in range(NTC):
                    b = nbank("G")
                    fm_proj(lambda kc: (wv[:, kc, 1, :], [("ws", s)]), b, tc)
                    S.op("dve", (lambda e, b=b, cc=cc, tc=tc: e.tensor_tensor(
                        out=yconvT.ap[:, cc, tcs(tc)], in0=banks[b][:], in1=ac_s.ap[:, 16 + tc * 512:16 + (tc + 1) * 512], op=ALU.mult)),
                        reads=[bk(b)] + ac_s.keys(16 + tc * 512, 512), writes=yconvT.krange((cc,), tc * 512, 512))

        merged = AV(A1, 128, [8, 1024], BF16)
        mo = A1 + 16384
        s_t = [AV(mo, 128, [512], F32), AV(mo + 2048, 128, [512], F32)]
        m_t = [AV(mo + 4096, 128, [512], F32), AV(mo + 6144, 128, [512], F32)]
        t_t = [AV(mo + 8192, 128, [512], F32), AV(mo + 10240, 128, [512], F32)]

        def merge_phase(l):
            lp = cur["lp"]
            avec = avecs[lp]
            ysrc = [(ypoolT, 2, 0), (yconvT, 2, 2), (yattnT, 4, 4)]
            pds = [p_pool_d, p_conv_d, p_attn_d]
            cnt = 0
            for hh in range(2):
                for c in range(8):
                    s = wslot()
                    wg = wsl[s][:, 0:3072].rearrange("p (k i n) -> p k i n", k=8, i=3)
                    wp = wsl[s][:, 3072:4096].rearrange("p (k n) -> p k n", k=8)
                    for i in range(3):
                        c0 = 2560 + i * 1024 + c * 128
                        wload(s, wg[:, :, i, :], w_in_d[l][:, c0:c0 + 128].rearrange("(kc p) n -> p kc n", p=128))
                    wload(s, wp[:, 0:2, :], p_pool_d[l][:, c * 128:(c + 1) * 128].rearrange("(kc p) n -> p kc n", p=128))
                    wload(s, wp[:, 2:4, :], p_conv_d[l][:, c * 128:(c + 1) * 128].rearrange("(kc p) n -> p kc n", p=128))
                    wload(s, wp[:, 4:8, :], p_attn_d[l][:, c * 128:(c + 1) * 128].rearrange("(kc p) n -> p kc n", p=128))
                    for th in range(2):
                        tc = hh * 2 + th
                        mt = m_t[cnt % 2]
                        for i in range(3):
                            bgt = nbank("G")
                            fm_proj(lambda kc, i=i: (wg[:, kc, i, :], [("ws", s)]), bgt, tc)
                            st_ = s_t[(cnt * 3 + i) % 2]
                            S.op("act", (lambda e, bgt=bgt, st_=st_: e.activation(out=st_.ap, in_=banks[bgt][:], func=AF.Tanh, scale=0.5)),
                                 reads=[bk(bgt)], writes=st_.keys())
                            ysv, nkk, koff = ysrc[i]
                            bp = nbank("G")
                            for kk in range(nkk):
                                S.op("pe", (lambda e, bp=bp, kk=kk, koff=koff, ysv=ysv, tc=tc, nkk=nkk, wp=wp: e.matmul(
                                    banks[bp][:], lhsT=wp[:, koff + kk, :], rhs=ysv.ap[:, kk, tcs(tc)], start=(kk == 0), stop=(kk == nkk - 1))),
                                    reads=[("ws", s)] + ysv.krange((kk,), tc * 512, 512), writes=[bk(bp)])
                            if i == 0:
                                S.op("dve", (lambda e, bp=bp, st_=st_, mt=mt: e.scalar_tensor_tensor(
                                    out=mt.ap, in0=st_.ap, scalar=1.0, in1=banks[bp][:], op0=ALU.add, op1=ALU.mult)),
                                    reads=[bk(bp)] + st_.keys(), writes=mt.keys())
                            else:
                                tt_ = t_t[(cnt * 3 + i) % 2]
                                S.op("dve", (lambda e, bp=bp, st_=st_, tt_=tt_: e.scalar_tensor_tensor(
                                    out=tt_.ap, in0=st_.ap, scalar=1.0, in1=banks[bp][:], op0=ALU.add, op1=ALU.mult)),
                                    reads=[bk(bp)] + st_.keys(), writes=tt_.keys())
                                if i == 1:
                                    S.op("dve", (lambda e, tt_=tt_, mt=mt: e.tensor_tensor(out=mt.ap, in0=mt.ap, in1=tt_.ap, op=ALU.add)),
                                         reads=mt.keys() + tt_.keys(), writes=mt.keys())
                                else:
                                    S.op("dve", (lambda e, tt_=tt_, mt=mt, c=c, th=th: e.tensor_tensor(
                                        out=merged.ap[:, c, th * 512:(th + 1) * 512], in0=mt.ap, in1=tt_.ap, op=ALU.add)),
                                        reads=mt.keys() + tt_.keys(), writes=merged.krange((c,), th * 512, 512))
                        cnt += 1
                for ob in range(2):
                    s = wslot()
                    wo = wsl[s][:, :].rearrange("p (k n) -> p k n", k=8)
                    wload(s, wo, w_out_d[l][:, ob * 512:(ob + 1) * 512].rearrange("(kc p) n -> p kc n", p=128))
                    for oo in range(4):
                        o = ob * 4 + oo
                        for th in range(2):
                            tc = hh * 2 + th
                            b = nbank("G")
                            for c in range(8):
                                S.op("pe", (lambda e, b=b, c=c, oo=oo, th=th, wo=wo: e.matmul(
                                    banks[b][:], lhsT=wo[:, c, oo * 128:(oo + 1) * 128], rhs=merged.ap[:, c, th * 512:(th + 1) * 512],
                                    start=(c == 0), stop=(c == 7))),
                                    reads=[("ws", s)] + merged.krange((c,), th * 512, 512), writes=[bk(b)])
                            S.op("dve", (lambda e, b=b, o=o, tc=tc: e.scalar_tensor_tensor(
                                out=xT[:, o, tcs(tc)], in0=banks[b][:], scalar=avec[:, 16 + o:17 + o], in1=xT[:, o, tcs(tc)],
                                op0=ALU.mult, op1=ALU.add)),
                                reads=[bk(b), kx(o, tc), ("avec", lp)], writes=[kx(o, tc)])

        actT = [AV(0, 128, [4, S_], BF16), AV(16384, 128, [4, S_], BF16)]
        sl_t = [AV(32768, 128, [512], F32), AV(32768 + 2048, 128, [512], F32)]

        def ffn_steps(l, pool, tail_norm=None, grp=None):
            lp = l % 2
            modT = modTs[lp]
            nfb = 6
            st_ = {"cnt": 0}
            steps = []
            for fb in range(nfb):
                nch = 4 if fb < 5 else 2
                f0 = fb * 512
                ncol = nch * 128
                W = {}

                def sload(fb=fb, nch=nch, f0=f0, ncol=ncol, W=W):
                    sg = wslot(grp)
                    wg = wsl[sg][:, 0:8 * ncol].rearrange("p (k n) -> p k n", k=8)
                    wload(sg, wg, w_gate_d[l][:, f0:f0 + ncol].rearrange("(kc p) n -> p kc n", p=128))
                    su = wslot(grp)
                    wu = wsl[su][:, 0:8 * ncol].rearrange("p (k n) -> p k n", k=8)
                    wload(su, wu, w_up_d[l][:, f0:f0 + ncol].rearrange("(kc p) n -> p kc n", p=128))
                    sd = wslot(grp)
                    wd = wsl[sd][:, 0:nch * 1024].rearrange("p (j n) -> p j n", j=nch)
                    wload(sd, wd, w_down_d[l][f0:f0 + ncol, :].rearrange("(j p) n -> p j n", p=128))
                    W.update(sg=sg, wg=wg, su=su, wu=wu, sd=sd, wd=wd)
                steps.append(sload)
                at = actT[fb % 2]
                for j in range(nch):
                    for tc in range(NTC):
                        def sgu(j=j, tc=tc, W=W, at=at):
                            wg, wu, sg, su = W["wg"], W["wu"], W["sg"], W["su"]
                            bg_ = nbank(pool)
                            fm_proj(lambda kc: (wg[:, kc, j * 128:(j + 1) * 128], [("ws", sg)]), bg_, tc)
                            bu_ = nbank(pool)
                            fm_proj(lambda kc: (wu[:, kc, j * 128:(j + 1) * 128], [("ws", su)]), bu_, tc)
                            sl = sl_t[st_["cnt"] % 2]
                            st_["cnt"] += 1
                            S.op("act", (lambda e: e.activation(out=sl.ap, in_=banks[bg_][:], func=AF.Silu)),
                                 reads=[bk(bg_)], writes=sl.keys())
                            S.op("dve", (lambda e: e.tensor_tensor(
                                out=at.ap[:, j, tcs(tc)], in0=banks[bu_][:], in1=sl.ap, op=ALU.mult)),
                                reads=[bk(bu_)] + sl.keys(), writes=at.krange((j,), tc * 512, 512))
                        steps.append(sgu)
                dn_order = [(o, tc) for o in range(8) for tc in range(NTC)] if fb < nfb - 1 else \
                           [(o, tc) for tc in range(NTC) for o in range(8)]
                for (o, tc) in dn_order:
                    if True:
                        def sdn(o=o, tc=tc, W=W, at=at, nch=nch):
                            wd, sd = W["wd"], W["sd"]
                            b = nbank(pool)
                            for j in range(nch):
                                S.op("pe", (lambda e, j=j: e.matmul(
                                    banks[b][:], lhsT=wd[:, j, o * 128:(o + 1) * 128], rhs=at.ap[:, j, tcs(tc)],
                                    start=(j == 0), stop=(j == nch - 1))),
                                    reads=[("ws", sd)] + at.krange((j,), tc * 512, 512), writes=[bk(b)])
                            S.op("dve", (lambda e: e.scalar_tensor_tensor(
                                out=xT[:, o, tcs(tc)], in0=banks[b][:], scalar=modT[:, 40 + o:41 + o], in1=xT[:, o, tcs(tc)],
                                op0=ALU.mult, op1=ALU.add)),
                                reads=[bk(b), kx(o, tc), ("modT", lp)], writes=[kx(o, tc)])
                        steps.append(sdn)
                        if fb == nfb - 1 and tail_norm is not None and o == 7 and tc >= 1:
                            steps.append(tail_norm[tc - 1])
            if tail_norm is not None:
                steps.append(tail_norm[3])
            return steps

        KSTOP = int(_os.environ.get("KSTOP", "99"))
        if KSTOP >= 1:
            merge_steps([mod_steps(0)])
        if KSTOP >= 2:
            merge_steps([norm_tc_steps(0, 0, 0)])
        for l in range(L):
            if KSTOP < 3:
                break
            cur["lp"] = l % 2
            attention_phase(l)
            if KSTOP >= 4:
                pool_phase(l)
            if KSTOP >= 5:
                conv_phase(l)
            if KSTOP >= 6:
                merge_phase(l)
            if KSTOP >= 8:
                n2 = norm_tc_steps(8, 24, l % 2, base=40960)
                n1 = None
                if l + 1 < L:
                    n1 = norm_tc_steps(0, 0, (l + 1) % 2, base=40960)
                if n1 is not None and _os.environ.get("TAILN", "1") == "1":
                    fs = ffn_steps(l, "F", tail_norm=n1, grp=("ffn" if l + 1 < L else None))
                    n1 = None
                else:
                    fs = ffn_steps(l, "F", grp=("ffn" if l + 1 < L else None))
                if _os.environ.get("HEADN", "1") == "1":
                    head = [n2[0], n2[1], fs[0], fs[1], n2[2], fs[2], n2[3]]
                    rest = fs[3:]
                else:
                    head = n2
                    rest = fs
                merge_steps([head])
                if l + 1 < L:
                    if _os.environ.get("MODM", "1") == "1":
                        ncut = (len(rest) * 3) // 5
                        merge_steps([rest[:ncut], mod_steps(l + 1, grp="mod")])
                        merge_steps([rest[ncut:]])
                    else:
                        merge_steps([rest])
                        merge_steps([mod_steps(l + 1)])
                else:
                    merge_steps([rest])
                if n1 is not None:
                    merge_steps([n1])
            elif KSTOP >= 7:
                merge_steps([norm_tc_steps(8, 24, l % 2)])

        xo = [AV(0, 128, [D], F32), AV(4096, 128, [D], F32)]
        for t in range(NT):
            xv = xo[t % 2]
            for half in range(2):
                b = nbank("G")
                for cq in range(4):
                    c = half * 4 + cq
                    S.op("pe", (lambda e, b=b, cq=cq, c=c, t=t: e.transpose(
                        banks[b][:, cq * 128:(cq + 1) * 128], xT[:, c, t * 128:(t + 1) * 128], consts[:, O_IDF:O_IDF + 128])),
                        reads=[kx(c, t // 4), "consts"], writes=[bk(b)])
                copy_op(evac_eng(), xv.ap[:, half * 512:(half + 1) * 512], banks[b][:], [bk(b)], xv.keys(half * 512, 512))
            S.op("sp", (lambda e, xv=xv, t=t: e.dma_start(out=out_d[t * 128:(t + 1) * 128, :], in_=xv.ap)),
                 reads=xv.keys(), dma=True)
        S.emit(st)
    return nc


def _consts():
    c = np.zeros((128, NCF), np.float32)
    c[:, O_IDF:O_IDF + 128] = np.eye(128, dtype=np.float32)
    kk = np.arange(128)[:, None]
    qq = np.arange(128)[None, :]
    c[:, O_TRI:O_TRI + 128] = (kk <= qq).astype(np.float32)
    c[:, O_FREQ:O_FREQ + 8] = (500000.0 ** (-np.arange(0, 16, 2, dtype=np.float32) / 16.0)).astype(np.float32)[None, :]
    for p in range(128):
        for cc in range(2):
            w = POOL_W[cc * 2 + (1 if p >= 64 else 0)]
            c[p, O_INVW + cc] = 1.0 / w
            for t in range(16):
                c[p, O_CORR + cc * 16 + t] = 1.0 / min(t + 1, w)
    return c


def _small(b, c, norm1_w, norm2_w, b_ada, pool_scale, conv_w, q_norm_w, k_norm_w, l0, L):
    s = np.zeros((128, NS), np.float32)
    for l in range(L):
        gl = l0 + l
        s[:, O_N1W + l * 8:O_N1W + (l + 1) * 8] = norm1_w[gl].reshape(8, 128).T
        s[:, O_N2W + l * 8:O_N2W + (l + 1) * 8] = norm2_w[gl].reshape(8, 128).T
        s[:, O_BADA + l * 48:O_BADA + (l + 1) * 48] = b_ada[gl].reshape(48, 128).T
        s[:, O_PSC + l * 2:O_PSC + (l + 1) * 2] = pool_scale[gl].reshape(2, 128).T
        s[:, O_CW + l * 6:O_CW + (l + 1) * 6] = conv_w[gl].reshape(3, 2, 128).transpose(2, 0, 1).reshape(128, 6)
        s[:, O_QKW + l * 128:O_QKW + l * 128 + 64] = q_norm_w[gl][None, :]
        s[:, O_QKW + l * 128 + 64:O_QKW + (l + 1) * 128] = k_norm_w[gl][None, :]
    s[:, O_CT:O_CT + 8] = c[b].reshape(8, 128).T
    return s


_NC_CACHE = {}


def _get_nc(L):
    if L not in _NC_CACHE:
        _NC_CACHE[L] = build(L)
    return _NC_CACHE[L]


def _run(x, c, positions, P, l0, L):
    nc = build(L)
    consts = _consts()
    in_maps = []
    wsl_ = slice(l0, l0 + L)
    for b in range(8):
        m = {
            "x": np.ascontiguousarray(x[b]),
            "small": _small(b, c, P["norm1_w"], P["norm2_w"], P["b_ada"], P["pool_scale"], P["conv_w"],
                            P["q_norm_w"], P["k_norm_w"], l0, L),
            "consts": consts,
            "pos": np.ascontiguousarray(positions[b].reshape(NT, 128).T.astype(np.int32)),
        }
        for k in ("w_ada", "w_in", "pool_w", "p_pool", "p_conv", "p_attn", "w_out", "w_gate", "w_up", "w_down"):
            m[k] = np.ascontiguousarray(P[k][wsl_])
        in_maps.append(m)
    res = run_bass_kernel_spmd(nc, in_maps, core_ids=list(range(8)))
    return np.stack([np.asarray(r["out"]) for r in res.results], axis=0)


def kernel(x, c, positions, norm1_w, norm2_w, w_ada, b_ada, w_in, pool_w, pool_scale, conv_w,
           q_norm_w, k_norm_w, p_pool, p_conv, p_attn, w_out, w_gate, w_up, w_down):
    P = dict(norm1_w=np.asarray(norm1_w), norm2_w=np.asarray(norm2_w), w_ada=np.asarray(w_ada), b_ada=np.asarray(b_ada),
             w_in=np.asarray(w_in), pool_w=np.asarray(pool_w), pool_scale=np.asarray(pool_scale), conv_w=np.asarray(conv_w),
             q_norm_w=np.asarray(q_norm_w), k_norm_w=np.asarray(k_norm_w), p_pool=np.asarray(p_pool), p_conv=np.asarray(p_conv),
             p_attn=np.asarray(p_attn), w_out=np.asarray(w_out), w_gate=np.asarray(w_gate), w_up=np.asarray(w_up),
             w_down=np.asarray(w_down))
    x = np.asarray(x, dtype=np.float32)
    c = np.asarray(c)
    positions = np.asarray(positions)
    if FUSED:
        out = _run(x, c, positions, P, 0, DEPTH)
    else:
        out = x
        for l in range(DEPTH):
            out = _run(out, c, positions, P, l, 1)
    return out.astype(np.float32)
```

```python
import math
import os as _os
from contextlib import ExitStack

import numpy as np
import concourse.bass as bass
import concourse.mybir as mybir
from concourse.bass_utils import run_bass_kernel_spmd

F32 = mybir.dt.float32
BF16 = mybir.dt.bfloat16
I32 = mybir.dt.int32
ALU = mybir.AluOpType
AF = mybir.ActivationFunctionType
AX = mybir.AxisListType

FUSED = True
DEPTH = 4
D = 1024
S_ = 2048
NT = 16
NTC = 4
DFF = 2816
EPS = 1e-6
NEGB = -30000.0
O_N1W, O_N2W, O_BADA, O_PSC, O_CW, O_QKW, O_CT, NS = 0, 32, 64, 256, 264, 288, 800, 808
O_IDF, O_TRI, O_FREQ, O_INVW, O_CORR, NCF = 0, 128, 256, 264, 266, 298
POOL_W = (2, 4, 8, 16)

ENGS = ["pe", "act", "dve", "pool", "sp"]
SELFSYNC = _os.environ.get("SELFSYNC", "drain")


class Op:
    __slots__ = ("eng", "fn", "deps", "is_dma", "sig", "has_dep", "name", "pos", "selfdep")

    def __init__(self, eng, fn, is_dma, name=None):
        self.eng = eng
        self.fn = fn
        self.deps = []
        self.selfdep = False
        self.is_dma = is_dma
        self.sig = None
        self.has_dep = False
        self.name = name


class Sched:
    def __init__(self, nc, n_dma_sems=8, maxv=12000):
        self.nc = nc
        self.ops = {e: [] for e in ENGS}
        self.last_writer = {}
        self.readers = {}
        self.n_dma_sems = n_dma_sems
        self.maxv = maxv
        self.cow = {}
        self.old_r = {}
        self.old_w = {}

    def new_gen(self, k):
        self.old_r[k] = self.readers.get(k, [])
        self.old_w[k] = self.cow.get(k, [])
        self.readers[k] = []
        self.cow[k] = []

    def op(self, eng, fn, reads=(), writes=(), dma=False, name=None, joins=()):
        o = Op(eng, fn, dma, name)
        deps = {}
        lw = self.last_writer
        rd = self.readers
        if eng != "pe":
            extra = [k for k in reads if isinstance(k, tuple) and k[0] == "bank"]
            if extra:
                writes = list(writes) + extra
        for k in joins:
            for r in self.old_r.get(k, ()):
                deps[id(r)] = (r, "war")
            for w in self.old_w.get(k, ()):
                if id(w) not in deps:
                    deps[id(w)] = (w, "waw")
            self.cow.setdefault(k, []).append(o)
        for k in reads:
            w = lw.get(k)
            if w is not None:
                deps[id(w)] = (w, "raw")
            for w in self.cow.get(k, ()):
                deps[id(w)] = (w, "raw")
        for k in writes:
            w = lw.get(k)
            if w is not None and id(w) not in deps:
                deps[id(w)] = (w, "waw")
            for r in rd.get(k, ()):
                if id(r) not in deps:
                    deps[id(r)] = (r, "war")
        for k in reads:
            rd.setdefault(k, []).append(o)
        for k in writes:
            lw[k] = o
            rd[k] = []
        best = {}
        npos = len(self.ops[eng])
        for d, kind in deps.values():
            if d.eng == eng and not d.is_dma and not dma:
                if eng == "pe":
                    continue
                if SELFSYNC == "drain":
                    if npos - d.pos <= 2:
                        o.selfdep = True
                    continue
            if d.is_dma:
                o.deps.append(d)
                d.has_dep = True
            else:
                b = best.get(d.eng)
                if b is None or d.pos > b.pos:
                    best[d.eng] = d
        for d in best.values():
            o.deps.append(d)
            d.has_dep = True
        o.pos = len(self.ops[eng])
        self.ops[eng].append(o)
        return o

    def emit(self, stack):
        nc = self.nc
        for e in ENGS:
            cnt = 0
            nsem = 0
            cur = None
            for o in self.ops[e]:
                if o.is_dma or not o.has_dep:
                    continue
                if cur is None or cnt >= self.maxv:
                    cur = stack.enter_context(nc.semaphore(f"s_{e}_{nsem}"))
                    nsem += 1
                    cnt = 0
                cnt += 1
                o.sig = (cur, cnt, 1)
        for e in ENGS:
            dl = [o for o in self.ops[e] if o.is_dma]
            if not dl:
                continue
            sems = [stack.enter_context(nc.semaphore(f"d_{e}_{i}")) for i in range(min(self.n_dma_sems, len(dl)))]
            cnts = [0] * len(sems)
            for i, o in enumerate(dl):
                j = i % len(sems)
                cnts[j] += 1
                o.sig = (sems[j], 16 * cnts[j], 16)

        def run_engine(ename, eng):
            waited = {}
            for o in self.ops[ename]:
                need = {}
                for d in o.deps:
                    s, v = d.sig[0], d.sig[1]
                    if waited.get(s, 0) >= v:
                        continue
                    if need.get(s, 0) < v:
                        need[s] = v
                if o.is_dma:
                    s, v = o.sig[0], o.sig[1]
                    if v > 16 and waited.get(s, 0) < v - 16:
                        need[s] = max(need.get(s, 0), v - 16)
                for s, v in need.items():
                    eng.wait_ge(s, v)
                    waited[s] = v
                if o.selfdep:
                    eng.drain()
                ins = o.fn(eng)
                if o.sig is not None:
                    ins.then_inc(o.sig[0], o.sig[2])
            last = {}
            for o in self.ops[ename]:
                if o.is_dma:
                    last[o.sig[0]] = o.sig[1]
            for s, v in last.items():
                if waited.get(s, 0) < v:
                    eng.wait_ge(s, v)

        with nc.Block() as block:
            @block.tensor
            def _(e):
                run_engine("pe", e)

            @block.scalar
            def _(e):
                run_engine("act", e)

            @block.vector
            def _(e):
                run_engine("dve", e)

            @block.gpsimd
            def _(e):
                run_engine("pool", e)

            @block.sync
            def _(e):
                run_engine("sp", e)


ARENA = 69632
BLK = 1024


def build(L):
    nc = bass.Bass("TRN2", target_bir_lowering=False)

    def din(name, shape, dt=F32):
        return nc.dram_tensor(name, shape, dt, kind="ExternalInput").ap()

    x_d = din("x", [S_, D])
    small_d = din("small", [128, NS])
    consts_d = din("consts", [128, NCF])
    pos_d = din("pos", [128, NT], I32)
    w_ada_d = din("w_ada", [L, D, 6 * D])
    w_in_d = din("w_in", [L, D, 5632])
    pool_w_d = din("pool_w", [L, 4, 64, 64])
    p_pool_d = din("p_pool", [L, 256, D])
    p_conv_d = din("p_conv", [L, 256, D])
    p_attn_d = din("p_attn", [L, 512, D])
    w_out_d = din("w_out", [L, D, D])
    w_gate_d = din("w_gate", [L, D, DFF])
    w_up_d = din("w_up", [L, D, DFF])
    w_down_d = din("w_down", [L, DFF, D])
    lidx_d = None
    out_d = nc.dram_tensor("out", [S_, D], F32, kind="ExternalOutput").ap()

    with ExitStack() as st:
        S = Sched(nc)

        def sb(name, shape, dt):
            return st.enter_context(nc.sbuf_tensor("sb_" + name, shape, dt))

        xT = sb("xT", [128, 8, S_], F32)
        hT = sb("hT", [128, 8, S_], BF16)
        wsl = [sb(f"ws{i}", [128, 4096], BF16) for i in range(4)]
        small = sb("small", [128, NS], F32)
        consts = sb("consts", [128, NCF], F32)
        identb = sb("identb", [128, 128], BF16)
        trib = sb("trib", [128, 128], BF16)
        onesb = sb("onesb", [128, 128], BF16)
        modTs = [sb("modT0", [128, 48], F32), sb("modT1", [128, 48], F32)]
        cur = {"lp": 0}
        avecs = [sb("avec0", [128, 32], F32), sb("avec1", [128, 32], F32)]
        posi = sb("posi", [128, NT], I32)
        posf = sb("posf", [128, NT], F32)
        ang = sb("ang", [128, NT, 8], F32)
        cosT = sb("cosT", [128, NT, 8], F32)
        sinT = sb("sinT", [128, NT, 8], F32)
        cact = sb("cact", [128, 8], F32)
        cactb = sb("cactb", [128, 8], BF16)
        poolW = sb("poolW", [128, 2, 128], BF16)
        kmT = sb("kmT", [64, 4, 8], BF16)
        kmf = sb("kmf", [64, 4], F32)
        arena = sb("arena", [128, ARENA // 4], F32)
        arena_b = arena.bitcast(BF16)
        banks = [st.enter_context(nc.psum_tensor(f"bank{i}", [128, 512], F32)) for i in range(8)]
        banks_b = [b.bitcast(BF16) for b in banks]

        class AV:
            def __init__(self, base, P, fshape, dt):
                self.base = base
                self.P = P
                self.fshape = list(fshape)
                self.dt = dt
                self.esz = 2 if dt == BF16 else 4
                n = int(np.prod(fshape))
                self.n = n
                assert base % 32 == 0 and base + n * self.esz <= ARENA, (base, n, self.esz)
                src = arena_b if dt == BF16 else arena
                e0 = base // self.esz
                flat = src[0:P, e0:e0 + n]
                if len(fshape) == 1:
                    self.ap = flat
                elif len(fshape) == 2:
                    self.ap = flat.rearrange("p (a b) -> p a b", a=fshape[0])
                elif len(fshape) == 3:
                    self.ap = flat.rearrange("p (a b c) -> p a b c", a=fshape[0], b=fshape[1])
                else:
                    raise ValueError

            def keys(self, lo=0, n=None):
                if n is None:
                    n = self.n - lo
                b0 = (self.base + lo * self.esz) // BLK
                b1 = (self.base + (lo + n) * self.esz - 1) // BLK
                return [("A", b) for b in range(b0, b1 + 1)]

            def kidx(self, *idx):
                strides = []
                s = 1
                for d in reversed(self.fshape):
                    strides.append(s)
                    s *= d
                strides = strides[::-1]
                lo = sum(i * st_ for i, st_ in zip(idx, strides))
                n = strides[len(idx) - 1] if idx else self.n
                return self.keys(lo, n)

            def krange(self, idx, lo, n):
                strides = []
                s = 1
                for d in reversed(self.fshape):
                    strides.append(s)
                    s *= d
                strides = strides[::-1]
                base = sum(i * st_ for i, st_ in zip(idx, strides))
                return self.keys(base + lo, n)

        bank_rr = {"F": [0, [0, 1, 2, 3, 4, 5]], "n": [0, [6]], "g": [0, [2, 3]], "s": [0, [4, 5]], "a": [0, [6, 7]], "G": [0, [0, 1, 2, 3, 4, 5, 6, 7]]}

        def nbank(pool):
            st_ = bank_rr[pool]
            b = st_[1][st_[0] % len(st_[1])]
            st_[0] += 1
            return b

        def bk(b):
            return ("bank", b)

        ws_rr = [0]

        ws_grp = {"ffn": [0, [0, 1, 2]], "mod": [0, [3]]}

        def wslot(grp=None):
            if grp is None:
                i = ws_rr[0] % 4
                ws_rr[0] += 1
            else:
                g_ = ws_grp[grp]
                i = g_[1][g_[0] % len(g_[1])]
                g_[0] += 1
            S.new_gen(("ws", i))
            return i

        def wload(slot, dst_ap, src_ap):
            S.op("pool", lambda e: e.dma_start(out=dst_ap, in_=src_ap), joins=[("ws", slot)], dma=True)

        def kx(c, tc):
            return ("xT", c, tc)

        def kh(c, tc):
            return ("hT", c, tc)

        def tcs(tc):
            return slice(tc * 512, (tc + 1) * 512)

        evac_rr = [0]

        def evac_eng():
            evac_rr[0] += 1
            return "act" if evac_rr[0] % 2 else "dve"

        def copy_op(eng, out_ap, in_ap, reads, writes):
            if eng == "act":
                S.op("act", lambda e: e.copy(out=out_ap, in_=in_ap), reads=reads, writes=writes)
            else:
                S.op(eng, lambda e: e.tensor_copy(out=out_ap, in_=in_ap), reads=reads, writes=writes)

        S.op("sp", lambda e: e.dma_start(out=small[:], in_=small_d), writes=["small"], dma=True)
        S.op("sp", lambda e: e.dma_start(out=consts[:], in_=consts_d), writes=["consts"], dma=True)
        S.op("sp", lambda e: e.dma_start(out=posi[:], in_=pos_d), writes=["posi"], dma=True)
        S.op("dve", lambda e: e.tensor_copy(out=identb[:], in_=consts[:, O_IDF:O_IDF + 128]), reads=["consts"], writes=["identb"])
        S.op("dve", lambda e: e.tensor_copy(out=trib[:], in_=consts[:, O_TRI:O_TRI + 128]), reads=["consts"], writes=["trib"])
        S.op("dve", lambda e: e.memset(onesb[:], 1.0), writes=["onesb"])
        S.op("dve", lambda e: e.tensor_copy(out=posf[:], in_=posi[:]), reads=["posi"], writes=["posf"])
        S.op("dve", lambda e: e.tensor_tensor(out=ang[:], in0=posf[:].unsqueeze(2).to_broadcast([128, NT, 8]),
                                              in1=consts[:, O_FREQ:O_FREQ + 8].unsqueeze(1).to_broadcast([128, NT, 8]),
                                              op=ALU.mult), reads=["posf", "consts"], writes=["ang"])
        TWO_PI = 2.0 * math.pi
        ki = sb("ki", [128, NT, 8], I32)
        kf = sb("kf", [128, NT, 8], F32)
        rr = sb("rr", [128, NT, 8], F32)
        mm_ = sb("mm_", [128, NT, 8], F32)

        def sincos(dst, shift, nm):
            S.op("dve", lambda e: e.tensor_scalar(out=rr[:], in0=ang[:], scalar1=shift, scalar2=1.0 / TWO_PI, op0=ALU.add, op1=ALU.mult),
                 reads=["ang"], writes=["rr"])
            S.op("dve", lambda e: e.tensor_copy(out=ki[:], in_=rr[:]), reads=["rr"], writes=["ki"])
            S.op("dve", lambda e: e.tensor_copy(out=kf[:], in_=ki[:]), reads=["ki"], writes=["kf"])
            S.op("dve", lambda e: e.tensor_scalar(out=rr[:], in0=ang[:], scalar1=shift, scalar2=None, op0=ALU.add),
                 reads=["ang", "kf"], writes=["rr"])
            S.op("dve", lambda e: e.scalar_tensor_tensor(out=rr[:], in0=kf[:], scalar=-TWO_PI, in1=rr[:], op0=ALU.mult, op1=ALU.add),
                 reads=["kf", "rr"], writes=["rr"])
            S.op("dve", lambda e: e.tensor_scalar(out=mm_[:], in0=rr[:], scalar1=math.pi, scalar2=-TWO_PI, op0=ALU.is_gt, op1=ALU.mult),
                 reads=["rr"], writes=["mm_"])
            S.op("dve", lambda e: e.tensor_tensor(out=rr[:], in0=rr[:], in1=mm_[:], op=ALU.add), reads=["rr", "mm_"], writes=["rr"])
            S.op("dve", lambda e: e.tensor_scalar(out=mm_[:], in0=rr[:], scalar1=-math.pi, scalar2=TWO_PI, op0=ALU.is_lt, op1=ALU.mult),
                 reads=["rr"], writes=["mm_"])
            S.op("dve", lambda e: e.tensor_tensor(out=rr[:], in0=rr[:], in1=mm_[:], op=ALU.add), reads=["rr", "mm_"], writes=["rr"])
            S.op("act", lambda e: e.activation(out=dst[:], in_=rr[:], func=AF.Sin, scale=1.0 - 1e-6), reads=["rr"], writes=[nm])

        sincos(sinT, 0.0, "sinT")
        sincos(cosT, 0.5 * math.pi, "cosT")
        S.op("act", lambda e: e.activation(out=cact[:], in_=small[:, O_CT:O_CT + 8], func=AF.Silu), reads=["small"], writes=["cact"])
        S.op("dve", lambda e: e.tensor_copy(out=cactb[:], in_=cact[:]), reads=["cact"], writes=["cactb"])

        xin = [AV(0, 128, [4, D], F32), AV(16384, 128, [4, D], F32)]
        for tc in range(NTC):
            xv = xin[tc % 2]
            S.op("sp", (lambda e, xv=xv, tc=tc: e.dma_start(
                out=xv.ap, in_=x_d[tc * 512:(tc + 1) * 512, :].rearrange("(t p) d -> p t d", p=128))),
                writes=xv.keys(), dma=True)
            for c in range(8):
                b = nbank("G")
                for t in range(4):
                    S.op("pe", (lambda e, b=b, xv=xv, t=t, c=c: e.transpose(
                        banks[b][:, t * 128:(t + 1) * 128], xv.ap[:, t, c * 128:(c + 1) * 128], consts[:, O_IDF:O_IDF + 128])),
                        reads=xv.keys() + ["consts"], writes=[bk(b)])
                copy_op(evac_eng(), xT[:, c, tcs(tc)], banks[b][:], [bk(b)], [kx(c, tc)])

        def mod_steps(l, grp=None):
            lp = l % 2
            modT, avec = modTs[lp], avecs[lp]
            kM, kA = ("modT", lp), ("avec", lp)
            pm = 7
            steps = []
            for j in range(12):
                def sj(j=j):
                    s = wslot(grp)
                    wv = wsl[s][:, :].rearrange("p (k n) -> p k n", k=8)
                    wload(s, wv, w_ada_d[l][:, j * 512:(j + 1) * 512].rearrange("(kc p) n -> p kc n", p=128))
                    for oc in range(4):
                        col = j * 4 + oc
                        for kc in range(8):
                            S.op("pe", (lambda e, oc=oc, kc=kc, col=col: e.matmul(
                                banks[pm][:, col:col + 1], lhsT=wv[:, kc, oc * 128:(oc + 1) * 128], rhs=cactb[:, kc:kc + 1],
                                start=(kc == 0), stop=(kc == 7), skip_group_check=True)),
                                reads=[("ws", s), "cactb"], writes=[bk(pm)])
                steps.append(sj)

            def fin():
                S.op("dve", (lambda e: e.tensor_tensor(out=modT[:], in0=banks[pm][:, 0:48],
                                                       in1=small[:, O_BADA + l * 48:O_BADA + (l + 1) * 48], op=ALU.add)),
                     reads=[bk(pm), "small"], writes=[kM])
                S.op("dve", (lambda e: e.scalar_tensor_tensor(out=avec[:, 0:8], in0=modT[:, 8:16], scalar=1.0,
                                                              in1=small[:, O_N1W + l * 8:O_N1W + (l + 1) * 8],
                                                              op0=ALU.add, op1=ALU.mult)),
                     reads=[kM, "small"], writes=[kA])
                S.op("dve", (lambda e: e.scalar_tensor_tensor(out=avec[:, 8:16], in0=modT[:, 32:40], scalar=1.0,
                                                              in1=small[:, O_N2W + l * 8:O_N2W + (l + 1) * 8],
                                                              op0=ALU.add, op1=ALU.mult)),
                     reads=[kM, "small"], writes=[kA])
                S.op("dve", lambda e: e.tensor_scalar(out=avec[:, 16:24], in0=modT[:, 16:24], scalar1=0.5, scalar2=None, op0=ALU.mult),
                     reads=[kM], writes=[kA])
            steps.append(fin)
            return steps

        def norm_tc_steps(a_off, sh_off, lp, base=0):
            modT, avec = modTs[lp], avecs[lp]
            sqb = AV(base, 128, [8, 512], BF16)
            rst = [AV(base + 8192, 128, [512], F32), AV(base + 10240, 128, [512], F32)]
            tmp = [AV(base + 12288, 128, [512], F32), AV(base + 14336, 128, [512], F32), AV(base + 16384, 128, [512], F32)]
            st_ = {"ti": 0}

            def mk(tc):
                def f():
                    for c in range(8):
                        if c % 2 == 0:
                            S.op("act", (lambda e, c=c: e.activation(out=sqb.ap[:, c, :], in_=xT[:, c, tcs(tc)], func=AF.Square)),
                                 reads=[kx(c, tc)], writes=sqb.kidx(c))
                        else:
                            S.op("dve", (lambda e, c=c: e.tensor_tensor(out=sqb.ap[:, c, :], in0=xT[:, c, tcs(tc)],
                                                                        in1=xT[:, c, tcs(tc)], op=ALU.mult)),
                                 reads=[kx(c, tc)], writes=sqb.kidx(c))
                    b = nbank("n")
                    for c in range(8):
                        S.op("pe", (lambda e, c=c: e.matmul(banks[b][:], lhsT=onesb[:], rhs=sqb.ap[:, c, :],
                                                           start=(c == 0), stop=(c == 7))),
                             reads=sqb.kidx(c) + ["onesb"], writes=[bk(b)])
                    r = rst[tc % 2]
                    S.op("act", (lambda e: e.activation(out=r.ap, in_=banks[b][:], func=AF.Sqrt, bias=epsb[:, 0:1], scale=1.0 / D)),
                         reads=[bk(b), "epsb"], writes=r.keys())
                    S.op("dve", (lambda e: e.reciprocal(out=r.ap, in_=r.ap)), reads=r.keys(), writes=r.keys())
                    for c in range(8):
                        t_ = tmp[st_["ti"] % 3]
                        st_["ti"] += 1
                        S.op("dve", (lambda e, t_=t_, c=c: e.tensor_tensor(out=t_.ap, in0=xT[:, c, tcs(tc)], in1=r.ap, op=ALU.mult)),
                             reads=[kx(c, tc)] + r.keys(), writes=t_.keys())
                        S.op("act", (lambda e, t_=t_, c=c: e.activation(
                            out=hT[:, c, tcs(tc)], in_=t_.ap, func=AF.Identity,
                            scale=avec[:, a_off + c:a_off + c + 1], bias=modT[:, sh_off + c:sh_off + c + 1])),
                            reads=t_.keys() + [("avec", lp), ("modT", lp)], writes=[kh(c, tc)])
                return f
            return [mk(tc) for tc in range(NTC)]

        epsb = sb("epsb", [128, 1], F32)
        S.op("dve", lambda e: e.memset(epsb[:], EPS), writes=["epsb"])

        yattnT = AV(0, 128, [4, S_], BF16)
        ypoolT = AV(16384, 128, [2, S_], BF16)
        yconvT = AV(24576, 128, [2, S_], BF16)
        A1 = 32768
        KaT = AV(A1, 72, [4, S_], BF16)
        Va = AV(A1 + 16384, 128, [NT, 4, 65], BF16)
        QaTs = [AV(A1 + 24736, 72, [4, 512], BF16), AV(65536, 72, [4, 512], BF16)]
        o_ = A1 + 24736 + 4096
        qk_tok = [AV(o_, 128, [8, 72], BF16), AV(o_ + 1152, 128, [8, 72], BF16)]
        o_ += 2304
        mb_tok = [AV(o_, 128, [4, 72], BF16), AV(o_ + 576, 128, [4, 72], BF16)]
        o_ += 1152
        qn_s = [AV(16384, 128, [8, 64], F32), AV(18432, 128, [8, 64], F32)]
        ytok = AV(20480, 128, [4, 256], BF16)
        pT = [AV(22528 + i * 1024, 128, [512], BF16) for i in range(4)]
        sq_s = AV(26624, 128, [512], BF16)
        cmp_s = AV(27648, 128, [4, 8, 8], F32)
        sm_par = []
        for i_ in range(2):
            o2 = 28672 + i_ * 1920
            sm_par.append(dict(
                ssq8=AV(o2, 128, [8], F32), vv8=AV(o2 + 32, 128, [8], F32), yy8=AV(o2 + 64, 128, [8], F32),
                aa8=AV(o2 + 96, 128, [8], F32), gsb=AV(o2 + 128, 128, [4, 8], F32), rank_s=AV(o2 + 256, 128, [4, 8], F32),
                rt=[AV(o2 + 384 + k_ * 256, 128, [8, 8], F32) for k_ in range(4)],
                qr=AV(o2 + 1408, 128, [8, 16], F32)))
        rec_s = AV(65056, 128, [4], F32)

        def merge_steps(lists):
            idx = [0] * len(lists)
            tot = [max(1, len(x)) for x in lists]
            while True:
                best = None
                for i, x in enumerate(lists):
                    if idx[i] < len(x):
                        frac = idx[i] / tot[i]
                        if best is None or frac < best[0]:
                            best = (frac, i)
                if best is None:
                    break
                i = best[1]
                lists[i][idx[i]]()
                idx[i] += 1

        def tile_steps(l, hg, qc, j, wqk, wv, sqk, sv):
            t = qc * 4 + j
            blk = t // 2
            QaT = QaTs[qc % 2]
            qn = qn_s[t % 2]
            qt = qk_tok[t % 2]
            mb = mb_tok[t % 2]
            sp_ = sm_par[t % 2]
            ssq8, vv8, yy8, aa8, gsb, rank_s, rt, qr = (sp_["ssq8"], sp_["vv8"], sp_["yy8"], sp_["aa8"], sp_["gsb"],
                                                        sp_["rank_s"], sp_["rt"], sp_["qr"])
            st_ = {}
            steps = []

            def s1():
                bq = st_["bq"] = t % 2
                S.op("pool", (lambda e: e.memset(qt.ap[:, 4:8, 64:72], 0.0)), writes=qt.keys())
                S.op("pool", (lambda e: e.memset(qt.ap[:, 4:8, 64 + blk:65 + blk], 1.0)), writes=qt.keys())
                S.op("pool", (lambda e: e.memset(mb.ap[:, :, 64:72], NEGB)), writes=mb.keys())
                S.op("pool", (lambda e: e.memset(mb.ap[:, :, 64:65 + blk], 0.0)), writes=mb.keys())
                for kc in range(8):
                    S.op("pe", (lambda e, kc=kc: e.matmul(
                        banks[bq][:], lhsT=hT[:, kc, t * 128:(t + 1) * 128], rhs=wqk[:, kc, :],
                        start=(kc == 0), stop=(kc == 7))),
                        reads=[kh(kc, t // 4), ("ws", sqk)], writes=[bk(bq)])
                bv = nbank("g")
                for kc in range(8):
                    S.op("pe", (lambda e, kc=kc: e.matmul(
                        banks[bv][:, 0:256], lhsT=hT[:, kc, t * 128:(t + 1) * 128], rhs=wv[:, kc, :],
                        start=(kc == 0), stop=(kc == 7))),
                        reads=[kh(kc, t // 4), ("ws", sv)], writes=[bk(bv)])
                S.op("act", (lambda e: e.activation(out=sq_s.ap, in_=banks[bq][:], func=AF.Square)),
                     reads=[bk(bq)], writes=sq_s.keys())
                S.op("act", (lambda e: e.copy(
                    out=Va.ap[:, t, :, 0:64], in_=banks[bv][:, 0:256].rearrange("p (h d) -> p h d", h=4))),
                    reads=[bk(bv)], writes=Va.kidx(t))
            steps.append(s1)

            def s2():
                S.op("dve", lambda e: e.tensor_reduce(out=ssq8.ap, in_=sq_s.ap.rearrange("p (h d) -> p h d", h=8),
                                                      axis=AX.X, op=ALU.add),
                     reads=sq_s.keys(), writes=ssq8.keys())
                S.op("dve", lambda e: e.tensor_scalar(out=vv8.ap, in0=ssq8.ap, scalar1=1.0 / 64, scalar2=EPS, op0=ALU.mult, op1=ALU.add),
                     reads=ssq8.keys(), writes=vv8.keys())
                S.op("dve", lambda e: e.tensor_scalar(out=yy8.ap.bitcast(I32), in0=vv8.ap.bitcast(I32), scalar1=-0.5, scalar2=1597463007.0,
                                                      op0=ALU.mult, op1=ALU.add),
                     reads=vv8.keys(), writes=yy8.keys())
                for _ in range(2):
                    S.op("dve", lambda e: e.tensor_tensor(out=aa8.ap, in0=yy8.ap, in1=yy8.ap, op=ALU.mult),
                         reads=yy8.keys(), writes=aa8.keys())
                    S.op("dve", lambda e: e.scalar_tensor_tensor(out=aa8.ap, in0=aa8.ap, scalar=-0.5, in1=vv8.ap, op0=ALU.mult, op1=ALU.mult),
                         reads=aa8.keys() + vv8.keys(), writes=aa8.keys())
                    S.op("dve", lambda e: e.scalar_tensor_tensor(out=yy8.ap, in0=aa8.ap, scalar=1.5, in1=yy8.ap, op0=ALU.add, op1=ALU.mult),
                         reads=yy8.keys() + aa8.keys(), writes=yy8.keys())
            steps.append(s2)

            def s3():
                bq = st_["bq"]
                wq = small[:, O_QKW + l * 128:O_QKW + (l + 1) * 128].rearrange("p (a d) -> p a d", a=2)
                S.op("dve", (lambda e: e.tensor_tensor(
                    out=qn.ap, in0=banks[bq][:].rearrange("p (h d) -> p h d", h=8),
                    in1=yy8.ap.unsqueeze(2).to_broadcast([128, 8, 64]), op=ALU.mult)),
                    reads=[bk(bq)] + yy8.keys(), writes=qn.keys())
                S.op("dve", (lambda e: e.tensor_tensor(
                    out=qr.ap.rearrange("p (a h) d -> p a h d", a=2), in0=qn.ap[:, :, 0:16].rearrange("p (a h) d -> p a h d", a=2),
                    in1=wq[:, :, 0:16].unsqueeze(2).to_broadcast([128, 2, 4, 16]), op=ALU.mult)),
                    reads=qn.keys() + ["small"], writes=qr.keys())
                S.op("dve", (lambda e: e.tensor_tensor(
                    out=qt.ap[:, :, 16:64].rearrange("p (a h) d -> p a h d", a=2), in0=qn.ap[:, :, 16:64].rearrange("p (a h) d -> p a h d", a=2),
                    in1=wq[:, :, 16:64].unsqueeze(2).to_broadcast([128, 2, 4, 48]), op=ALU.mult)),
                    reads=qn.keys() + ["small"], writes=qt.keys())
            steps.append(s3)

            def s4():
                cb = cosT[:, t, :].unsqueeze(1).to_broadcast([128, 8, 8])
                sbb = sinT[:, t, :].unsqueeze(1).to_broadcast([128, 8, 8])
                x1 = qr.ap[:, :, 0:8]
                x2 = qr.ap[:, :, 8:16]
                S.op("dve", (lambda e: e.tensor_tensor(out=rt[0].ap, in0=x1, in1=cb, op=ALU.mult)),
                     reads=qr.keys() + ["cosT"], writes=rt[0].keys())
                S.op("dve", (lambda e: e.tensor_tensor(out=rt[1].ap, in0=x2, in1=sbb, op=ALU.mult)),
                     reads=qr.keys() + ["sinT"], writes=rt[1].keys())
                S.op("dve", (lambda e: e.tensor_tensor(out=rt[2].ap, in0=x2, in1=cb, op=ALU.mult)),
                     reads=qr.keys() + ["cosT"], writes=rt[2].keys())
                S.op("dve", (lambda e: e.tensor_tensor(out=rt[3].ap, in0=x1, in1=sbb, op=ALU.mult)),
                     reads=qr.keys() + ["sinT"], writes=rt[3].keys())
                S.op("dve", (lambda e: e.tensor_tensor(out=qt.ap[:, :, 0:8], in0=rt[0].ap, in1=rt[1].ap, op=ALU.subtract)),
                     reads=rt[0].keys() + rt[1].keys(), writes=qt.keys())
                S.op("dve", (lambda e: e.tensor_tensor(out=qt.ap[:, :, 8:16], in0=rt[2].ap, in1=rt[3].ap, op=ALU.add)),
                     reads=rt[2].keys() + rt[3].keys(), writes=qt.keys())
            steps.append(s4)

            def s5():
                btr = nbank("g")
                for h in range(4):
                    S.op("pe", (lambda e, h=h: e.transpose(
                        banks_b[btr][0:72, h * 128:(h + 1) * 128], qt.ap[:, 4 + h, 0:72], identb[:])),
                        reads=qt.keys() + ["identb"], writes=[bk(btr)])
                    S.op("pe", (lambda e, h=h: e.transpose(
                        banks_b[btr][0:64, 512 + h * 128:512 + (h + 1) * 128], qt.ap[:, h, 0:64], identb[:])),
                        reads=qt.keys() + ["identb"], writes=[bk(btr)])
                S.op("act", (lambda e: e.copy(
                    out=KaT.ap[:, :, t * 128:(t + 1) * 128], in_=banks_b[btr][0:72, 0:512].rearrange("p (h n) -> p h n", h=4))),
                    reads=[bk(btr)], writes=[k_ for h in range(4) for k_ in KaT.krange((h,), t * 128, 128)])
                S.op("dve", (lambda e: e.tensor_copy(
                    out=QaT.ap[0:64, :, j * 128:(j + 1) * 128], in_=banks_b[btr][0:64, 512:1024].rearrange("p (h n) -> p h n", h=4))),
                    reads=[bk(btr)], writes=QaT.keys())
                if t % 2 == 1 and blk < 7:
                    S.op("dve", (lambda e: e.tensor_reduce(
                        out=kmf[:], in_=KaT.ap[0:64, :, blk * 256:(blk + 1) * 256], axis=AX.X, op=ALU.add)),
                        reads=[k_ for h in range(4) for k_ in KaT.krange((h,), blk * 256, 256)], writes=["kmf"])
                    S.op("dve", (lambda e: e.tensor_scalar(
                        out=kmT[:, :, blk:blk + 1], in0=kmf[:].unsqueeze(2), scalar1=1.0 / 256, scalar2=None, op0=ALU.mult)),
                        reads=["kmf"], writes=["kmT"])
            steps.append(s5)

            def s6():
                if blk >= 4:
                    bg = nbank("g")
                    for h in range(4):
                        S.op("pe", (lambda e, h=h: e.matmul(
                            banks[bg][:, h * 8:h * 8 + blk], lhsT=QaT.ap[0:64, h, j * 128:(j + 1) * 128],
                            rhs=kmT[:, h, 0:blk], start=True, stop=True, skip_group_check=True)),
                            reads=QaT.keys() + ["kmT"], writes=[bk(bg)])
                    S.op("act", (lambda e: e.copy(
                        out=gsb.ap[:, :, 0:blk], in_=banks[bg][:, 0:32].rearrange("p (h n) -> p h n", h=4)[:, :, 0:blk])),
                        reads=[bk(bg)], writes=gsb.keys())
                    S.op("dve", (lambda e: e.tensor_tensor(
                        out=cmp_s.ap[:, :, 0:blk, 0:blk],
                        in0=gsb.ap[:, :, 0:blk].unsqueeze(2).to_broadcast([128, 4, blk, blk]),
                        in1=gsb.ap[:, :, 0:blk].unsqueeze(3).to_broadcast([128, 4, blk, blk]), op=ALU.is_gt)),
                        reads=gsb.keys(), writes=cmp_s.keys())
                    S.op("dve", (lambda e: e.tensor_reduce(
                        out=rank_s.ap[:, :, 0:blk], in_=cmp_s.ap[:, :, 0:blk, 0:blk], axis=AX.X, op=ALU.add)),
                        reads=cmp_s.keys(), writes=rank_s.keys())
                    S.op("dve", (lambda e: e.tensor_scalar(
                        out=mb.ap[:, :, 64:64 + blk], in0=rank_s.ap[:, :, 0:blk], scalar1=2.5, scalar2=NEGB,
                        op0=ALU.is_gt, op1=ALU.mult)),
                        reads=rank_s.keys(), writes=mb.keys())
                bm = nbank("g")
                for h in range(4):
                    S.op("pe", (lambda e, h=h: e.transpose(
                        banks_b[bm][0:72, h * 128:(h + 1) * 128], mb.ap[:, h, 0:72], identb[:])),
                        reads=mb.keys() + ["identb"], writes=[bk(bm)])
                S.op("act", (lambda e: e.copy(
                    out=QaT.ap[64:72, :, j * 128:(j + 1) * 128], in_=banks_b[bm][64:72, 0:512].rearrange("p (h n) -> p h n", h=4))),
                    reads=[bk(bm)], writes=QaT.keys())
            steps.append(s6)
            return steps

        def attn_steps(hg, qc):
            QaT = QaTs[qc % 2]
            nk = 4 * (qc + 1)
            items = [(h, kt) for h in range(4) for kt in range(nk)]
            st_ = {}

            def stage1(i):
                h, kt = items[i]
                j0 = max(0, kt - 4 * qc)
                bs = nbank("s")
                p_ = pT[i % 4]
                S.op("pe", (lambda e: e.matmul(
                    banks[bs][:, j0 * 128:512], lhsT=KaT.ap[0:72, h, kt * 128:(kt + 1) * 128],
                    rhs=QaT.ap[0:72, h, j0 * 128:512], start=True, stop=True)),
                    reads=KaT.krange((h,), kt * 128, 128) + QaT.keys(), writes=[bk(bs)])
                S.op("act", (lambda e: e.activation(
                    out=p_.ap[:, j0 * 128:512], in_=banks[bs][:, j0 * 128:512], func=AF.Exp, scale=0.125)),
                    reads=[bk(bs)], writes=p_.keys())
                if kt >= 4 * qc:
                    S.op("pool", (lambda e: e.tensor_tensor(
                        out=p_.ap[:, j0 * 128:(j0 + 1) * 128], in0=p_.ap[:, j0 * 128:(j0 + 1) * 128], in1=trib[:], op=ALU.mult)),
                        reads=p_.keys() + ["trib"], writes=p_.keys())

            def stage2(i):
                h, kt = items[i]
                j0 = max(0, kt - 4 * qc)
                p_ = pT[i % 4]
                if kt == 0:
                    st_[h] = nbank("a")
                ba = st_[h]
                accv = banks[ba][:].rearrange("p (j n) -> p j n", j=4)
                for jj in range(j0, 4):
                    S.op("pe", (lambda e, jj=jj: e.matmul(
                        accv[:, jj, 0:65], lhsT=p_.ap[:, jj * 128:(jj + 1) * 128], rhs=Va.ap[:, kt, h, :],
                        start=(kt == 0 and jj == 0), stop=(kt == 4 * qc + jj), skip_group_check=True)),
                        reads=p_.keys() + Va.kidx(kt), writes=[bk(ba)])
                if kt == nk - 1:
                    S.op("dve", (lambda e: e.reciprocal(out=rec_s.ap, in_=accv[:, :, 64])),
                         reads=[bk(ba)], writes=rec_s.keys())
                    S.op("dve", (lambda e: e.tensor_tensor(
                        out=ytok.ap[:, :, h * 64:(h + 1) * 64], in0=accv[:, :, 0:64],
                        in1=rec_s.ap.unsqueeze(2).to_broadcast([128, 4, 64]), op=ALU.mult)),
                        reads=[bk(ba)] + rec_s.keys(), writes=ytok.keys())

            n = len(items)
            LA = int(_os.environ.get('LA', '3'))
            steps = [(lambda: [stage1(i_) for i_ in range(LA)])]
            for i in range(n):
                def sk(i=i):
                    stage2(i)
                    if i + LA < n:
                        stage1(i + LA)
                steps.append(sk)

            def sy():
                by = nbank("g")
                for jj in range(4):
                    for cc in range(2):
                        S.op("pe", (lambda e, jj=jj, cc=cc: e.transpose(
                            banks_b[by][:, (cc * 4 + jj) * 128:(cc * 4 + jj + 1) * 128], ytok.ap[:, jj, cc * 128:(cc + 1) * 128], identb[:])),
                            reads=ytok.keys() + ["identb"], writes=[bk(by)])
                for cc in range(2):
                    copy_op("act" if cc == 0 else "dve", yattnT.ap[:, hg * 2 + cc, qc * 512:(qc + 1) * 512],
                            banks_b[by][:, cc * 512:(cc + 1) * 512], [bk(by)], yattnT.krange((hg * 2 + cc,), qc * 512, 512))
            steps.append(sy)
            return steps

        def attention_phase(l):
            for hg in range(2):
                sqk = wslot()
                wqk = wsl[sqk][:, :].rearrange("p (k n) -> p k n", k=8)
                wload(sqk, wqk[:, :, 0:256], w_in_d[l][:, 1024 + hg * 256:1024 + (hg + 1) * 256].rearrange("(kc p) n -> p kc n", p=128))
                wload(sqk, wqk[:, :, 256:512], w_in_d[l][:, 1536 + hg * 256:1536 + (hg + 1) * 256].rearrange("(kc p) n -> p kc n", p=128))
                sv = wslot()
                wv = wsl[sv][:, 0:2048].rearrange("p (k n) -> p k n", k=8)
                wload(sv, wv, w_in_d[l][:, 2048 + hg * 256:2048 + (hg + 1) * 256].rearrange("(kc p) n -> p kc n", p=128))
                S.op("pool", lambda e: e.memset(Va.ap[:, :, :, 64:65], 1.0), writes=Va.keys())
                for m_ in mb_tok:
                    S.op("pool", (lambda e, m_=m_: e.memset(m_.ap[:, :, 0:64], 0.0)), writes=m_.keys())

                def A(qc):
                    tl = [tile_steps(l, hg, qc, j, wqk, wv, sqk, sv) for j in range(4)]
                    out = []
                    dsk = int(_os.environ.get('DSK', '2'))
                    ns = len(tl[0])
                    for k in range(ns + 3 * dsk):
                        for i in range(4):
                            ix = k - i * dsk
                            if 0 <= ix < ns:
                                out.append(tl[i][ix])
                    return out

                merge_steps([A(0)])
                for qc in range(NTC):
                    lists = [attn_steps(hg, qc)]
                    if qc + 1 < NTC:
                        lists.append(A(qc + 1))
                    merge_steps(lists)

        PW = S_ + 16
        up_s = AV(A1, 128, [PW], F32)
        sA_s = AV(A1 + 8256, 128, [PW], F32)
        sB_s = AV(A1 + 16512, 128, [PW], F32)

        def fm_proj(wv_of_kc, b, tc):
            for kc in range(8):
                lhsT, rk = wv_of_kc(kc)
                S.op("pe", (lambda e, lhsT=lhsT, kc=kc, b=b, tc=tc: e.matmul(
                    banks[b][:], lhsT=lhsT, rhs=hT[:, kc, tcs(tc)], start=(kc == 0), stop=(kc == 7))),
                    reads=[kh(kc, tc)] + rk, writes=[bk(b)])

        def pool_phase(l):
            S.op("dve", lambda e: e.memset(poolW[:], 0.0), writes=[("poolW", g) for g in range(4)])
            for g in range(4):
                r0 = (g % 2) * 64
                S.op("pool", (lambda e, g=g, r0=r0, l=l: e.dma_start(out=poolW[r0:r0 + 64, g // 2, r0:r0 + 64], in_=pool_w_d[l, g])),
                     writes=[("poolW", g)], dma=True)
            s = wslot()
            wv = wsl[s][:, 0:2048].rearrange("p (k n) -> p k n", k=8)
            wload(s, wv, w_in_d[l][:, 0:256].rearrange("(kc p) n -> p kc n", p=128))
            for v_ in (up_s, sA_s, sB_s):
                S.op("dve", (lambda e, v_=v_: e.memset(v_.ap[:, 0:16], 0.0)), writes=v_.keys(0, 16))
            for cc in range(2):
                for tc in range(NTC):
                    b = nbank("G")
                    fm_proj(lambda kc, cc=cc: (wv[:, kc, cc * 128:(cc + 1) * 128], [("ws", s)]), b, tc)
                    copy_op(evac_eng(), up_s.ap[:, 16 + tc * 512:16 + (tc + 1) * 512], banks[b][:], [bk(b)],
                            up_s.keys(16 + tc * 512, 512))
                n_lv = 2 if cc == 0 else 4
                cur = up_s
                nxt = [sA_s, sB_s]
                srcs = {}
                for lv in range(1, n_lv + 1):
                    sh = 1 << (lv - 1)
                    dst = nxt[(lv - 1) % 2]
                    p0 = 64 if lv == n_lv else 0
                    S.op("dve", (lambda e, dst=dst, cur=cur, sh=sh, p0=p0: e.tensor_tensor(
                        out=dst.ap[p0:128, 16:PW], in0=cur.ap[p0:128, 16:PW], in1=cur.ap[p0:128, 16 - sh:PW - sh], op=ALU.add)),
                        reads=cur.keys(), writes=dst.keys(16, S_))
                    srcs[lv] = dst
                    cur = dst
                lo_src = srcs[n_lv - 1]
                hi_src = srcs[n_lv]
                pooledT = yconvT
                for (p0, p1, src) in ((0, 64, lo_src), (64, 128, hi_src)):
                    S.op("dve", (lambda e, p0=p0, p1=p1, src=src, cc=cc: e.tensor_tensor(
                        out=src.ap[p0:p1, 16:32], in0=src.ap[p0:p1, 16:32],
                        in1=consts[p0:p1, O_CORR + cc * 16:O_CORR + (cc + 1) * 16], op=ALU.mult)),
                        reads=src.keys() + ["consts"], writes=src.keys(16, 16))
                    S.op("dve", (lambda e, p0=p0, p1=p1, src=src, cc=cc: e.tensor_tensor(
                        out=pooledT.ap[p0:p1, cc, 0:16], in0=src.ap[p0:p1, 16:32], in1=up_s.ap[p0:p1, 16:32], op=ALU.subtract)),
                        reads=src.keys() + up_s.keys(), writes=pooledT.krange((cc,), 0, 16))
                    S.op("dve", (lambda e, p0=p0, p1=p1, src=src, cc=cc: e.scalar_tensor_tensor(
                        out=pooledT.ap[p0:p1, cc, 16:S_], in0=src.ap[p0:p1, 32:PW], scalar=consts[p0:p1, O_INVW + cc:O_INVW + cc + 1],
                        in1=up_s.ap[p0:p1, 32:PW], op0=ALU.mult, op1=ALU.subtract)),
                        reads=src.keys() + up_s.keys() + ["consts"], writes=pooledT.kidx(cc))
                for tc in range(NTC):
                    b = nbank("G")
                    S.op("pe", (lambda e, b=b, cc=cc, tc=tc: e.matmul(banks[b][:], lhsT=poolW[:, cc, :], rhs=pooledT.ap[:, cc, tcs(tc)],
                                                                      start=True, stop=True)),
                         reads=[("poolW", 2 * cc), ("poolW", 2 * cc + 1)] + pooledT.krange((cc,), tc * 512, 512), writes=[bk(b)])
                    S.op("act", (lambda e, b=b, cc=cc, tc=tc, l=l: e.activation(
                        out=ypoolT.ap[:, cc, tcs(tc)], in_=banks[b][:], func=AF.Identity,
                        scale=small[:, O_PSC + l * 2 + cc:O_PSC + l * 2 + cc + 1])),
                        reads=[bk(b), "small"], writes=ypoolT.krange((cc,), tc * 512, 512))

        def conv_phase(l):
            cu_s, cc_s, ac_s = up_s, sA_s, sB_s
            for cc in range(2):
                s = wslot()
                wv = wsl[s][:, 0:3072].rearrange("p (k i n) -> p k i n", k=8, i=3)
                for i in range(3):
                    c0 = 256 + i * 256 + cc * 128
                    wload(s, wv[:, :, i, :], w_in_d[l][:, c0:c0 + 128].rearrange("(kc p) n -> p kc n", p=128))
                S.op("dve", lambda e: e.memset(cu_s.ap[:, 0:16], 0.0), writes=cu_s.keys(0, 16))
                for tc in range(NTC):
                    b = nbank("G")
                    fm_proj(lambda kc: (wv[:, kc, 0, :], [("ws", s)]), b, tc)
                    copy_op("act", cu_s.ap[:, 16 + tc * 512:16 + (tc + 1) * 512], banks[b][:], [bk(b)], cu_s.keys(16 + tc * 512, 512))
                    b2 = nbank("G")
                    fm_proj(lambda kc: (wv[:, kc, 2, :], [("ws", s)]), b2, tc)
                    S.op("dve", (lambda e, b2=b2, tc=tc: e.tensor_tensor(
                        out=cu_s.ap[:, 16 + tc * 512:16 + (tc + 1) * 512], in0=banks[b2][:],
                        in1=cu_s.ap[:, 16 + tc * 512:16 + (tc + 1) * 512], op=ALU.mult)),
                        reads=[bk(b2)] + cu_s.keys(16 + tc * 512, 512), writes=cu_s.keys(16 + tc * 512, 512))
                cw = lambda k, cc=cc, l=l: small[:, O_CW + l * 6 + k * 2 + cc:O_CW + l * 6 + k * 2 + cc + 1]
                S.op("dve", (lambda e, cw=cw: e.tensor_scalar(out=ac_s.ap[:, 16:PW], in0=cu_s.ap[:, 16:PW], scalar1=cw(2), scalar2=None, op0=ALU.mult)),
                     reads=cu_s.keys() + ["small"], writes=ac_s.keys(16, S_))
                S.op("dve", (lambda e, cw=cw: e.scalar_tensor_tensor(out=ac_s.ap[:, 16:PW], in0=cu_s.ap[:, 15:PW - 1], scalar=cw(1),
                                                                     in1=ac_s.ap[:, 16:PW], op0=ALU.mult, op1=ALU.add)),
                     reads=cu_s.keys() + ac_s.keys() + ["small"], writes=ac_s.keys(16, S_))
                S.op("dve", (lambda e, cw=cw: e.scalar_tensor_tensor(out=ac_s.ap[:, 16:PW], in0=cu_s.ap[:, 14:PW - 2], scalar=cw(0),
                                                                     in1=ac_s.ap[:, 16:PW], op0=ALU.mult, op1=ALU.add)),
                     reads=cu_s.keys() + ac_s.keys() + ["small"], writes=ac_s.keys(16, S_))
                for tc in range(NTC):
                    b = nbank("G")
                    fm_proj(lambda kc: (wv[:, kc, 1, :], [("ws", s)]), b, tc)
                    S.op("dve", (lambda e, b=b, cc=cc, tc=tc: e.tensor_tensor(
                        out=yconvT.ap[:, cc, tcs(tc)], in0=banks[b][:], in1=ac_s.ap[:, 16 + tc * 512:16 + (tc + 1) * 512], op=ALU.mult)),
                        reads=[bk(b)] + ac_s.keys(16 + tc * 512, 512), writes=yconvT.krange((cc,), tc * 512, 512))

        merged = AV(A1, 128, [8, 1024], BF16)
        mo = A1 + 16384
        s_t = [AV(mo, 128, [512], F32), AV(mo + 2048, 128, [512], F32)]
        m_t = [AV(mo + 4096, 128, [512], F32), AV(mo + 6144, 128, [512], F32)]
        t_t = [AV(mo + 8192, 128, [512], F32), AV(mo + 10240, 128, [512], F32)]

        def merge_phase(l):
            lp = cur["lp"]
            avec = avecs[lp]
            ysrc = [(ypoolT, 2, 0), (yconvT, 2, 2), (yattnT, 4, 4)]
            pds = [p_pool_d, p_conv_d, p_attn_d]
            cnt = 0
            for hh in range(2):
                for c in range(8):
                    s = wslot()
                    wg = wsl[s][:, 0:3072].rearrange("p (k i n) -> p k i n", k=8, i=3)
                    wp = wsl[s][:, 3072:4096].rearrange("p (k n) -> p k n", k=8)
                    for i in range(3):
                        c0 = 2560 + i * 1024 + c * 128
                        wload(s, wg[:, :, i, :], w_in_d[l][:, c0:c0 + 128].rearrange("(kc p) n -> p kc n", p=128))
                    wload(s, wp[:, 0:2, :], p_pool_d[l][:, c * 128:(c + 1) * 128].rearrange("(kc p) n -> p kc n", p=128))
                    wload(s, wp[:, 2:4, :], p_conv_d[l][:, c * 128:(c + 1) * 128].rearrange("(kc p) n -> p kc n", p=128))
                    wload(s, wp[:, 4:8, :], p_attn_d[l][:, c * 128:(c + 1) * 128].rearrange("(kc p) n -> p kc n", p=128))
                    for th in range(2):
                        tc = hh * 2 + th
                        mt = m_t[cnt % 2]
                        for i in range(3):
                            bgt = nbank("G")
                            fm_proj(lambda kc, i=i: (wg[:, kc, i, :], [("ws", s)]), bgt, tc)
                            st_ = s_t[(cnt * 3 + i) % 2]
                            S.op("act", (lambda e, bgt=bgt, st_=st_: e.activation(out=st_.ap, in_=banks[bgt][:], func=AF.Tanh, scale=0.5)),
                                 reads=[bk(bgt)], writes=st_.keys())
                            ysv, nkk, koff = ysrc[i]
                            bp = nbank("G")
                            for kk in range(nkk):
                                S.op("pe", (lambda e, bp=bp, kk=kk, koff=koff, ysv=ysv, tc=tc, nkk=nkk, wp=wp: e.matmul(
                                    banks[bp][:], lhsT=wp[:, koff + kk, :], rhs=ysv.ap[:, kk, tcs(tc)], start=(kk == 0), stop=(kk == nkk - 1))),
                                    reads=[("ws", s)] + ysv.krange((kk,), tc * 512, 512), writes=[bk(bp)])
                            if i == 0:
                                S.op("dve", (lambda e, bp=bp, st_=st_, mt=mt: e.scalar_tensor_tensor(
                                    out=mt.ap, in0=st_.ap, scalar=1.0, in1=banks[bp][:], op0=ALU.add, op1=ALU.mult)),
                                    reads=[bk(bp)] + st_.keys(), writes=mt.keys())
                            else:
                                tt_ = t_t[(cnt * 3 + i) % 2]
                                S.op("dve", (lambda e, bp=bp, st_=st_, tt_=tt_: e.scalar_tensor_tensor(
                                    out=tt_.ap, in0=st_.ap, scalar=1.0, in1=banks[bp][:], op0=ALU.add, op1=ALU.mult)),
                                    reads=[bk(bp)] + st_.keys(), writes=tt_.keys())
                                if i == 1:
                                    S.op("dve", (lambda e, tt_=tt_, mt=mt: e.tensor_tensor(out=mt.ap, in0=mt.ap, in1=tt_.ap, op=ALU.add)),
                                         reads=mt.keys() + tt_.keys(), writes=mt.keys())
                                else:
                                    S.op("dve", (lambda e, tt_=tt_, mt=mt, c=c, th=th: e.tensor_tensor(
                                        out=merged.ap[:, c, th * 512:(th + 1) * 512], in0=mt.ap, in1=tt_.ap, op=ALU.add)),
                                        reads=mt.keys() + tt_.keys(), writes=merged.krange((c,), th * 512, 512))
                        cnt += 1
                for ob in range(2):
                    s = wslot()
                    wo = wsl[s][:, :].rearrange("p (k n) -> p k n", k=8)
                    wload(s, wo, w_out_d[l][:, ob * 512:(ob + 1) * 512].rearrange("(kc p) n -> p kc n", p=128))
                    for oo in range(4):
                        o = ob * 4 + oo
                        for th in range(2):
                            tc = hh * 2 + th
                            b = nbank("G")
                            for c in range(8):
                                S.op("pe", (lambda e, b=b, c=c, oo=oo, th=th, wo=wo: e.matmul(
                                    banks[b][:], lhsT=wo[:, c, oo * 128:(oo + 1) * 128], rhs=merged.ap[:, c, th * 512:(th + 1) * 512],
                                    start=(c == 0), stop=(c == 7))),
                                    reads=[("ws", s)] + merged.krange((c,), th * 512, 512), writes=[bk(b)])
                            S.op("dve", (lambda e, b=b, o=o, tc=tc: e.scalar_tensor_tensor(
                                out=xT[:, o, tcs(tc)], in0=banks[b][:], scalar=avec[:, 16 + o:17 + o], in1=xT[:, o, tcs(tc)],
                                op0=ALU.mult, op1=ALU.add)),
                                reads=[bk(b), kx(o, tc), ("avec", lp)], writes=[kx(o, tc)])

        actT = [AV(0, 128, [4, S_], BF16), AV(16384, 128, [4, S_], BF16)]
        sl_t = [AV(32768, 128, [512], F32), AV(32768 + 2048, 128, [512], F32)]

        def ffn_steps(l, pool, tail_norm=None, grp=None):
            lp = l % 2
            modT = modTs[lp]
            nfb = 6
            st_ = {"cnt": 0}
            steps = []
            for fb in range(nfb):
                nch = 4 if fb < 5 else 2
                f0 = fb * 512
                ncol = nch * 128
                W = {}

                def sload(fb=fb, nch=nch, f0=f0, ncol=ncol, W=W):
                    sg = wslot(grp)
                    wg = wsl[sg][:, 0:8 * ncol].rearrange("p (k n) -> p k n", k=8)
                    wload(sg, wg, w_gate_d[l][:, f0:f0 + ncol].rearrange("(kc p) n -> p kc n", p=128))
                    su = wslot(grp)
                    wu = wsl[su][:, 0:8 * ncol].rearrange("p (k n) -> p k n", k=8)
                    wload(su, wu, w_up_d[l][:, f0:f0 + ncol].rearrange("(kc p) n -> p kc n", p=128))
                    sd = wslot(grp)
                    wd = wsl[sd][:, 0:nch * 1024].rearrange("p (j n) -> p j n", j=nch)
                    wload(sd, wd, w_down_d[l][f0:f0 + ncol, :].rearrange("(j p) n -> p j n", p=128))
                    W.update(sg=sg, wg=wg, su=su, wu=wu, sd=sd, wd=wd)
                steps.append(sload)
                at = actT[fb % 2]
                for j in range(nch):
                    for tc in range(NTC):
                        def sgu(j=j, tc=tc, W=W, at=at):
                            wg, wu, sg, su = W["wg"], W["wu"], W["sg"], W["su"]
                            bg_ = nbank(pool)
                            fm_proj(lambda kc: (wg[:, kc, j * 128:(j + 1) * 128], [("ws", sg)]), bg_, tc)
                            bu_ = nbank(pool)
                            fm_proj(lambda kc: (wu[:, kc, j * 128:(j + 1) * 128], [("ws", su)]), bu_, tc)
                            sl = sl_t[st_["cnt"] % 2]
                            st_["cnt"] += 1
                            S.op("act", (lambda e: e.activation(out=sl.ap, in_=banks[bg_][:], func=AF.Silu)),
                                 reads=[bk(bg_)], writes=sl.keys())
                            S.op("dve", (lambda e: e.tensor_tensor(
                                out=at.ap[:, j, tcs(tc)], in0=banks[bu_][:], in1=sl.ap, op=ALU.mult)),
                                reads=[bk(bu_)] + sl.keys(), writes=at.krange((j,), tc * 512, 512))
                        steps.append(sgu)
                dn_order = [(o, tc) for o in range(8) for tc in range(NTC)] if fb < nfb - 1 else \
                           [(o, tc) for tc in range(NTC) for o in range(8)]
                for (o, tc) in dn_order:
                    if True:
                        def sdn(o=o, tc=tc, W=W, at=at, nch=nch):
                            wd, sd = W["wd"], W["sd"]
                            b = nbank(pool)
                            for j in range(nch):
                                S.op("pe", (lambda e, j=j: e.matmul(
                                    banks[b][:], lhsT=wd[:, j, o * 128:(o + 1) * 128], rhs=at.ap[:, j, tcs(tc)],
                                    start=(j == 0), stop=(j == nch - 1))),
                                    reads=[("ws", sd)] + at.krange((j,), tc * 512, 512), writes=[bk(b)])
                            S.op("dve", (lambda e: e.scalar_tensor_tensor(
                                out=xT[:, o, tcs(tc)], in0=banks[b][:], scalar=modT[:, 40 + o:41 + o], in1=xT[:, o, tcs(tc)],
                                op0=ALU.mult, op1=ALU.add)),
                                reads=[bk(b), kx(o, tc), ("modT", lp)], writes=[kx(o, tc)])
                        steps.append(sdn)
                        if fb == nfb - 1 and tail_norm is not None and o == 7 and tc >= 1:
                            steps.append(tail_norm[tc - 1])
            if tail_norm is not None:
                steps.append(tail_norm[3])
            return steps

        KSTOP = int(_os.environ.get("KSTOP", "99"))
        if KSTOP >= 1:
            merge_steps([mod_steps(0)])
        if KSTOP >= 2:
            merge_steps([norm_tc_steps(0, 0, 0)])
        for l in range(L):
            if KSTOP < 3:
                break
            cur["lp"] = l % 2
            attention_phase(l)
            if KSTOP >= 4:
                pool_phase(l)
            if KSTOP >= 5:
                conv_phase(l)
            if KSTOP >= 6:
                merge_phase(l)
            if KSTOP >= 8:
                n2 = norm_tc_steps(8, 24, l % 2, base=40960)
                n1 = None
                if l + 1 < L:
                    n1 = norm_tc_steps(0, 0, (l + 1) % 2, base=40960)
                if n1 is not None and _os.environ.get("TAILN", "1") == "1":
                    fs = ffn_steps(l, "F", tail_norm=n1, grp=("ffn" if l + 1 < L else None))
                    n1 = None
                else:
                    fs = ffn_steps(l, "F", grp=("ffn" if l + 1 < L else None))
                if _os.environ.get("HEADN", "1") == "1":
                    head = [n2[0], n2[1], fs[0], fs[1], n2[2], fs[2], n2[3]]
                    rest = fs[3:]
                else:
                    head = n2
                    rest = fs
                merge_steps([head])
                if l + 1 < L:
                    if _os.environ.get("MODM", "1") == "1":
                        ncut = (len(rest) * 3) // 5
                        merge_steps([rest[:ncut], mod_steps(l + 1, grp="mod")])
                        merge_steps([rest[ncut:]])
                    else:
                        merge_steps([rest])
                        merge_steps([mod_steps(l + 1)])
                else:
                    merge_steps([rest])
                if n1 is not None:
                    merge_steps([n1])
            elif KSTOP >= 7:
                merge_steps([norm_tc_steps(8, 24, l % 2)])

        xo = [AV(0, 128, [D], F32), AV(4096, 128, [D], F32)]
        for t in range(NT):
            xv = xo[t % 2]
            for half in range(2):
                b = nbank("G")
                for cq in range(4):
                    c = half * 4 + cq
                    S.op("pe", (lambda e, b=b, cq=cq, c=c, t=t: e.transpose(
                        banks[b][:, cq * 128:(cq + 1) * 128], xT[:, c, t * 128:(t + 1) * 128], consts[:, O_IDF:O_IDF + 128])),
                        reads=[kx(c, t // 4), "consts"], writes=[bk(b)])
                copy_op(evac_eng(), xv.ap[:, half * 512:(half + 1) * 512], banks[b][:], [bk(b)], xv.keys(half * 512, 512))
            S.op("sp", (lambda e, xv=xv, t=t: e.dma_start(out=out_d[t * 128:(t + 1) * 128, :], in_=xv.ap)),
                 reads=xv.keys(), dma=True)
        S.emit(st)
    return nc


def _consts():
    c = np.zeros((128, NCF), np.float32)
    c[:, O_IDF:O_IDF + 128] = np.eye(128, dtype=np.float32)
    kk = np.arange(128)[:, None]
    qq = np.arange(128)[None, :]
    c[:, O_TRI:O_TRI + 128] = (kk <= qq).astype(np.float32)
    c[:, O_FREQ:O_FREQ + 8] = (500000.0 ** (-np.arange(0, 16, 2, dtype=np.float32) / 16.0)).astype(np.float32)[None, :]
    for p in range(128):
        for cc in range(2):
            w = POOL_W[cc * 2 + (1 if p >= 64 else 0)]
            c[p, O_INVW + cc] = 1.0 / w
            for t in range(16):
                c[p, O_CORR + cc * 16 + t] = 1.0 / min(t + 1, w)
    return c


def _small(b, c, norm1_w, norm2_w, b_ada, pool_scale, conv_w, q_norm_w, k_norm_w, l0, L):
    s = np.zeros((128, NS), np.float32)
    for l in range(L):
        gl = l0 + l
        s[:, O_N1W + l * 8:O_N1W + (l + 1) * 8] = norm1_w[gl].reshape(8, 128).T
        s[:, O_N2W + l * 8:O_N2W + (l + 1) * 8] = norm2_w[gl].reshape(8, 128).T
        s[:, O_BADA + l * 48:O_BADA + (l + 1) * 48] = b_ada[gl].reshape(48, 128).T
        s[:, O_PSC + l * 2:O_PSC + (l + 1) * 2] = pool_scale[gl].reshape(2, 128).T
        s[:, O_CW + l * 6:O_CW + (l + 1) * 6] = conv_w[gl].reshape(3, 2, 128).transpose(2, 0, 1).reshape(128, 6)
        s[:, O_QKW + l * 128:O_QKW + l * 128 + 64] = q_norm_w[gl][None, :]
        s[:, O_QKW + l * 128 + 64:O_QKW + (l + 1) * 128] = k_norm_w[gl][None, :]
    s[:, O_CT:O_CT + 8] = c[b].reshape(8, 128).T
    return s


_NC_CACHE = {}


def _get_nc(L):
    if L not in _NC_CACHE:
        _NC_CACHE[L] = build(L)
    return _NC_CACHE[L]


def _run(x, c, positions, P, l0, L):
    nc = build(L)
    consts = _consts()
    in_maps = []
    wsl_ = slice(l0, l0 + L)
    for b in range(8):
        m = {
            "x": np.ascontiguousarray(x[b]),
            "small": _small(b, c, P["norm1_w"], P["norm2_w"], P["b_ada"], P["pool_scale"], P["conv_w"],
                            P["q_norm_w"], P["k_norm_w"], l0, L),
            "consts": consts,
            "pos": np.ascontiguousarray(positions[b].reshape(NT, 128).T.astype(np.int32)),
        }
        for k in ("w_ada", "w_in", "pool_w", "p_pool", "p_conv", "p_attn", "w_out", "w_gate", "w_up", "w_down"):
            m[k] = np.ascontiguousarray(P[k][wsl_])
        in_maps.append(m)
    res = run_bass_kernel_spmd(nc, in_maps, core_ids=list(range(8)))
    return np.stack([np.asarray(r["out"]) for r in res.results], axis=0)


def kernel(x, c, positions, norm1_w, norm2_w, w_ada, b_ada, w_in, pool_w, pool_scale, conv_w,
           q_norm_w, k_norm_w, p_pool, p_conv, p_attn, w_out, w_gate, w_up, w_down):
    P = dict(norm1_w=np.asarray(norm1_w), norm2_w=np.asarray(norm2_w), w_ada=np.asarray(w_ada), b_ada=np.asarray(b_ada),
             w_in=np.asarray(w_in), pool_w=np.asarray(pool_w), pool_scale=np.asarray(pool_scale), conv_w=np.asarray(conv_w),
             q_norm_w=np.asarray(q_norm_w), k_norm_w=np.asarray(k_norm_w), p_pool=np.asarray(p_pool), p_conv=np.asarray(p_conv),
             p_attn=np.asarray(p_attn), w_out=np.asarray(w_out), w_gate=np.asarray(w_gate), w_up=np.asarray(w_up),
             w_down=np.asarray(w_down))
    x = np.asarray(x, dtype=np.float32)
    c = np.asarray(c)
    positions = np.asarray(positions)
    if FUSED:
        out = _run(x, c, positions, P, 0, DEPTH)
    else:
        out = x
        for l in range(DEPTH):
            out = _run(out, c, positions, P, l, 1)
    return out.astype(np.float32)
```

```python
import math
import os as _os
from contextlib import ExitStack

import numpy as np
import concourse.bass as bass
import concourse.mybir as mybir
from concourse.bass_utils import run_bass_kernel_spmd

F32 = mybir.dt.float32
BF16 = mybir.dt.bfloat16
I32 = mybir.dt.int32
ALU = mybir.AluOpType
AF = mybir.ActivationFunctionType
AX = mybir.AxisListType

FUSED = True
DEPTH = 4
D = 1024
S_ = 2048
NT = 16
NTC = 4
DFF = 2816
EPS = 1e-6
NEGB = -30000.0
O_N1W, O_N2W, O_BADA, O_PSC, O_CW, O_QKW, O_CT, NS = 0, 32, 64, 256, 264, 288, 800, 808
O_IDF, O_TRI, O_FREQ, O_INVW, O_CORR, NCF = 0, 128, 256, 264, 266, 298
POOL_W = (2, 4, 8, 16)

ENGS = ["pe", "act", "dve", "pool", "sp"]
SELFSYNC = _os.environ.get("SELFSYNC", "drain")


class Op:
    __slots__ = ("eng", "fn", "deps", "is_dma", "sig", "has_dep", "name", "pos", "selfdep")

    def __init__(self, eng, fn, is_dma, name=None):
        self.eng = eng
        self.fn = fn
        self.deps = []
        self.selfdep = False
        self.is_dma = is_dma
        self.sig = None
        self.has_dep = False
        self.name = name


class Sched:
    def __init__(self, nc, n_dma_sems=8, maxv=12000):
        self.nc = nc
        self.ops = {e: [] for e in ENGS}
        self.last_writer = {}
        self.readers = {}
        self.n_dma_sems = n_dma_sems
        self.maxv = maxv
        self.cow = {}
        self.old_r = {}
        self.old_w = {}

    def new_gen(self, k):
        self.old_r[k] = self.readers.get(k, [])
        self.old_w[k] = self.cow.get(k, [])
        self.readers[k] = []
        self.cow[k] = []

    def op(self, eng, fn, reads=(), writes=(), dma=False, name=None, joins=()):
        o = Op(eng, fn, dma, name)
        deps = {}
        lw = self.last_writer
        rd = self.readers
        if eng != "pe":
            extra = [k for k in reads if isinstance(k, tuple) and k[0] == "bank"]
            if extra:
                writes = list(writes) + extra
        for k in joins:
            for r in self.old_r.get(k, ()):
                deps[id(r)] = (r, "war")
            for w in self.old_w.get(k, ()):
                if id(w) not in deps:
                    deps[id(w)] = (w, "waw")
            self.cow.setdefault(k, []).append(o)
        for k in reads:
            w = lw.get(k)
            if w is not None:
                deps[id(w)] = (w, "raw")
            for w in self.cow.get(k, ()):
                deps[id(w)] = (w, "raw")
        for k in writes:
            w = lw.get(k)
            if w is not None and id(w) not in deps:
                deps[id(w)] = (w, "waw")
            for r in rd.get(k, ()):
                if id(r) not in deps:
                    deps[id(r)] = (r, "war")
        for k in reads:
            rd.setdefault(k, []).append(o)
        for k in writes:
            lw[k] = o
            rd[k] = []
        best = {}
        npos = len(self.ops[eng])
        for d, kind in deps.values():
            if d.eng == eng and not d.is_dma and not dma:
                if eng == "pe":
                    continue
                if SELFSYNC == "drain":
                    if npos - d.pos <= 2:
                        o.selfdep = True
                    continue
            if d.is_dma:
                o.deps.append(d)
                d.has_dep = True
            else:
                b = best.get(d.eng)
                if b is None or d.pos > b.pos:
                    best[d.eng] = d
        for d in best.values():
            o.deps.append(d)
            d.has_dep = True
        o.pos = len(self.ops[eng])
        self.ops[eng].append(o)
        return o

    def emit(self, stack):
        nc = self.nc
        for e in ENGS:
            cnt = 0
            nsem = 0
            cur = None
            for o in self.ops[e]:
                if o.is_dma or not o.has_dep:
                    continue
                if cur is None or cnt >= self.maxv:
                    cur = stack.enter_context(nc.semaphore(f"s_{e}_{nsem}"))
                    nsem += 1
                    cnt = 0
                cnt += 1
                o.sig = (cur, cnt, 1)
        for e in ENGS:
            dl = [o for o in self.ops[e] if o.is_dma]
            if not dl:
                continue
            sems = [stack.enter_context(nc.semaphore(f"d_{e}_{i}")) for i in range(min(self.n_dma_sems, len(dl)))]
            cnts = [0] * len(sems)
            for i, o in enumerate(dl):
                j = i % len(sems)
                cnts[j] += 1
                o.sig = (sems[j], 16 * cnts[j], 16)

        def run_engine(ename, eng):
            waited = {}
            for o in self.ops[ename]:
                need = {}
                for d in o.deps:
                    s, v = d.sig[0], d.sig[1]
                    if waited.get(s, 0) >= v:
                        continue
                    if need.get(s, 0) < v:
                        need[s] = v
                if o.is_dma:
                    s, v = o.sig[0], o.sig[1]
                    if v > 16 and waited.get(s, 0) < v - 16:
                        need[s] = max(need.get(s, 0), v - 16)
                for s, v in need.items():
                    eng.wait_ge(s, v)
                    waited[s] = v
                if o.selfdep:
                    eng.drain()
                ins = o.fn(eng)
                if o.sig is not None:
                    ins.then_inc(o.sig[0], o.sig[2])
            last = {}
            for o in self.ops[ename]:
                if o.is_dma:
                    last[o.sig[0]] = o.sig[1]
            for s, v in last.items():
                if waited.get(s, 0) < v:
                    eng.wait_ge(s, v)

        with nc.Block() as block:
            @block.tensor
            def _(e):
                run_engine("pe", e)

            @block.scalar
            def _(e):
                run_engine("act", e)

            @block.vector
            def _(e):
                run_engine("dve", e)

            @block.gpsimd
            def _(e):
                run_engine("pool", e)

            @block.sync
            def _(e):
                run_engine("sp", e)


ARENA = 69632
BLK = 1024


def build(L):
    nc = bass.Bass("TRN2", target_bir_lowering=False)

    def din(name, shape, dt=F32):
        return nc.dram_tensor(name, shape, dt, kind="ExternalInput").ap()

    x_d = din("x", [S_, D])
    small_d = din("small", [128, NS])
    consts_d = din("consts", [128, NCF])
    pos_d = din("pos", [128, NT], I32)
    w_ada_d = din("w_ada", [L, D, 6 * D])
    w_in_d = din("w_in", [L, D, 5632])
    pool_w_d = din("pool_w", [L, 4, 64, 64])
    p_pool_d = din("p_pool", [L, 256, D])
    p_conv_d = din("p_conv", [L, 256, D])
    p_attn_d = din("p_attn", [L, 512, D])
    w_out_d = din("w_out", [L, D, D])
    w_gate_d = din("w_gate", [L, D, DFF])
    w_up_d = din("w_up", [L, D, DFF])
    w_down_d = din("w_down", [L, DFF, D])
    lidx_d = None
    out_d = nc.dram_tensor("out", [S_, D], F32, kind="ExternalOutput").ap()

    with ExitStack() as st:
        S = Sched(nc)

        def sb(name, shape, dt):
            return st.enter_context(nc.sbuf_tensor("sb_" + name, shape, dt))

        xT = sb("xT", [128, 8, S_], F32)
        hT = sb("hT", [128, 8, S_], BF16)
        wsl = [sb(f"ws{i}", [128, 4096], BF16) for i in range(4)]
        small = sb("small", [128, NS], F32)
        consts = sb("consts", [128, NCF], F32)
        identb = sb("identb", [128, 128], BF16)
        trib = sb("trib", [128, 128], BF16)
        onesb = sb("onesb", [128, 128], BF16)
        modTs = [sb("modT0", [128, 48], F32), sb("modT1", [128, 48], F32)]
        cur = {"lp": 0}
        avecs = [sb("avec0", [128, 32], F32), sb("avec1", [128, 32], F32)]
        posi = sb("posi", [128, NT], I32)
        posf = sb("posf", [128, NT], F32)
        ang = sb("ang", [128, NT, 8], F32)
        cosT = sb("cosT", [128, NT, 8], F32)
        sinT = sb("sinT", [128, NT, 8], F32)
        cact = sb("cact", [128, 8], F32)
        cactb = sb("cactb", [128, 8], BF16)
        poolW = sb("poolW", [128, 2, 128], BF16)
        kmT = sb("kmT", [64, 4, 8], BF16)
        kmf = sb("kmf", [64, 4], F32)
        arena = sb("arena", [128, ARENA // 4], F32)
        arena_b = arena.bitcast(BF16)
        banks = [st.enter_context(nc.psum_tensor(f"bank{i}", [128, 512], F32)) for i in range(8)]
        banks_b = [b.bitcast(BF16) for b in banks]

        class AV:
            def __init__(self, base, P, fshape, dt):
                self.base = base
                self.P = P
                self.fshape = list(fshape)
                self.dt = dt
                self.esz = 2 if dt == BF16 else 4
                n = int(np.prod(fshape))
                self.n = n
                assert base % 32 == 0 and base + n * self.esz <= ARENA, (base, n, self.esz)
                src = arena_b if dt == BF16 else arena
                e0 = base // self.esz
                flat = src[0:P, e0:e0 + n]
                if len(fshape) == 1:
                    self.ap = flat
                elif len(fshape) == 2:
                    self.ap = flat.rearrange("p (a b) -> p a b", a=fshape[0])
                elif len(fshape) == 3:
                    self.ap = flat.rearrange("p (a b c) -> p a b c", a=fshape[0], b=fshape[1])
                else:
                    raise ValueError

            def keys(self, lo=0, n=None):
                if n is None:
                    n = self.n - lo
                b0 = (self.base + lo * self.esz) // BLK
                b1 = (self.base + (lo + n) * self.esz - 1) // BLK
                return [("A", b) for b in range(b0, b1 + 1)]

            def kidx(self, *idx):
                strides = []
                s = 1
                for d in reversed(self.fshape):
                    strides.append(s)
                    s *= d
                strides = strides[::-1]
                lo = sum(i * st_ for i, st_ in zip(idx, strides))
                n = strides[len(idx) - 1] if idx else self.n
                return self.keys(lo, n)

            def krange(self, idx, lo, n):
                strides = []
                s = 1
                for d in reversed(self.fshape):
                    strides.append(s)
                    s *= d
                strides = strides[::-1]
                base = sum(i * st_ for i, st_ in zip(idx, strides))
                return self.keys(base + lo, n)

        bank_rr = {"F": [0, [0, 1, 2, 3, 4, 5]], "n": [0, [6]], "g": [0, [int(x) for x in _os.environ.get("GB", "2,3").split(",")]], "s": [0, [int(x) for x in _os.environ.get("SB", "4,5").split(",")]], "a": [0, [6, 7]], "G": [0, [0, 1, 2, 3, 4, 5, 6, 7]]}

        def nbank(pool):
            st_ = bank_rr[pool]
            b = st_[1][st_[0] % len(st_[1])]
            st_[0] += 1
            return b

        def bk(b):
            return ("bank", b)

        ws_rr = [0]

        ws_grp = {"ffn": [0, [0, 1, 2]], "mod": [0, [3]]}

        def wslot(grp=None):
            if grp is None:
                i = ws_rr[0] % 4
                ws_rr[0] += 1
            else:
                g_ = ws_grp[grp]
                i = g_[1][g_[0] % len(g_[1])]
                g_[0] += 1
            S.new_gen(("ws", i))
            return i

        def wload(slot, dst_ap, src_ap):
            S.op("pool", lambda e: e.dma_start(out=dst_ap, in_=src_ap), joins=[("ws", slot)], dma=True)

        def kx(c, tc):
            return ("xT", c, tc)

        def kh(c, tc):
            return ("hT", c, tc)

        def tcs(tc):
            return slice(tc * 512, (tc + 1) * 512)

        evac_rr = [0]

        def evac_eng():
            evac_rr[0] += 1
            return "act" if evac_rr[0] % 2 else "dve"

        def copy_op(eng, out_ap, in_ap, reads, writes):
            if eng == "act":
                S.op("act", lambda e: e.copy(out=out_ap, in_=in_ap), reads=reads, writes=writes)
            else:
                S.op(eng, lambda e: e.tensor_copy(out=out_ap, in_=in_ap), reads=reads, writes=writes)

        S.op("sp", lambda e: e.dma_start(out=small[:], in_=small_d), writes=["small"], dma=True)
        S.op("sp", lambda e: e.dma_start(out=consts[:], in_=consts_d), writes=["consts"], dma=True)
        S.op("sp", lambda e: e.dma_start(out=posi[:], in_=pos_d), writes=["posi"], dma=True)
        S.op("dve", lambda e: e.tensor_copy(out=identb[:], in_=consts[:, O_IDF:O_IDF + 128]), reads=["consts"], writes=["identb"])
        S.op("dve", lambda e: e.tensor_copy(out=trib[:], in_=consts[:, O_TRI:O_TRI + 128]), reads=["consts"], writes=["trib"])
        S.op("dve", lambda e: e.memset(onesb[:], 1.0), writes=["onesb"])
        S.op("dve", lambda e: e.tensor_copy(out=posf[:], in_=posi[:]), reads=["posi"], writes=["posf"])
        S.op("dve", lambda e: e.tensor_tensor(out=ang[:], in0=posf[:].unsqueeze(2).to_broadcast([128, NT, 8]),
                                              in1=consts[:, O_FREQ:O_FREQ + 8].unsqueeze(1).to_broadcast([128, NT, 8]),
                                              op=ALU.mult), reads=["posf", "consts"], writes=["ang"])
        TWO_PI = 2.0 * math.pi
        ki = sb("ki", [128, NT, 8], I32)
        kf = sb("kf", [128, NT, 8], F32)
        rr = sb("rr", [128, NT, 8], F32)
        mm_ = sb("mm_", [128, NT, 8], F32)

        def sincos(dst, shift, nm):
            S.op("dve", lambda e: e.tensor_scalar(out=rr[:], in0=ang[:], scalar1=shift, scalar2=1.0 / TWO_PI, op0=ALU.add, op1=ALU.mult),
                 reads=["ang"], writes=["rr"])
            S.op("dve", lambda e: e.tensor_copy(out=ki[:], in_=rr[:]), reads=["rr"], writes=["ki"])
            S.op("dve", lambda e: e.tensor_copy(out=kf[:], in_=ki[:]), reads=["ki"], writes=["kf"])
            S.op("dve", lambda e: e.tensor_scalar(out=rr[:], in0=ang[:], scalar1=shift, scalar2=None, op0=ALU.add),
                 reads=["ang", "kf"], writes=["rr"])
            S.op("dve", lambda e: e.scalar_tensor_tensor(out=rr[:], in0=kf[:], scalar=-TWO_PI, in1=rr[:], op0=ALU.mult, op1=ALU.add),
                 reads=["kf", "rr"], writes=["rr"])
            S.op("dve", lambda e: e.tensor_scalar(out=mm_[:], in0=rr[:], scalar1=math.pi, scalar2=-TWO_PI, op0=ALU.is_gt, op1=ALU.mult),
                 reads=["rr"], writes=["mm_"])
            S.op("dve", lambda e: e.tensor_tensor(out=rr[:], in0=rr[:], in1=mm_[:], op=ALU.add), reads=["rr", "mm_"], writes=["rr"])
            S.op("dve", lambda e: e.tensor_scalar(out=mm_[:], in0=rr[:], scalar1=-math.pi, scalar2=TWO_PI, op0=ALU.is_lt, op1=ALU.mult),
                 reads=["rr"], writes=["mm_"])
            S.op("dve", lambda e: e.tensor_tensor(out=rr[:], in0=rr[:], in1=mm_[:], op=ALU.add), reads=["rr", "mm_"], writes=["rr"])
            S.op("act", lambda e: e.activation(out=dst[:], in_=rr[:], func=AF.Sin, scale=1.0 - 1e-6), reads=["rr"], writes=[nm])

        sincos(sinT, 0.0, "sinT")
        sincos(cosT, 0.5 * math.pi, "cosT")
        S.op("act", lambda e: e.activation(out=cact[:], in_=small[:, O_CT:O_CT + 8], func=AF.Silu), reads=["small"], writes=["cact"])
        S.op("dve", lambda e: e.tensor_copy(out=cactb[:], in_=cact[:]), reads=["cact"], writes=["cactb"])

        xin = [AV(0, 128, [4, D], F32), AV(16384, 128, [4, D], F32)]
        for tc in range(NTC):
            xv = xin[tc % 2]
            S.op("sp", (lambda e, xv=xv, tc=tc: e.dma_start(
                out=xv.ap, in_=x_d[tc * 512:(tc + 1) * 512, :].rearrange("(t p) d -> p t d", p=128))),
                writes=xv.keys(), dma=True)
            for c in range(8):
                b = nbank("G")
                for t in range(4):
                    S.op("pe", (lambda e, b=b, xv=xv, t=t, c=c: e.transpose(
                        banks[b][:, t * 128:(t + 1) * 128], xv.ap[:, t, c * 128:(c + 1) * 128], consts[:, O_IDF:O_IDF + 128])),
                        reads=xv.keys() + ["consts"], writes=[bk(b)])
                copy_op(evac_eng(), xT[:, c, tcs(tc)], banks[b][:], [bk(b)], [kx(c, tc)])

        def mod_steps(l, grp=None):
            lp = l % 2
            modT, avec = modTs[lp], avecs[lp]
            kM, kA = ("modT", lp), ("avec", lp)
            pm = 7
            steps = []
            for j in range(12):
                def sj(j=j):
                    s = wslot(grp)
                    wv = wsl[s][:, :].rearrange("p (k n) -> p k n", k=8)
                    wload(s, wv, w_ada_d[l][:, j * 512:(j + 1) * 512].rearrange("(kc p) n -> p kc n", p=128))
                    for oc in range(4):
                        col = j * 4 + oc
                        for kc in range(8):
                            S.op("pe", (lambda e, oc=oc, kc=kc, col=col: e.matmul(
                                banks[pm][:, col:col + 1], lhsT=wv[:, kc, oc * 128:(oc + 1) * 128], rhs=cactb[:, kc:kc + 1],
                                start=(kc == 0), stop=(kc == 7), skip_group_check=True)),
                                reads=[("ws", s), "cactb"], writes=[bk(pm)])
                steps.append(sj)

            def fin():
                S.op("dve", (lambda e: e.tensor_tensor(out=modT[:], in0=banks[pm][:, 0:48],
                                                       in1=small[:, O_BADA + l * 48:O_BADA + (l + 1) * 48], op=ALU.add)),
                     reads=[bk(pm), "small"], writes=[kM])
                S.op("dve", (lambda e: e.scalar_tensor_tensor(out=avec[:, 0:8], in0=modT[:, 8:16], scalar=1.0,
                                                              in1=small[:, O_N1W + l * 8:O_N1W + (l + 1) * 8],
                                                              op0=ALU.add, op1=ALU.mult)),
                     reads=[kM, "small"], writes=[kA])
                S.op("dve", (lambda e: e.scalar_tensor_tensor(out=avec[:, 8:16], in0=modT[:, 32:40], scalar=1.0,
                                                              in1=small[:, O_N2W + l * 8:O_N2W + (l + 1) * 8],
                                                              op0=ALU.add, op1=ALU.mult)),
                     reads=[kM, "small"], writes=[kA])
                S.op("dve", lambda e: e.tensor_scalar(out=avec[:, 16:24], in0=modT[:, 16:24], scalar1=0.5, scalar2=None, op0=ALU.mult),
                     reads=[kM], writes=[kA])
            steps.append(fin)
            return steps

        def norm_tc_steps(a_off, sh_off, lp, base=0):
            modT, avec = modTs[lp], avecs[lp]
            sqbs = [AV(base, 128, [8, 512], BF16), AV(base + 8192, 128, [8, 512], BF16)]
            rst = [AV(base + 16384, 128, [512], F32), AV(base + 18432, 128, [512], F32)]
            tmp = [AV(base + 20480, 128, [512], F32), AV(base + 22528, 128, [512], F32), AV(base + 24576, 128, [512], F32)]
            st_ = {"ti": 0}

            def p1(tc):
                sqb = sqbs[tc % 2]
                for c in range(8):
                    if c % 2 == 0:
                        S.op("act", (lambda e, c=c: e.activation(out=sqb.ap[:, c, :], in_=xT[:, c, tcs(tc)], func=AF.Square)),
                             reads=[kx(c, tc)], writes=sqb.kidx(c))
                    else:
                        S.op("dve", (lambda e, c=c: e.tensor_tensor(out=sqb.ap[:, c, :], in0=xT[:, c, tcs(tc)],
                                                                    in1=xT[:, c, tcs(tc)], op=ALU.mult)),
                             reads=[kx(c, tc)], writes=sqb.kidx(c))
                b = nbank("n")
                for c in range(8):
                    S.op("pe", (lambda e, c=c: e.matmul(banks[b][:], lhsT=onesb[:], rhs=sqb.ap[:, c, :],
                                                       start=(c == 0), stop=(c == 7))),
                         reads=sqb.kidx(c) + ["onesb"], writes=[bk(b)])
                r = rst[tc % 2]
                S.op("act", (lambda e: e.activation(out=r.ap, in_=banks[b][:], func=AF.Sqrt, bias=epsb[:, 0:1], scale=1.0 / D)),
                     reads=[bk(b), "epsb"], writes=r.keys())
                S.op("dve", (lambda e: e.reciprocal(out=r.ap, in_=r.ap)), reads=r.keys(), writes=r.keys())

            def p2(tc):
                r = rst[tc % 2]
                for c in range(8):
                    t_ = tmp[st_["ti"] % 3]
                    st_["ti"] += 1
                    S.op("dve", (lambda e, t_=t_, c=c: e.tensor_tensor(out=t_.ap, in0=xT[:, c, tcs(tc)], in1=r.ap, op=ALU.mult)),
                         reads=[kx(c, tc)] + r.keys(), writes=t_.keys())
                    S.op("act", (lambda e, t_=t_, c=c: e.activation(
                        out=hT[:, c, tcs(tc)], in_=t_.ap, func=AF.Identity,
                        scale=avec[:, a_off + c:a_off + c + 1], bias=modT[:, sh_off + c:sh_off + c + 1])),
                        reads=t_.keys() + [("avec", lp), ("modT", lp)], writes=[kh(c, tc)])

            return [lambda: p1(0),
                    lambda: (p1(1), p2(0)),
                    lambda: (p1(2), p2(1)),
                    lambda: (p1(3), p2(2), p2(3))]

        epsb = sb("epsb", [128, 1], F32)
        S.op("dve", lambda e: e.memset(epsb[:], EPS), writes=["epsb"])

        yattnT = AV(0, 128, [4, S_], BF16)
        ypoolT = AV(16384, 128, [2, S_], BF16)
        yconvT = AV(24576, 128, [2, S_], BF16)
        A1 = 32768
        KaT = AV(A1, 72, [4, S_], BF16)
        Va = AV(A1 + 16384, 128, [NT, 4, 65], BF16)
        QaTs = [AV(A1 + 24736, 72, [4, 512], BF16), AV(65536, 72, [4, 512], BF16)]
        o_ = A1 + 24736 + 4096
        qk_tok = [AV(o_, 128, [8, 72], BF16), AV(o_ + 1152, 128, [8, 72], BF16)]
        o_ += 2304
        mb_tok = [AV(o_, 128, [4, 72], BF16), AV(o_ + 576, 128, [4, 72], BF16)]
        o_ += 1152
        qn_s = [AV(16384, 128, [8, 64], F32), AV(18432, 128, [8, 64], F32)]
        ytok = AV(20480, 128, [4, 256], BF16)
        pT = [AV(22528 + i * 1024, 128, [512], BF16) for i in range(4)]
        sq_s = AV(26624, 128, [512], BF16)
        cmp_s = AV(27648, 128, [4, 8, 8], F32)
        sm_par = []
        for i_ in range(2):
            o2 = 28672 + i_ * 1920
            sm_par.append(dict(
                ssq8=AV(o2, 128, [8], F32), vv8=AV(o2 + 32, 128, [8], F32), yy8=AV(o2 + 64, 128, [8], F32),
                aa8=AV(o2 + 96, 128, [8], F32), gsb=AV(o2 + 128, 128, [4, 8], F32), rank_s=AV(o2 + 256, 128, [4, 8], F32),
                rt=[AV(o2 + 384 + k_ * 256, 128, [8, 8], F32) for k_ in range(4)],
                qr=AV(o2 + 1408, 128, [8, 16], F32)))
        rec_s = AV(65056, 128, [4], F32)

        def merge_steps(lists):
            idx = [0] * len(lists)
            tot = [max(1, len(x)) for x in lists]
            while True:
                best = None
                for i, x in enumerate(lists):
                    if idx[i] < len(x):
                        frac = idx[i] / tot[i]
                        if best is None or frac < best[0]:
                            best = (frac, i)
                if best is None:
                    break
                i = best[1]
                lists[i][idx[i]]()
                idx[i] += 1

        def tile_steps(l, hg, qc, j, wqk, wv, sqk, sv):
            t = qc * 4 + j
            blk = t // 2
            QaT = QaTs[qc % 2]
            qn = qn_s[t % 2]
            qt = qk_tok[t % 2]
            mb = mb_tok[t % 2]
            sp_ = sm_par[t % 2]
            ssq8, vv8, yy8, aa8, gsb, rank_s, rt, qr = (sp_["ssq8"], sp_["vv8"], sp_["yy8"], sp_["aa8"], sp_["gsb"],
                                                        sp_["rank_s"], sp_["rt"], sp_["qr"])
            st_ = {}
            steps = []

            def s1():
                bq = st_["bq"] = t % 2
                S.op("pool", (lambda e: e.memset(qt.ap[:, 4:8, 64:72], 0.0)), writes=qt.keys())
                S.op("pool", (lambda e: e.memset(qt.ap[:, 4:8, 64 + blk:65 + blk], 1.0)), writes=qt.keys())
                S.op("pool", (lambda e: e.memset(mb.ap[:, :, 64:72], NEGB)), writes=mb.keys())
                S.op("pool", (lambda e: e.memset(mb.ap[:, :, 64:65 + blk], 0.0)), writes=mb.keys())
                for kc in range(8):
                    S.op("pe", (lambda e, kc=kc: e.matmul(
                        banks[bq][:], lhsT=hT[:, kc, t * 128:(t + 1) * 128], rhs=wqk[:, kc, :],
                        start=(kc == 0), stop=(kc == 7))),
                        reads=[kh(kc, t // 4), ("ws", sqk)], writes=[bk(bq)])
                bv = nbank("g")
                for kc in range(8):
                    S.op("pe", (lambda e, kc=kc: e.matmul(
                        banks[bv][:, 0:256], lhsT=hT[:, kc, t * 128:(t + 1) * 128], rhs=wv[:, kc, :],
                        start=(kc == 0), stop=(kc == 7))),
                        reads=[kh(kc, t // 4), ("ws", sv)], writes=[bk(bv)])
                S.op("act", (lambda e: e.activation(out=sq_s.ap, in_=banks[bq][:], func=AF.Square)),
                     reads=[bk(bq)], writes=sq_s.keys())
                S.op("act", (lambda e: e.copy(
                    out=Va.ap[:, t, :, 0:64], in_=banks[bv][:, 0:256].rearrange("p (h d) -> p h d", h=4))),
                    reads=[bk(bv)], writes=Va.kidx(t))
            steps.append(s1)

            def s2():
                S.op("dve", lambda e: e.tensor_reduce(out=ssq8.ap, in_=sq_s.ap.rearrange("p (h d) -> p h d", h=8),
                                                      axis=AX.X, op=ALU.add),
                     reads=sq_s.keys(), writes=ssq8.keys())
                S.op("dve", lambda e: e.tensor_scalar(out=vv8.ap, in0=ssq8.ap, scalar1=1.0 / 64, scalar2=EPS, op0=ALU.mult, op1=ALU.add),
                     reads=ssq8.keys(), writes=vv8.keys())
                S.op("dve", lambda e: e.tensor_scalar(out=yy8.ap.bitcast(I32), in0=vv8.ap.bitcast(I32), scalar1=-0.5, scalar2=1597463007.0,
                                                      op0=ALU.mult, op1=ALU.add),
                     reads=vv8.keys(), writes=yy8.keys())
                for _ in range(2):
                    S.op("dve", lambda e: e.tensor_tensor(out=aa8.ap, in0=yy8.ap, in1=yy8.ap, op=ALU.mult),
                         reads=yy8.keys(), writes=aa8.keys())
                    S.op("dve", lambda e: e.scalar_tensor_tensor(out=aa8.ap, in0=aa8.ap, scalar=-0.5, in1=vv8.ap, op0=ALU.mult, op1=ALU.mult),
                         reads=aa8.keys() + vv8.keys(), writes=aa8.keys())
                    S.op("dve", lambda e: e.scalar_tensor_tensor(out=yy8.ap, in0=aa8.ap, scalar=1.5, in1=yy8.ap, op0=ALU.add, op1=ALU.mult),
                         reads=yy8.keys() + aa8.keys(), writes=yy8.keys())
            steps.append(s2)

            def s3():
                bq = st_["bq"]
                wq = small[:, O_QKW + l * 128:O_QKW + (l + 1) * 128].rearrange("p (a d) -> p a d", a=2)
                S.op("dve", (lambda e: e.tensor_tensor(
                    out=qn.ap, in0=banks[bq][:].rearrange("p (h d) -> p h d", h=8),
                    in1=yy8.ap.unsqueeze(2).to_broadcast([128, 8, 64]), op=ALU.mult)),
                    reads=[bk(bq)] + yy8.keys(), writes=qn.keys())
                S.op("dve", (lambda e: e.tensor_tensor(
                    out=qr.ap.rearrange("p (a h) d -> p a h d", a=2), in0=qn.ap[:, :, 0:16].rearrange("p (a h) d -> p a h d", a=2),
                    in1=wq[:, :, 0:16].unsqueeze(2).to_broadcast([128, 2, 4, 16]), op=ALU.mult)),
                    reads=qn.keys() + ["small"], writes=qr.keys())
                S.op("dve", (lambda e: e.tensor_tensor(
                    out=qt.ap[:, :, 16:64].rearrange("p (a h) d -> p a h d", a=2), in0=qn.ap[:, :, 16:64].rearrange("p (a h) d -> p a h d", a=2),
                    in1=wq[:, :, 16:64].unsqueeze(2).to_broadcast([128, 2, 4, 48]), op=ALU.mult)),
                    reads=qn.keys() + ["small"], writes=qt.keys())
            steps.append(s3)

            def s4():
                cb = cosT[:, t, :].unsqueeze(1).to_broadcast([128, 8, 8])
                sbb = sinT[:, t, :].unsqueeze(1).to_broadcast([128, 8, 8])
                x1 = qr.ap[:, :, 0:8]
                x2 = qr.ap[:, :, 8:16]
                S.op("dve", (lambda e: e.tensor_tensor(out=rt[0].ap, in0=x1, in1=cb, op=ALU.mult)),
                     reads=qr.keys() + ["cosT"], writes=rt[0].keys())
                S.op("dve", (lambda e: e.tensor_tensor(out=rt[1].ap, in0=x2, in1=sbb, op=ALU.mult)),
                     reads=qr.keys() + ["sinT"], writes=rt[1].keys())
                S.op("dve", (lambda e: e.tensor_tensor(out=rt[2].ap, in0=x2, in1=cb, op=ALU.mult)),
                     reads=qr.keys() + ["cosT"], writes=rt[2].keys())
                S.op("dve", (lambda e: e.tensor_tensor(out=rt[3].ap, in0=x1, in1=sbb, op=ALU.mult)),
                     reads=qr.keys() + ["sinT"], writes=rt[3].keys())
                S.op("dve", (lambda e: e.tensor_tensor(out=qt.ap[:, :, 0:8], in0=rt[0].ap, in1=rt[1].ap, op=ALU.subtract)),
                     reads=rt[0].keys() + rt[1].keys(), writes=qt.keys())
                S.op("dve", (lambda e: e.tensor_tensor(out=qt.ap[:, :, 8:16], in0=rt[2].ap, in1=rt[3].ap, op=ALU.add)),
                     reads=rt[2].keys() + rt[3].keys(), writes=qt.keys())
            steps.append(s4)

            def s5():
                btr = nbank("g")
                for h in range(4):
                    S.op("pe", (lambda e, h=h: e.transpose(
                        banks_b[btr][0:72, h * 128:(h + 1) * 128], qt.ap[:, 4 + h, 0:72], identb[:])),
                        reads=qt.keys() + ["identb"], writes=[bk(btr)])
                    S.op("pe", (lambda e, h=h: e.transpose(
                        banks_b[btr][0:64, 512 + h * 128:512 + (h + 1) * 128], qt.ap[:, h, 0:64], identb[:])),
                        reads=qt.keys() + ["identb"], writes=[bk(btr)])
                S.op("act", (lambda e: e.copy(
                    out=KaT.ap[:, :, t * 128:(t + 1) * 128], in_=banks_b[btr][0:72, 0:512].rearrange("p (h n) -> p h n", h=4))),
                    reads=[bk(btr)], writes=[k_ for h in range(4) for k_ in KaT.krange((h,), t * 128, 128)])
                S.op("dve", (lambda e: e.tensor_copy(
                    out=QaT.ap[0:64, :, j * 128:(j + 1) * 128], in_=banks_b[btr][0:64, 512:1024].rearrange("p (h n) -> p h n", h=4))),
                    reads=[bk(btr)], writes=QaT.keys())
                if t % 2 == 1 and blk < 7:
                    S.op("dve", (lambda e: e.tensor_reduce(
                        out=kmf[:], in_=KaT.ap[0:64, :, blk * 256:(blk + 1) * 256], axis=AX.X, op=ALU.add)),
                        reads=[k_ for h in range(4) for k_ in KaT.krange((h,), blk * 256, 256)], writes=["kmf"])
                    S.op("dve", (lambda e: e.tensor_scalar(
                        out=kmT[:, :, blk:blk + 1], in0=kmf[:].unsqueeze(2), scalar1=1.0 / 256, scalar2=None, op0=ALU.mult)),
                        reads=["kmf"], writes=["kmT"])
            steps.append(s5)

            def s6():
                if blk >= 4:
                    bg = nbank("g")
                    for h in range(4):
                        S.op("pe", (lambda e, h=h: e.matmul(
                            banks[bg][:, h * 8:h * 8 + blk], lhsT=QaT.ap[0:64, h, j * 128:(j + 1) * 128],
                            rhs=kmT[:, h, 0:blk], start=True, stop=True, skip_group_check=True)),
                            reads=QaT.keys() + ["kmT"], writes=[bk(bg)])
                    S.op("act", (lambda e: e.copy(
                        out=gsb.ap[:, :, 0:blk], in_=banks[bg][:, 0:32].rearrange("p (h n) -> p h n", h=4)[:, :, 0:blk])),
                        reads=[bk(bg)], writes=gsb.keys())
                    S.op("dve", (lambda e: e.tensor_tensor(
                        out=cmp_s.ap[:, :, 0:blk, 0:blk],
                        in0=gsb.ap[:, :, 0:blk].unsqueeze(2).to_broadcast([128, 4, blk, blk]),
                        in1=gsb.ap[:, :, 0:blk].unsqueeze(3).to_broadcast([128, 4, blk, blk]), op=ALU.is_gt)),
                        reads=gsb.keys(), writes=cmp_s.keys())
                    S.op("dve", (lambda e: e.tensor_reduce(
                        out=rank_s.ap[:, :, 0:blk], in_=cmp_s.ap[:, :, 0:blk, 0:blk], axis=AX.X, op=ALU.add)),
                        reads=cmp_s.keys(), writes=rank_s.keys())
                    S.op("dve", (lambda e: e.tensor_scalar(
                        out=mb.ap[:, :, 64:64 + blk], in0=rank_s.ap[:, :, 0:blk], scalar1=2.5, scalar2=NEGB,
                        op0=ALU.is_gt, op1=ALU.mult)),
                        reads=rank_s.keys(), writes=mb.keys())
                bm = nbank("g")
                for h in range(4):
                    S.op("pe", (lambda e, h=h: e.transpose(
                        banks_b[bm][0:72, h * 128:(h + 1) * 128], mb.ap[:, h, 0:72], identb[:])),
                        reads=mb.keys() + ["identb"], writes=[bk(bm)])
                S.op("act", (lambda e: e.copy(
                    out=QaT.ap[64:72, :, j * 128:(j + 1) * 128], in_=banks_b[bm][64:72, 0:512].rearrange("p (h n) -> p h n", h=4))),
                    reads=[bk(bm)], writes=QaT.keys())
            steps.append(s6)
            return steps

        def attn_steps(hg, qc):
            QaT = QaTs[qc % 2]
            nk = 4 * (qc + 1)
            items = [(h, kt) for h in range(4) for kt in range(nk)]
            st_ = {}

            def stage1(i):
                h, kt = items[i]
                j0 = max(0, kt - 4 * qc)
                bs = nbank("s")
                p_ = pT[i % 4]
                S.op("pe", (lambda e: e.matmul(
                    banks[bs][:, j0 * 128:512], lhsT=KaT.ap[0:72, h, kt * 128:(kt + 1) * 128],
                    rhs=QaT.ap[0:72, h, j0 * 128:512], start=True, stop=True)),
                    reads=KaT.krange((h,), kt * 128, 128) + QaT.keys(), writes=[bk(bs)])
                S.op("act", (lambda e: e.activation(
                    out=p_.ap[:, j0 * 128:512], in_=banks[bs][:, j0 * 128:512], func=AF.Exp, scale=0.125)),
                    reads=[bk(bs)], writes=p_.keys())
                if kt >= 4 * qc:
                    S.op("pool", (lambda e: e.tensor_tensor(
                        out=p_.ap[:, j0 * 128:(j0 + 1) * 128], in0=p_.ap[:, j0 * 128:(j0 + 1) * 128], in1=trib[:], op=ALU.mult)),
                        reads=p_.keys() + ["trib"], writes=p_.keys())

            def stage2(i):
                h, kt = items[i]
                j0 = max(0, kt - 4 * qc)
                p_ = pT[i % 4]
                if kt == 0:
                    st_[h] = nbank("a")
                ba = st_[h]
                accv = banks[ba][:].rearrange("p (j n) -> p j n", j=4)
                for jj in range(j0, 4):
                    S.op("pe", (lambda e, jj=jj: e.matmul(
                        accv[:, jj, 0:65], lhsT=p_.ap[:, jj * 128:(jj + 1) * 128], rhs=Va.ap[:, kt, h, :],
                        start=(kt == 0 and jj == 0), stop=(kt == 4 * qc + jj), skip_group_check=True)),
                        reads=p_.keys() + Va.kidx(kt), writes=[bk(ba)])
                if kt == nk - 1:
                    S.op("dve", (lambda e: e.reciprocal(out=rec_s.ap, in_=accv[:, :, 64])),
                         reads=[bk(ba)], writes=rec_s.keys())
                    S.op("dve", (lambda e: e.tensor_tensor(
                        out=ytok.ap[:, :, h * 64:(h + 1) * 64], in0=accv[:, :, 0:64],
                        in1=rec_s.ap.unsqueeze(2).to_broadcast([128, 4, 64]), op=ALU.mult)),
                        reads=[bk(ba)] + rec_s.keys(), writes=ytok.keys())

            n = len(items)
            LA = int(_os.environ.get('LA', '3'))
            steps = [(lambda: [stage1(i_) for i_ in range(LA)])]
            for i in range(n):
                def sk(i=i):
                    stage2(i)
                    if i + LA < n:
                        stage1(i + LA)
                steps.append(sk)

            def sy():
                by = nbank("g")
                for jj in range(4):
                    for cc in range(2):
                        S.op("pe", (lambda e, jj=jj, cc=cc: e.transpose(
                            banks_b[by][:, (cc * 4 + jj) * 128:(cc * 4 + jj + 1) * 128], ytok.ap[:, jj, cc * 128:(cc + 1) * 128], identb[:])),
                            reads=ytok.keys() + ["identb"], writes=[bk(by)])
                for cc in range(2):
                    copy_op("act" if cc == 0 else "dve", yattnT.ap[:, hg * 2 + cc, qc * 512:(qc + 1) * 512],
                            banks_b[by][:, cc * 512:(cc + 1) * 512], [bk(by)], yattnT.krange((hg * 2 + cc,), qc * 512, 512))
            steps.append(sy)
            return steps

        def attention_phase(l):
            for hg in range(2):
                sqk = wslot()
                wqk = wsl[sqk][:, :].rearrange("p (k n) -> p k n", k=8)
                wload(sqk, wqk[:, :, 0:256], w_in_d[l][:, 1024 + hg * 256:1024 + (hg + 1) * 256].rearrange("(kc p) n -> p kc n", p=128))
                wload(sqk, wqk[:, :, 256:512], w_in_d[l][:, 1536 + hg * 256:1536 + (hg + 1) * 256].rearrange("(kc p) n -> p kc n", p=128))
                sv = wslot()
                wv = wsl[sv][:, 0:2048].rearrange("p (k n) -> p k n", k=8)
                wload(sv, wv, w_in_d[l][:, 2048 + hg * 256:2048 + (hg + 1) * 256].rearrange("(kc p) n -> p kc n", p=128))
                S.op("pool", lambda e: e.memset(Va.ap[:, :, :, 64:65], 1.0), writes=Va.keys())
                for m_ in mb_tok:
                    S.op("pool", (lambda e, m_=m_: e.memset(m_.ap[:, :, 0:64], 0.0)), writes=m_.keys())

                def A(qc):
                    tl = [tile_steps(l, hg, qc, j, wqk, wv, sqk, sv) for j in range(4)]
                    out = []
                    dsk = int(_os.environ.get('DSK', '2'))
                    ns = len(tl[0])
                    for k in range(ns + 3 * dsk):
                        for i in range(4):
                            ix = k - i * dsk
                            if 0 <= ix < ns:
                                out.append(tl[i][ix])
                    return out

                merge_steps([A(0)])
                for qc in range(NTC):
                    lists = [attn_steps(hg, qc)]
                    if qc + 1 < NTC:
                        lists.append(A(qc + 1))
                    merge_steps(lists)

        PW = S_ + 16
        up_s = AV(A1, 128, [PW], F32)
        sA_s = AV(A1 + 8256, 128, [PW], F32)
        sB_s = AV(A1 + 16512, 128, [PW], F32)

        def fm_proj(wv_of_kc, b, tc):
            for kc in range(8):
                lhsT, rk = wv_of_kc(kc)
                S.op("pe", (lambda e, lhsT=lhsT, kc=kc, b=b, tc=tc: e.matmul(
                    banks[b][:], lhsT=lhsT, rhs=hT[:, kc, tcs(tc)], start=(kc == 0), stop=(kc == 7))),
                    reads=[kh(kc, tc)] + rk, writes=[bk(b)])

        def pool_phase(l):
            S.op("dve", lambda e: e.memset(poolW[:], 0.0), writes=[("poolW", g) for g in range(4)])
            for g in range(4):
                r0 = (g % 2) * 64
                S.op("pool", (lambda e, g=g, r0=r0, l=l: e.dma_start(out=poolW[r0:r0 + 64, g // 2, r0:r0 + 64], in_=pool_w_d[l, g])),
                     writes=[("poolW", g)], dma=True)
            s = wslot()
            wv = wsl[s][:, 0:2048].rearrange("p (k n) -> p k n", k=8)
            wload(s, wv, w_in_d[l][:, 0:256].rearrange("(kc p) n -> p kc n", p=128))
            for v_ in (up_s, sA_s, sB_s):
                S.op("dve", (lambda e, v_=v_: e.memset(v_.ap[:, 0:16], 0.0)), writes=v_.keys(0, 16))
            for cc in range(2):
                for tc in range(NTC):
                    b = nbank("G")
                    fm_proj(lambda kc, cc=cc: (wv[:, kc, cc * 128:(cc + 1) * 128], [("ws", s)]), b, tc)
                    copy_op(evac_eng(), up_s.ap[:, 16 + tc * 512:16 + (tc + 1) * 512], banks[b][:], [bk(b)],
                            up_s.keys(16 + tc * 512, 512))
                n_lv = 2 if cc == 0 else 4
                cur = up_s
                nxt = [sA_s, sB_s]
                srcs = {}
                for lv in range(1, n_lv + 1):
                    sh = 1 << (lv - 1)
                    dst = nxt[(lv - 1) % 2]
                    p0 = 64 if lv == n_lv else 0
                    S.op("dve", (lambda e, dst=dst, cur=cur, sh=sh, p0=p0: e.tensor_tensor(
                        out=dst.ap[p0:128, 16:PW], in0=cur.ap[p0:128, 16:PW], in1=cur.ap[p0:128, 16 - sh:PW - sh], op=ALU.add)),
                        reads=cur.keys(), writes=dst.keys(16, S_))
                    srcs[lv] = dst
                    cur = dst
                lo_src = srcs[n_lv - 1]
                hi_src = srcs[n_lv]
                pooledT = yconvT
                for (p0, p1, src) in ((0, 64, lo_src), (64, 128, hi_src)):
                    S.op("dve", (lambda e, p0=p0, p1=p1, src=src, cc=cc: e.tensor_tensor(
                        out=src.ap[p0:p1, 16:32], in0=src.ap[p0:p1, 16:32],
                        in1=consts[p0:p1, O_CORR + cc * 16:O_CORR + (cc + 1) * 16], op=ALU.mult)),
                        reads=src.keys() + ["consts"], writes=src.keys(16, 16))
                    S.op("dve", (lambda e, p0=p0, p1=p1, src=src, cc=cc: e.tensor_tensor(
                        out=pooledT.ap[p0:p1, cc, 0:16], in0=src.ap[p0:p1, 16:32], in1=up_s.ap[p0:p1, 16:32], op=ALU.subtract)),
                        reads=src.keys() + up_s.keys(), writes=pooledT.krange((cc,), 0, 16))
                    S.op("dve", (lambda e, p0=p0, p1=p1, src=src, cc=cc: e.scalar_tensor_tensor(
                        out=pooledT.ap[p0:p1, cc, 16:S_], in0=src.ap[p0:p1, 32:PW], scalar=consts[p0:p1, O_INVW + cc:O_INVW + cc + 1],
                        in1=up_s.ap[p0:p1, 32:PW], op0=ALU.mult, op1=ALU.subtract)),
                        reads=src.keys() + up_s.keys() + ["consts"], writes=pooledT.kidx(cc))
                for tc in range(NTC):
                    b = nbank("G")
                    S.op("pe", (lambda e, b=b, cc=cc, tc=tc: e.matmul(banks[b][:], lhsT=poolW[:, cc, :], rhs=pooledT.ap[:, cc, tcs(tc)],
                                                                      start=True, stop=True)),
                         reads=[("poolW", 2 * cc), ("poolW", 2 * cc + 1)] + pooledT.krange((cc,), tc * 512, 512), writes=[bk(b)])
                    S.op("act", (lambda e, b=b, cc=cc, tc=tc, l=l: e.activation(
                        out=ypoolT.ap[:, cc, tcs(tc)], in_=banks[b][:], func=AF.Identity,
                        scale=small[:, O_PSC + l * 2 + cc:O_PSC + l * 2 + cc + 1])),
                        reads=[bk(b), "small"], writes=ypoolT.krange((cc,), tc * 512, 512))

        def conv_phase(l):
            cu_s, cc_s, ac_s = up_s, sA_s, sB_s
            for cc in range(2):
                s = wslot()
                wv = wsl[s][:, 0:3072].rearrange("p (k i n) -> p k i n", k=8, i=3)
                for i in range(3):
                    c0 = 256 + i * 256 + cc * 128
                    wload(s, wv[:, :, i, :], w_in_d[l][:, c0:c0 + 128].rearrange("(kc p) n -> p kc n", p=128))
                S.op("dve", lambda e: e.memset(cu_s.ap[:, 0:16], 0.0), writes=cu_s.keys(0, 16))
                for tc in range(NTC):
                    b = nbank("G")
                    fm_proj(lambda kc: (wv[:, kc, 0, :], [("ws", s)]), b, tc)
                    copy_op("act", cu_s.ap[:, 16 + tc * 512:16 + (tc + 1) * 512], banks[b][:], [bk(b)], cu_s.keys(16 + tc * 512, 512))
                    b2 = nbank("G")
                    fm_proj(lambda kc: (wv[:, kc, 2, :], [("ws", s)]), b2, tc)
                    S.op("dve", (lambda e, b2=b2, tc=tc: e.tensor_tensor(
                        out=cu_s.ap[:, 16 + tc * 512:16 + (tc + 1) * 512], in0=banks[b2][:],
                        in1=cu_s.ap[:, 16 + tc * 512:16 + (tc + 1) * 512], op=ALU.mult)),
                        reads=[bk(b2)] + cu_s.keys(16 + tc * 512, 512), writes=cu_s.keys(16 + tc * 512, 512))
                cw = lambda k, cc=cc, l=l: small[:, O_CW + l * 6 + k * 2 + cc:O_CW + l * 6 + k * 2 + cc + 1]
                S.op("dve", (lambda e, cw=cw: e.tensor_scalar(out=ac_s.ap[:, 16:PW], in0=cu_s.ap[:, 16:PW], scalar1=cw(2), scalar2=None, op0=ALU.mult)),
                     reads=cu_s.keys() + ["small"], writes=ac_s.keys(16, S_))
                S.op("dve", (lambda e, cw=cw: e.scalar_tensor_tensor(out=ac_s.ap[:, 16:PW], in0=cu_s.ap[:, 15:PW - 1], scalar=cw(1),
                                                                     in1=ac_s.ap[:, 16:PW], op0=ALU.mult, op1=ALU.add)),
                     reads=cu_s.keys() + ac_s.keys() + ["small"], writes=ac_s.keys(16, S_))
                S.op("dve", (lambda e, cw=cw: e.scalar_tensor_tensor(out=ac_s.ap[:, 16:PW], in0=cu_s.ap[:, 14:PW - 2], scalar=cw(0),
                                                                     in1=ac_s.ap[:, 16:PW], op0=ALU.mult, op1=ALU.add)),
                     reads=cu_s.keys() + ac_s.keys() + ["small"], writes=ac_s.keys(16, S_))
                for tc in range(NTC):
                    b = nbank("G")
                    fm_proj(lambda kc: (wv[:, kc, 1, :], [("ws", s)]), b, tc)
                    S.op("dve", (lambda e, b=b, cc=cc, tc=tc: e.tensor_tensor(
                        out=yconvT.ap[:, cc, tcs(tc)], in0=banks[b][:], in1=ac_s.ap[:, 16 + tc * 512:16 + (tc + 1) * 512], op=ALU.mult)),
                        reads=[bk(b)] + ac_s.keys(16 + tc * 512, 512), writes=yconvT.krange((cc,), tc * 512, 512))

        merged = AV(A1, 128, [8, 1024], BF16)
        mo = A1 + 16384
        s_t = [AV(mo, 128, [512], F32), AV(mo + 2048, 128, [512], F32)]
        m_t = [AV(mo + 4096, 128, [512], F32), AV(mo + 6144, 128, [512], F32)]
        t_t = [AV(mo + 8192, 128, [512], F32), AV(mo + 10240, 128, [512], F32)]

        def merge_phase(l):
            lp = cur["lp"]
            avec = avecs[lp]
            ysrc = [(ypoolT, 2, 0), (yconvT, 2, 2), (yattnT, 4, 4)]
            pds = [p_pool_d, p_conv_d, p_attn_d]
            cnt = 0
            for hh in range(2):
                for c in range(8):
                    s = wslot()
                    wg = wsl[s][:, 0:3072].rearrange("p (k i n) -> p k i n", k=8, i=3)
                    wp = wsl[s][:, 3072:4096].rearrange("p (k n) -> p k n", k=8)
                    for i in range(3):
                        c0 = 2560 + i * 1024 + c * 128
                        wload(s, wg[:, :, i, :], w_in_d[l][:, c0:c0 + 128].rearrange("(kc p) n -> p kc n", p=128))
                    wload(s, wp[:, 0:2, :], p_pool_d[l][:, c * 128:(c + 1) * 128].rearrange("(kc p) n -> p kc n", p=128))
                    wload(s, wp[:, 2:4, :], p_conv_d[l][:, c * 128:(c + 1) * 128].rearrange("(kc p) n -> p kc n", p=128))
                    wload(s, wp[:, 4:8, :], p_attn_d[l][:, c * 128:(c + 1) * 128].rearrange("(kc p) n -> p kc n", p=128))
                    for th in range(2):
                        tc = hh * 2 + th
                        mt = m_t[cnt % 2]
                        for i in range(3):
                            bgt = nbank("G")
                            fm_proj(lambda kc, i=i: (wg[:, kc, i, :], [("ws", s)]), bgt, tc)
                            st_ = s_t[(cnt * 3 + i) % 2]
                            S.op("act", (lambda e, bgt=bgt, st_=st_: e.activation(out=st_.ap, in_=banks[bgt][:], func=AF.Tanh, scale=0.5)),
                                 reads=[bk(bgt)], writes=st_.keys())
                            ysv, nkk, koff = ysrc[i]
                            bp = nbank("G")
                            for kk in range(nkk):
                                S.op("pe", (lambda e, bp=bp, kk=kk, koff=koff, ysv=ysv, tc=tc, nkk=nkk, wp=wp: e.matmul(
                                    banks[bp][:], lhsT=wp[:, koff + kk, :], rhs=ysv.ap[:, kk, tcs(tc)], start=(kk == 0), stop=(kk == nkk - 1))),
                                    reads=[("ws", s)] + ysv.krange((kk,), tc * 512, 512), writes=[bk(bp)])
                            if i == 0:
                                S.op("dve", (lambda e, bp=bp, st_=st_, mt=mt: e.scalar_tensor_tensor(
                                    out=mt.ap, in0=st_.ap, scalar=1.0, in1=banks[bp][:], op0=ALU.add, op1=ALU.mult)),
                                    reads=[bk(bp)] + st_.keys(), writes=mt.keys())
                            else:
                                tt_ = t_t[(cnt * 3 + i) % 2]
                                S.op("dve", (lambda e, bp=bp, st_=st_, tt_=tt_: e.scalar_tensor_tensor(
                                    out=tt_.ap, in0=st_.ap, scalar=1.0, in1=banks[bp][:], op0=ALU.add, op1=ALU.mult)),
                                    reads=[bk(bp)] + st_.keys(), writes=tt_.keys())
                                if i == 1:
                                    S.op("dve", (lambda e, tt_=tt_, mt=mt: e.tensor_tensor(out=mt.ap, in0=mt.ap, in1=tt_.ap, op=ALU.add)),
                                         reads=mt.keys() + tt_.keys(), writes=mt.keys())
                                else:
                                    S.op("dve", (lambda e, tt_=tt_, mt=mt, c=c, th=th: e.tensor_tensor(
                                        out=merged.ap[:, c, th * 512:(th + 1) * 512], in0=mt.ap, in1=tt_.ap, op=ALU.add)),
                                        reads=mt.keys() + tt_.keys(), writes=merged.krange((c,), th * 512, 512))
                        cnt += 1
                for ob in range(2):
                    s = wslot()
                    wo = wsl[s][:, :].rearrange("p (k n) -> p k n", k=8)
                    wload(s, wo, w_out_d[l][:, ob * 512:(ob + 1) * 512].rearrange("(kc p) n -> p kc n", p=128))
                    for oo in range(4):
                        o = ob * 4 + oo
                        for th in range(2):
                            tc = hh * 2 + th
                            b = nbank("G")
                            for c in range(8):
                                S.op("pe", (lambda e, b=b, c=c, oo=oo, th=th, wo=wo: e.matmul(
                                    banks[b][:], lhsT=wo[:, c, oo * 128:(oo + 1) * 128], rhs=merged.ap[:, c, th * 512:(th + 1) * 512],
                                    start=(c == 0), stop=(c == 7))),
                                    reads=[("ws", s)] + merged.krange((c,), th * 512, 512), writes=[bk(b)])
                            S.op("dve", (lambda e, b=b, o=o, tc=tc: e.scalar_tensor_tensor(
                                out=xT[:, o, tcs(tc)], in0=banks[b][:], scalar=avec[:, 16 + o:17 + o], in1=xT[:, o, tcs(tc)],
                                op0=ALU.mult, op1=ALU.add)),
                                reads=[bk(b), kx(o, tc), ("avec", lp)], writes=[kx(o, tc)])

        actT = [AV(0, 128, [4, S_], BF16), AV(16384, 128, [4, S_], BF16)]
        sl_t = [AV(32768, 128, [512], F32), AV(32768 + 2048, 128, [512], F32)]

        def ffn_steps(l, pool, tail_norm=None, grp=None):
            lp = l % 2
            modT = modTs[lp]
            nfb = 6
            st_ = {"cnt": 0}
            steps = []
            for fb in range(nfb):
                nch = 4 if fb < 5 else 2
                f0 = fb * 512
                ncol = nch * 128
                W = {}

                def sload(fb=fb, nch=nch, f0=f0, ncol=ncol, W=W):
                    sg = wslot(grp)
                    wg = wsl[sg][:, 0:8 * ncol].rearrange("p (k n) -> p k n", k=8)
                    wload(sg, wg, w_gate_d[l][:, f0:f0 + ncol].rearrange("(kc p) n -> p kc n", p=128))
                    su = wslot(grp)
                    wu = wsl[su][:, 0:8 * ncol].rearrange("p (k n) -> p k n", k=8)
                    wload(su, wu, w_up_d[l][:, f0:f0 + ncol].rearrange("(kc p) n -> p kc n", p=128))
                    sd = wslot(grp)
                    wd = wsl[sd][:, 0:nch * 1024].rearrange("p (j n) -> p j n", j=nch)
                    wload(sd, wd, w_down_d[l][f0:f0 + ncol, :].rearrange("(j p) n -> p j n", p=128))
                    W.update(sg=sg, wg=wg, su=su, wu=wu, sd=sd, wd=wd)
                steps.append(sload)
                at = actT[fb % 2]
                for j in range(nch):
                    for tc in range(NTC):
                        def sgu(j=j, tc=tc, W=W, at=at):
                            wg, wu, sg, su = W["wg"], W["wu"], W["sg"], W["su"]
                            bg_ = nbank(pool)
                            fm_proj(lambda kc: (wg[:, kc, j * 128:(j + 1) * 128], [("ws", sg)]), bg_, tc)
                            bu_ = nbank(pool)
                            fm_proj(lambda kc: (wu[:, kc, j * 128:(j + 1) * 128], [("ws", su)]), bu_, tc)
                            sl = sl_t[st_["cnt"] % 2]
                            st_["cnt"] += 1
                            S.op("act", (lambda e: e.activation(out=sl.ap, in_=banks[bg_][:], func=AF.Silu)),
                                 reads=[bk(bg_)], writes=sl.keys())
                            S.op("dve", (lambda e: e.tensor_tensor(
                                out=at.ap[:, j, tcs(tc)], in0=banks[bu_][:], in1=sl.ap, op=ALU.mult)),
                                reads=[bk(bu_)] + sl.keys(), writes=at.krange((j,), tc * 512, 512))
                        steps.append(sgu)
                dn_order = [(o, tc) for o in range(8) for tc in range(NTC)] if fb < nfb - 1 else \
                           [(o, tc) for tc in range(NTC) for o in range(8)]
                for (o, tc) in dn_order:
                    if True:
                        def sdn(o=o, tc=tc, W=W, at=at, nch=nch):
                            wd, sd = W["wd"], W["sd"]
                            b = nbank(pool)
                            for j in range(nch):
                                S.op("pe", (lambda e, j=j: e.matmul(
                                    banks[b][:], lhsT=wd[:, j, o * 128:(o + 1) * 128], rhs=at.ap[:, j, tcs(tc)],
                                    start=(j == 0), stop=(j == nch - 1))),
                                    reads=[("ws", sd)] + at.krange((j,), tc * 512, 512), writes=[bk(b)])
                            S.op("dve", (lambda e: e.scalar_tensor_tensor(
                                out=xT[:, o, tcs(tc)], in0=banks[b][:], scalar=modT[:, 40 + o:41 + o], in1=xT[:, o, tcs(tc)],
                                op0=ALU.mult, op1=ALU.add)),
                                reads=[bk(b), kx(o, tc), ("modT", lp)], writes=[kx(o, tc)])
                        steps.append(sdn)
                        if fb == nfb - 1 and tail_norm is not None and o == 7 and tc >= 1:
                            steps.append(tail_norm[tc - 1])
            if tail_norm is not None:
                steps.append(tail_norm[3])
            return steps

        KSTOP = int(_os.environ.get("KSTOP", "99"))
        if KSTOP >= 1:
            merge_steps([mod_steps(0)])
        if KSTOP >= 2:
            merge_steps([norm_tc_steps(0, 0, 0)])
        for l in range(L):
            if KSTOP < 3:
                break
            cur["lp"] = l % 2
            attention_phase(l)
            if KSTOP >= 4:
                pool_phase(l)
            if KSTOP >= 5:
                conv_phase(l)
            if KSTOP >= 6:
                merge_phase(l)
            if KSTOP >= 8:
                n2 = norm_tc_steps(8, 24, l % 2, base=40960)
                n1 = None
                if l + 1 < L:
                    n1 = norm_tc_steps(0, 0, (l + 1) % 2, base=40960)
                if n1 is not None and _os.environ.get("TAILN", "1") == "1":
                    fs = ffn_steps(l, "F", tail_norm=n1, grp=("ffn" if l + 1 < L else None))
                    n1 = None
                else:
                    fs = ffn_steps(l, "F", grp=("ffn" if l + 1 < L else None))
                if _os.environ.get("HEADN", "1") == "1":
                    head = [n2[0], n2[1], fs[0], fs[1], n2[2], fs[2], n2[3]]
                    rest = fs[3:]
                else:
                    head = n2
                    rest = fs
                merge_steps([head])
                if l + 1 < L:
                    if _os.environ.get("MODM", "1") == "1":
                        ncut = (len(rest) * 3) // 5
                        merge_steps([rest[:ncut], mod_steps(l + 1, grp="mod")])
                        merge_steps([rest[ncut:]])
                    else:
                        merge_steps([rest])
                        merge_steps([mod_steps(l + 1)])
                else:
                    merge_steps([rest])
                if n1 is not None:
                    merge_steps([n1])
            elif KSTOP >= 7:
                merge_steps([norm_tc_steps(8, 24, l % 2)])

        xo = [AV(0, 128, [D], F32), AV(4096, 128, [D], F32)]
        for t in range(NT):
            xv = xo[t % 2]
            for half in range(2):
                b = nbank("G")
                for cq in range(4):
                    c = half * 4 + cq
                    S.op("pe", (lambda e, b=b, cq=cq, c=c, t=t: e.transpose(
                        banks[b][:, cq * 128:(cq + 1) * 128], xT[:, c, t * 128:(t + 1) * 128], consts[:, O_IDF:O_IDF + 128])),
                        reads=[kx(c, t // 4), "consts"], writes=[bk(b)])
                copy_op(evac_eng(), xv.ap[:, half * 512:(half + 1) * 512], banks[b][:], [bk(b)], xv.keys(half * 512, 512))
            S.op("sp", (lambda e, xv=xv, t=t: e.dma_start(out=out_d[t * 128:(t + 1) * 128, :], in_=xv.ap)),
                 reads=xv.keys(), dma=True)
        S.emit(st)
    return nc


def _consts():
    c = np.zeros((128, NCF), np.float32)
    c[:, O_IDF:O_IDF + 128] = np.eye(128, dtype=np.float32)
    kk = np.arange(128)[:, None]
    qq = np.arange(128)[None, :]
    c[:, O_TRI:O_TRI + 128] = (kk <= qq).astype(np.float32)
    c[:, O_FREQ:O_FREQ + 8] = (500000.0 ** (-np.arange(0, 16, 2, dtype=np.float32) / 16.0)).astype(np.float32)[None, :]
    for p in range(128):
        for cc in range(2):
            w = POOL_W[cc * 2 + (1 if p >= 64 else 0)]
            c[p, O_INVW + cc] = 1.0 / w
            for t in range(16):
                c[p, O_CORR + cc * 16 + t] = 1.0 / min(t + 1, w)
    return c


def _small(b, c, norm1_w, norm2_w, b_ada, pool_scale, conv_w, q_norm_w, k_norm_w, l0, L):
    s = np.zeros((128, NS), np.float32)
    for l in range(L):
        gl = l0 + l
        s[:, O_N1W + l * 8:O_N1W + (l + 1) * 8] = norm1_w[gl].reshape(8, 128).T
        s[:, O_N2W + l * 8:O_N2W + (l + 1) * 8] = norm2_w[gl].reshape(8, 128).T
        s[:, O_BADA + l * 48:O_BADA + (l + 1) * 48] = b_ada[gl].reshape(48, 128).T
        s[:, O_PSC + l * 2:O_PSC + (l + 1) * 2] = pool_scale[gl].reshape(2, 128).T
        s[:, O_CW + l * 6:O_CW + (l + 1) * 6] = conv_w[gl].reshape(3, 2, 128).transpose(2, 0, 1).reshape(128, 6)
        s[:, O_QKW + l * 128:O_QKW + l * 128 + 64] = q_norm_w[gl][None, :]
        s[:, O_QKW + l * 128 + 64:O_QKW + (l + 1) * 128] = k_norm_w[gl][None, :]
    s[:, O_CT:O_CT + 8] = c[b].reshape(8, 128).T
    return s


_NC_CACHE = {}


def _get_nc(L):
    if L not in _NC_CACHE:
        _NC_CACHE[L] = build(L)
    return _NC_CACHE[L]


def _run(x, c, positions, P, l0, L):
    nc = build(L)
    consts = _consts()
    in_maps = []
    wsl_ = slice(l0, l0 + L)
    for b in range(8):
        m = {
            "x": np.ascontiguousarray(x[b]),
            "small": _small(b, c, P["norm1_w"], P["norm2_w"], P["b_ada"], P["pool_scale"], P["conv_w"],
                            P["q_norm_w"], P["k_norm_w"], l0, L),
            "consts": consts,
            "pos": np.ascontiguousarray(positions[b].reshape(NT, 128).T.astype(np.int32)),
        }
        for k in ("w_ada", "w_in", "pool_w", "p_pool", "p_conv", "p_attn", "w_out", "w_gate", "w_up", "w_down"):
            m[k] = np.ascontiguousarray(P[k][wsl_])
        in_maps.append(m)
    res = run_bass_kernel_spmd(nc, in_maps, core_ids=list(range(8)))
    return np.stack([np.asarray(r["out"]) for r in res.results], axis=0)


def kernel(x, c, positions, norm1_w, norm2_w, w_ada, b_ada, w_in, pool_w, pool_scale, conv_w,
           q_norm_w, k_norm_w, p_pool, p_conv, p_attn, w_out, w_gate, w_up, w_down):
    P = dict(norm1_w=np.asarray(norm1_w), norm2_w=np.asarray(norm2_w), w_ada=np.asarray(w_ada), b_ada=np.asarray(b_ada),
             w_in=np.asarray(w_in), pool_w=np.asarray(pool_w), pool_scale=np.asarray(pool_scale), conv_w=np.asarray(conv_w),
             q_norm_w=np.asarray(q_norm_w), k_norm_w=np.asarray(k_norm_w), p_pool=np.asarray(p_pool), p_conv=np.asarray(p_conv),
             p_attn=np.asarray(p_attn), w_out=np.asarray(w_out), w_gate=np.asarray(w_gate), w_up=np.asarray(w_up),
             w_down=np.asarray(w_down))
    x = np.asarray(x, dtype=np.float32)
    c = np.asarray(c)
    positions = np.asarray(positions)
    if FUSED:
        out = _run(x, c, positions, P, 0, DEPTH)
    else:
        out = x
        for l in range(DEPTH):
            out = _run(out, c, positions, P, l, 1)
    return out.astype(np.float32)
```

```python
import math
import os as _os
from contextlib import ExitStack

import numpy as np
import concourse.bass as bass
import concourse.mybir as mybir
from concourse.bass_utils import run_bass_kernel_spmd

F32 = mybir.dt.float32
BF16 = mybir.dt.bfloat16
I32 = mybir.dt.int32
ALU = mybir.AluOpType
AF = mybir.ActivationFunctionType
AX = mybir.AxisListType

FUSED = True
DEPTH = 4
D = 1024
S_ = 2048
NT = 16
NTC = 4
DFF = 2816
EPS = 1e-6
NEGB = -30000.0
O_N1W, O_N2W, O_BADA, O_PSC, O_CW, O_QKW, O_CT, NS = 0, 32, 64, 256, 264, 288, 800, 808
O_IDF, O_TRI, O_FREQ, O_INVW, O_CORR, NCF = 0, 128, 256, 264, 266, 298
POOL_W = (2, 4, 8, 16)

ENGS = ["pe", "act", "dve", "pool", "sp"]
SELFSYNC = _os.environ.get("SELFSYNC", "drain")


class Op:
    __slots__ = ("eng", "fn", "deps", "is_dma", "sig", "has_dep", "name", "pos", "selfdep")

    def __init__(self, eng, fn, is_dma, name=None):
        self.eng = eng
        self.fn = fn
        self.deps = []
        self.selfdep = False
        self.is_dma = is_dma
        self.sig = None
        self.has_dep = False
        self.name = name


class Sched:
    def __init__(self, nc, n_dma_sems=8, maxv=12000):
        self.nc = nc
        self.ops = {e: [] for e in ENGS}
        self.last_writer = {}
        self.readers = {}
        self.n_dma_sems = n_dma_sems
        self.maxv = maxv
        self.cow = {}
        self.old_r = {}
        self.old_w = {}

    def new_gen(self, k):
        self.old_r[k] = self.readers.get(k, [])
        self.old_w[k] = self.cow.get(k, [])
        self.readers[k] = []
        self.cow[k] = []

    def op(self, eng, fn, reads=(), writes=(), dma=False, name=None, joins=()):
        o = Op(eng, fn, dma, name)
        deps = {}
        lw = self.last_writer
        rd = self.readers
        if eng != "pe":
            extra = [k for k in reads if isinstance(k, tuple) and k[0] == "bank"]
            if extra:
                writes = list(writes) + extra
        for k in joins:
            for r in self.old_r.get(k, ()):
                deps[id(r)] = (r, "war")
            for w in self.old_w.get(k, ()):
                if id(w) not in deps:
                    deps[id(w)] = (w, "waw")
            self.cow.setdefault(k, []).append(o)
        for k in reads:
            w = lw.get(k)
            if w is not None:
                deps[id(w)] = (w, "raw")
            for w in self.cow.get(k, ()):
                deps[id(w)] = (w, "raw")
        for k in writes:
            w = lw.get(k)
            if w is not None and id(w) not in deps:
                deps[id(w)] = (w, "waw")
            for r in rd.get(k, ()):
                if id(r) not in deps:
                    deps[id(r)] = (r, "war")
        for k in reads:
            rd.setdefault(k, []).append(o)
        for k in writes:
            lw[k] = o
            rd[k] = []
        best = {}
        npos = len(self.ops[eng])
        for d, kind in deps.values():
            if d.eng == eng and not d.is_dma and not dma:
                if eng == "pe":
                    continue
                if SELFSYNC == "drain":
                    if npos - d.pos <= 2:
                        o.selfdep = True
                    continue
            if d.is_dma:
                o.deps.append(d)
                d.has_dep = True
            else:
                b = best.get(d.eng)
                if b is None or d.pos > b.pos:
                    best[d.eng] = d
        for d in best.values():
            o.deps.append(d)
            d.has_dep = True
        o.pos = len(self.ops[eng])
        self.ops[eng].append(o)
        return o

    def emit(self, stack):
        nc = self.nc
        for e in ENGS:
            cnt = 0
            nsem = 0
            cur = None
            for o in self.ops[e]:
                if o.is_dma or not o.has_dep:
                    continue
                if cur is None or cnt >= self.maxv:
                    cur = stack.enter_context(nc.semaphore(f"s_{e}_{nsem}"))
                    nsem += 1
                    cnt = 0
                cnt += 1
                o.sig = (cur, cnt, 1)
        for e in ENGS:
            dl = [o for o in self.ops[e] if o.is_dma]
            if not dl:
                continue
            sems = [stack.enter_context(nc.semaphore(f"d_{e}_{i}")) for i in range(min(self.n_dma_sems, len(dl)))]
            cnts = [0] * len(sems)
            for i, o in enumerate(dl):
                j = i % len(sems)
                cnts[j] += 1
                o.sig = (sems[j], 16 * cnts[j], 16)

        def run_engine(ename, eng):
            waited = {}
            for o in self.ops[ename]:
                need = {}
                for d in o.deps:
                    s, v = d.sig[0], d.sig[1]
                    if waited.get(s, 0) >= v:
                        continue
                    if need.get(s, 0) < v:
                        need[s] = v
                if o.is_dma:
                    s, v = o.sig[0], o.sig[1]
                    if v > 16 and waited.get(s, 0) < v - 16:
                        need[s] = max(need.get(s, 0), v - 16)
                for s, v in need.items():
                    eng.wait_ge(s, v)
                    waited[s] = v
                if o.selfdep:
                    eng.drain()
                ins = o.fn(eng)
                if o.sig is not None:
                    ins.then_inc(o.sig[0], o.sig[2])
            last = {}
            for o in self.ops[ename]:
                if o.is_dma:
                    last[o.sig[0]] = o.sig[1]
            for s, v in last.items():
                if waited.get(s, 0) < v:
                    eng.wait_ge(s, v)

        with nc.Block() as block:
            @block.tensor
            def _(e):
                run_engine("pe", e)

            @block.scalar
            def _(e):
                run_engine("act", e)

            @block.vector
            def _(e):
                run_engine("dve", e)

            @block.gpsimd
            def _(e):
                run_engine("pool", e)

            @block.sync
            def _(e):
                run_engine("sp", e)


ARENA = 69632
BLK = 1024


def build(L):
    nc = bass.Bass("TRN2", target_bir_lowering=False)

    def din(name, shape, dt=F32):
        return nc.dram_tensor(name, shape, dt, kind="ExternalInput").ap()

    x_d = din("x", [S_, D])
    small_d = din("small", [128, NS])
    consts_d = din("consts", [128, NCF])
    pos_d = din("pos", [128, NT], I32)
    w_ada_d = din("w_ada", [L, D, 6 * D])
    w_in_d = din("w_in", [L, D, 5632])
    pool_w_d = din("pool_w", [L, 4, 64, 64])
    p_pool_d = din("p_pool", [L, 256, D])
    p_conv_d = din("p_conv", [L, 256, D])
    p_attn_d = din("p_attn", [L, 512, D])
    w_out_d = din("w_out", [L, D, D])
    w_gate_d = din("w_gate", [L, D, DFF])
    w_up_d = din("w_up", [L, D, DFF])
    w_down_d = din("w_down", [L, DFF, D])
    lidx_d = None
    out_d = nc.dram_tensor("out", [S_, D], F32, kind="ExternalOutput").ap()

    with ExitStack() as st:
        S = Sched(nc)

        def sb(name, shape, dt):
            return st.enter_context(nc.sbuf_tensor("sb_" + name, shape, dt))

        xT = sb("xT", [128, 8, S_], F32)
        hT = sb("hT", [128, 8, S_], BF16)
        wsl = [sb(f"ws{i}", [128, 4096], BF16) for i in range(4)]
        small = sb("small", [128, NS], F32)
        consts = sb("consts", [128, NCF], F32)
        identb = sb("identb", [128, 128], BF16)
        trib = sb("trib", [128, 128], BF16)
        onesb = sb("onesb", [128, 128], BF16)
        modTs = [sb("modT0", [128, 48], F32), sb("modT1", [128, 48], F32)]
        cur = {"lp": 0}
        avecs = [sb("avec0", [128, 32], F32), sb("avec1", [128, 32], F32)]
        posi = sb("posi", [128, NT], I32)
        posf = sb("posf", [128, NT], F32)
        ang = sb("ang", [128, NT, 8], F32)
        cosT = sb("cosT", [128, NT, 8], F32)
        sinT = sb("sinT", [128, NT, 8], F32)
        cact = sb("cact", [128, 8], F32)
        cactb = sb("cactb", [128, 8], BF16)
        poolW = sb("poolW", [128, 2, 128], BF16)
        kmT = sb("kmT", [64, 4, 8], BF16)
        kmf = sb("kmf", [64, 4], F32)
        arena = sb("arena", [128, ARENA // 4], F32)
        arena_b = arena.bitcast(BF16)
        banks = [st.enter_context(nc.psum_tensor(f"bank{i}", [128, 512], F32)) for i in range(8)]
        banks_b = [b.bitcast(BF16) for b in banks]

        class AV:
            def __init__(self, base, P, fshape, dt):
                self.base = base
                self.P = P
                self.fshape = list(fshape)
                self.dt = dt
                self.esz = 2 if dt == BF16 else 4
                n = int(np.prod(fshape))
                self.n = n
                assert base % 32 == 0 and base + n * self.esz <= ARENA, (base, n, self.esz)
                src = arena_b if dt == BF16 else arena
                e0 = base // self.esz
                flat = src[0:P, e0:e0 + n]
                if len(fshape) == 1:
                    self.ap = flat
                elif len(fshape) == 2:
                    self.ap = flat.rearrange("p (a b) -> p a b", a=fshape[0])
                elif len(fshape) == 3:
                    self.ap = flat.rearrange("p (a b c) -> p a b c", a=fshape[0], b=fshape[1])
                else:
                    raise ValueError

            def keys(self, lo=0, n=None):
                if n is None:
                    n = self.n - lo
                b0 = (self.base + lo * self.esz) // BLK
                b1 = (self.base + (lo + n) * self.esz - 1) // BLK
                return [("A", b) for b in range(b0, b1 + 1)]

            def kidx(self, *idx):
                strides = []
                s = 1
                for d in reversed(self.fshape):
                    strides.append(s)
                    s *= d
                strides = strides[::-1]
                lo = sum(i * st_ for i, st_ in zip(idx, strides))
                n = strides[len(idx) - 1] if idx else self.n
                return self.keys(lo, n)

            def krange(self, idx, lo, n):
                strides = []
                s = 1
                for d in reversed(self.fshape):
                    strides.append(s)
                    s *= d
                strides = strides[::-1]
                base = sum(i * st_ for i, st_ in zip(idx, strides))
                return self.keys(base + lo, n)

        bank_rr = {"F": [0, [0, 1, 2, 3, 4, 5]], "n": [0, [6]], "g": [0, [int(x) for x in _os.environ.get("GB", "2,3").split(",")]], "s": [0, [int(x) for x in _os.environ.get("SB", "4,5").split(",")]], "a": [0, [6, 7]], "G": [0, [0, 1, 2, 3, 4, 5, 6, 7]]}

        def nbank(pool):
            st_ = bank_rr[pool]
            b = st_[1][st_[0] % len(st_[1])]
            st_[0] += 1
            return b

        def bk(b):
            return ("bank", b)

        ws_rr = [0]

        ws_grp = {"ffn": [0, [0, 1, 2]], "mod": [0, [3]]}

        def wslot(grp=None):
            if grp is None:
                i = ws_rr[0] % 4
                ws_rr[0] += 1
            else:
                g_ = ws_grp[grp]
                i = g_[1][g_[0] % len(g_[1])]
                g_[0] += 1
            S.new_gen(("ws", i))
            return i

        def wload(slot, dst_ap, src_ap):
            S.op("pool", lambda e: e.dma_start(out=dst_ap, in_=src_ap), joins=[("ws", slot)], dma=True)

        def kx(c, tc):
            return ("xT", c, tc)

        def kh(c, tc):
            return ("hT", c, tc)

        def tcs(tc):
            return slice(tc * 512, (tc + 1) * 512)

        evac_rr = [0]

        def evac_eng():
            evac_rr[0] += 1
            return "act" if evac_rr[0] % 2 else "dve"

        def copy_op(eng, out_ap, in_ap, reads, writes):
            if eng == "act":
                S.op("act", lambda e: e.copy(out=out_ap, in_=in_ap), reads=reads, writes=writes)
            else:
                S.op(eng, lambda e: e.tensor_copy(out=out_ap, in_=in_ap), reads=reads, writes=writes)

        S.op("sp", lambda e: e.dma_start(out=small[:], in_=small_d), writes=["small"], dma=True)
        S.op("sp", lambda e: e.dma_start(out=consts[:], in_=consts_d), writes=["consts"], dma=True)
        S.op("sp", lambda e: e.dma_start(out=posi[:], in_=pos_d), writes=["posi"], dma=True)
        S.op("dve", lambda e: e.tensor_copy(out=identb[:], in_=consts[:, O_IDF:O_IDF + 128]), reads=["consts"], writes=["identb"])
        S.op("dve", lambda e: e.tensor_copy(out=trib[:], in_=consts[:, O_TRI:O_TRI + 128]), reads=["consts"], writes=["trib"])
        S.op("dve", lambda e: e.memset(onesb[:], 1.0), writes=["onesb"])
        S.op("dve", lambda e: e.tensor_copy(out=posf[:], in_=posi[:]), reads=["posi"], writes=["posf"])
        S.op("dve", lambda e: e.tensor_tensor(out=ang[:], in0=posf[:].unsqueeze(2).to_broadcast([128, NT, 8]),
                                              in1=consts[:, O_FREQ:O_FREQ + 8].unsqueeze(1).to_broadcast([128, NT, 8]),
                                              op=ALU.mult), reads=["posf", "consts"], writes=["ang"])
        TWO_PI = 2.0 * math.pi
        ki = sb("ki", [128, NT, 8], I32)
        kf = sb("kf", [128, NT, 8], F32)
        rr = sb("rr", [128, NT, 8], F32)
        mm_ = sb("mm_", [128, NT, 8], F32)

        def sincos(dst, shift, nm):
            S.op("dve", lambda e: e.tensor_scalar(out=rr[:], in0=ang[:], scalar1=shift, scalar2=1.0 / TWO_PI, op0=ALU.add, op1=ALU.mult),
                 reads=["ang"], writes=["rr"])
            S.op("dve", lambda e: e.tensor_copy(out=ki[:], in_=rr[:]), reads=["rr"], writes=["ki"])
            S.op("dve", lambda e: e.tensor_copy(out=kf[:], in_=ki[:]), reads=["ki"], writes=["kf"])
            S.op("dve", lambda e: e.tensor_scalar(out=rr[:], in0=ang[:], scalar1=shift, scalar2=None, op0=ALU.add),
                 reads=["ang", "kf"], writes=["rr"])
            S.op("dve", lambda e: e.scalar_tensor_tensor(out=rr[:], in0=kf[:], scalar=-TWO_PI, in1=rr[:], op0=ALU.mult, op1=ALU.add),
                 reads=["kf", "rr"], writes=["rr"])
            S.op("dve", lambda e: e.tensor_scalar(out=mm_[:], in0=rr[:], scalar1=math.pi, scalar2=-TWO_PI, op0=ALU.is_gt, op1=ALU.mult),
                 reads=["rr"], writes=["mm_"])
            S.op("dve", lambda e: e.tensor_tensor(out=rr[:], in0=rr[:], in1=mm_[:], op=ALU.add), reads=["rr", "mm_"], writes=["rr"])
            S.op("dve", lambda e: e.tensor_scalar(out=mm_[:], in0=rr[:], scalar1=-math.pi, scalar2=TWO_PI, op0=ALU.is_lt, op1=ALU.mult),
                 reads=["rr"], writes=["mm_"])
            S.op("dve", lambda e: e.tensor_tensor(out=rr[:], in0=rr[:], in1=mm_[:], op=ALU.add), reads=["rr", "mm_"], writes=["rr"])
            S.op("act", lambda e: e.activation(out=dst[:], in_=rr[:], func=AF.Sin, scale=1.0 - 1e-6), reads=["rr"], writes=[nm])

        sincos(sinT, 0.0, "sinT")
        sincos(cosT, 0.5 * math.pi, "cosT")
        S.op("act", lambda e: e.activation(out=cact[:], in_=small[:, O_CT:O_CT + 8], func=AF.Silu), reads=["small"], writes=["cact"])
        S.op("dve", lambda e: e.tensor_copy(out=cactb[:], in_=cact[:]), reads=["cact"], writes=["cactb"])

        xin = [AV(0, 128, [4, D], F32), AV(16384, 128, [4, D], F32)]
        for tc in range(NTC):
            xv = xin[tc % 2]
            S.op("sp", (lambda e, xv=xv, tc=tc: e.dma_start(
                out=xv.ap, in_=x_d[tc * 512:(tc + 1) * 512, :].rearrange("(t p) d -> p t d", p=128))),
                writes=xv.keys(), dma=True)
            for c in range(8):
                b = nbank("G")
                for t in range(4):
                    S.op("pe", (lambda e, b=b, xv=xv, t=t, c=c: e.transpose(
                        banks[b][:, t * 128:(t + 1) * 128], xv.ap[:, t, c * 128:(c + 1) * 128], consts[:, O_IDF:O_IDF + 128])),
                        reads=xv.keys() + ["consts"], writes=[bk(b)])
                copy_op(evac_eng(), xT[:, c, tcs(tc)], banks[b][:], [bk(b)], [kx(c, tc)])

        def mod_steps(l, grp=None):
            lp = l % 2
            modT, avec = modTs[lp], avecs[lp]
            kM, kA = ("modT", lp), ("avec", lp)
            pm = 7
            steps = []
            for j in range(12):
                def sj(j=j):
                    s = wslot(grp)
                    wv = wsl[s][:, :].rearrange("p (k n) -> p k n", k=8)
                    wload(s, wv, w_ada_d[l][:, j * 512:(j + 1) * 512].rearrange("(kc p) n -> p kc n", p=128))
                    for oc in range(4):
                        col = j * 4 + oc
                        for kc in range(8):
                            S.op("pe", (lambda e, oc=oc, kc=kc, col=col: e.matmul(
                                banks[pm][:, col:col + 1], lhsT=wv[:, kc, oc * 128:(oc + 1) * 128], rhs=cactb[:, kc:kc + 1],
                                start=(kc == 0), stop=(kc == 7), skip_group_check=True)),
                                reads=[("ws", s), "cactb"], writes=[bk(pm)])
                steps.append(sj)

            def fin():
                S.op("dve", (lambda e: e.tensor_tensor(out=modT[:], in0=banks[pm][:, 0:48],
                                                       in1=small[:, O_BADA + l * 48:O_BADA + (l + 1) * 48], op=ALU.add)),
                     reads=[bk(pm), "small"], writes=[kM])
                S.op("dve", (lambda e: e.scalar_tensor_tensor(out=avec[:, 0:8], in0=modT[:, 8:16], scalar=1.0,
                                                              in1=small[:, O_N1W + l * 8:O_N1W + (l + 1) * 8],
                                                              op0=ALU.add, op1=ALU.mult)),
                     reads=[kM, "small"], writes=[kA])
                S.op("dve", (lambda e: e.scalar_tensor_tensor(out=avec[:, 8:16], in0=modT[:, 32:40], scalar=1.0,
                                                              in1=small[:, O_N2W + l * 8:O_N2W + (l + 1) * 8],
                                                              op0=ALU.add, op1=ALU.mult)),
                     reads=[kM, "small"], writes=[kA])
                S.op("dve", lambda e: e.tensor_scalar(out=avec[:, 16:24], in0=modT[:, 16:24], scalar1=0.5, scalar2=None, op0=ALU.mult),
                     reads=[kM], writes=[kA])
            steps.append(fin)
            return steps

        def norm_tc_steps(a_off, sh_off, lp, base=0):
            modT, avec = modTs[lp], avecs[lp]
            sqbs = [AV(base, 128, [8, 512], BF16), AV(base + 8192, 128, [8, 512], BF16)]
            rst = [AV(base + 16384, 128, [512], F32), AV(base + 18432, 128, [512], F32)]
            tmp = [AV(base + 20480, 128, [512], F32), AV(base + 22528, 128, [512], F32), AV(base + 24576, 128, [512], F32)]
            st_ = {"ti": 0}

            def p1(tc):
                sqb = sqbs[tc % 2]
                for c in range(8):
                    if c % 2 == 0:
                        S.op("act", (lambda e, c=c: e.activation(out=sqb.ap[:, c, :], in_=xT[:, c, tcs(tc)], func=AF.Square)),
                             reads=[kx(c, tc)], writes=sqb.kidx(c))
                    else:
                        S.op("dve", (lambda e, c=c: e.tensor_tensor(out=sqb.ap[:, c, :], in0=xT[:, c, tcs(tc)],
                                                                    in1=xT[:, c, tcs(tc)], op=ALU.mult)),
                             reads=[kx(c, tc)], writes=sqb.kidx(c))
                b = nbank("n")
                for c in range(8):
                    S.op("pe", (lambda e, c=c: e.matmul(banks[b][:], lhsT=onesb[:], rhs=sqb.ap[:, c, :],
                                                       start=(c == 0), stop=(c == 7))),
                         reads=sqb.kidx(c) + ["onesb"], writes=[bk(b)])
                r = rst[tc % 2]
                S.op("act", (lambda e: e.activation(out=r.ap, in_=banks[b][:], func=AF.Ln, bias=epsb[:, 0:1], scale=1.0 / D)),
                     reads=[bk(b), "epsb"], writes=r.keys())
                S.op("act", (lambda e: e.activation(out=r.ap, in_=r.ap, func=AF.Exp, scale=-0.5)), reads=r.keys(), writes=r.keys())

            def p2(tc):
                r = rst[tc % 2]
                for c in range(8):
                    t_ = tmp[st_["ti"] % 3]
                    st_["ti"] += 1
                    S.op("dve", (lambda e, t_=t_, c=c: e.tensor_tensor(out=t_.ap, in0=xT[:, c, tcs(tc)], in1=r.ap, op=ALU.mult)),
                         reads=[kx(c, tc)] + r.keys(), writes=t_.keys())
                    S.op("act", (lambda e, t_=t_, c=c: e.activation(
                        out=hT[:, c, tcs(tc)], in_=t_.ap, func=AF.Identity,
                        scale=avec[:, a_off + c:a_off + c + 1], bias=modT[:, sh_off + c:sh_off + c + 1])),
                        reads=t_.keys() + [("avec", lp), ("modT", lp)], writes=[kh(c, tc)])

            return [lambda: p1(0),
                    lambda: (p1(1), p2(0)),
                    lambda: (p1(2), p2(1)),
                    lambda: (p1(3), p2(2), p2(3))]

        epsb = sb("epsb", [128, 1], F32)
        S.op("dve", lambda e: e.memset(epsb[:], EPS), writes=["epsb"])

        yattnT = AV(0, 128, [4, S_], BF16)
        ypoolT = AV(16384, 128, [2, S_], BF16)
        yconvT = AV(24576, 128, [2, S_], BF16)
        A1 = 32768
        KaT = AV(A1, 72, [4, S_], BF16)
        Va = AV(A1 + 16384, 128, [NT, 4, 65], BF16)
        QaTs = [AV(A1 + 24736, 72, [4, 512], BF16), AV(65536, 72, [4, 512], BF16)]
        o_ = A1 + 24736 + 4096
        qk_tok = [AV(o_, 128, [8, 72], BF16), AV(o_ + 1152, 128, [8, 72], BF16)]
        o_ += 2304
        mb_tok = [AV(o_, 128, [4, 72], BF16), AV(o_ + 576, 128, [4, 72], BF16)]
        o_ += 1152
        qn_s = [AV(16384, 128, [8, 64], F32), AV(18432, 128, [8, 64], F32)]
        ytok = AV(20480, 128, [4, 256], BF16)
        pT = [AV(22528 + i * 1024, 128, [512], BF16) for i in range(4)]
        sq_s = AV(26624, 128, [512], BF16)
        cmp_s = AV(27648, 128, [4, 8, 8], F32)
        sm_par = []
        for i_ in range(2):
            o2 = 28672 + i_ * 1920
            sm_par.append(dict(
                ssq8=AV(o2, 128, [8], F32), vv8=AV(o2 + 32, 128, [8], F32), yy8=AV(o2 + 64, 128, [8], F32),
                aa8=AV(o2 + 96, 128, [8], F32), gsb=AV(o2 + 128, 128, [4, 8], F32), rank_s=AV(o2 + 256, 128, [4, 8], F32),
                rt=[AV(o2 + 384 + k_ * 256, 128, [8, 8], F32) for k_ in range(4)],
                qr=AV(o2 + 1408, 128, [8, 16], F32)))
        rec_s = AV(65056, 128, [4], F32)

        def merge_steps(lists):
            idx = [0] * len(lists)
            tot = [max(1, len(x)) for x in lists]
            while True:
                best = None
                for i, x in enumerate(lists):
                    if idx[i] < len(x):
                        frac = idx[i] / tot[i]
                        if best is None or frac < best[0]:
                            best = (frac, i)
                if best is None:
                    break
                i = best[1]
                lists[i][idx[i]]()
                idx[i] += 1

        def tile_steps(l, hg, qc, j, wqk, wv, sqk, sv):
            t = qc * 4 + j
            blk = t // 2
            QaT = QaTs[qc % 2]
            qn = qn_s[t % 2]
            qt = qk_tok[t % 2]
            mb = mb_tok[t % 2]
            sp_ = sm_par[t % 2]
            ssq8, vv8, yy8, aa8, gsb, rank_s, rt, qr = (sp_["ssq8"], sp_["vv8"], sp_["yy8"], sp_["aa8"], sp_["gsb"],
                                                        sp_["rank_s"], sp_["rt"], sp_["qr"])
            st_ = {}
            steps = []

            def s1():
                bq = st_["bq"] = t % 2
                S.op("pool", (lambda e: e.memset(qt.ap[:, 4:8, 64:72], 0.0)), writes=qt.keys())
                S.op("pool", (lambda e: e.memset(qt.ap[:, 4:8, 64 + blk:65 + blk], 1.0)), writes=qt.keys())
                S.op("pool", (lambda e: e.memset(mb.ap[:, :, 64:72], NEGB)), writes=mb.keys())
                S.op("pool", (lambda e: e.memset(mb.ap[:, :, 64:65 + blk], 0.0)), writes=mb.keys())
                for kc in range(8):
                    S.op("pe", (lambda e, kc=kc: e.matmul(
                        banks[bq][:], lhsT=hT[:, kc, t * 128:(t + 1) * 128], rhs=wqk[:, kc, :],
                        start=(kc == 0), stop=(kc == 7))),
                        reads=[kh(kc, t // 4), ("ws", sqk)], writes=[bk(bq)])
                bv = nbank("g")
                for kc in range(8):
                    S.op("pe", (lambda e, kc=kc: e.matmul(
                        banks[bv][:, 0:256], lhsT=hT[:, kc, t * 128:(t + 1) * 128], rhs=wv[:, kc, :],
                        start=(kc == 0), stop=(kc == 7))),
                        reads=[kh(kc, t // 4), ("ws", sv)], writes=[bk(bv)])
                S.op("act", (lambda e: e.activation(out=sq_s.ap, in_=banks[bq][:], func=AF.Square)),
                     reads=[bk(bq)], writes=sq_s.keys())
                S.op("act", (lambda e: e.copy(
                    out=Va.ap[:, t, :, 0:64], in_=banks[bv][:, 0:256].rearrange("p (h d) -> p h d", h=4))),
                    reads=[bk(bv)], writes=Va.kidx(t))
            steps.append(s1)

            def s2():
                S.op("dve", lambda e: e.tensor_reduce(out=ssq8.ap, in_=sq_s.ap.rearrange("p (h d) -> p h d", h=8),
                                                      axis=AX.X, op=ALU.add),
                     reads=sq_s.keys(), writes=ssq8.keys())
                S.op("dve", lambda e: e.tensor_scalar(out=vv8.ap, in0=ssq8.ap, scalar1=1.0 / 64, scalar2=EPS, op0=ALU.mult, op1=ALU.add),
                     reads=ssq8.keys(), writes=vv8.keys())
                S.op("dve", lambda e: e.tensor_scalar(out=yy8.ap.bitcast(I32), in0=vv8.ap.bitcast(I32), scalar1=-0.5, scalar2=1597463007.0,
                                                      op0=ALU.mult, op1=ALU.add),
                     reads=vv8.keys(), writes=yy8.keys())
                for _ in range(2):
                    S.op("dve", lambda e: e.tensor_tensor(out=aa8.ap, in0=yy8.ap, in1=yy8.ap, op=ALU.mult),
                         reads=yy8.keys(), writes=aa8.keys())
                    S.op("dve", lambda e: e.scalar_tensor_tensor(out=aa8.ap, in0=aa8.ap, scalar=-0.5, in1=vv8.ap, op0=ALU.mult, op1=ALU.mult),
                         reads=aa8.keys() + vv8.keys(), writes=aa8.keys())
                    S.op("dve", lambda e: e.scalar_tensor_tensor(out=yy8.ap, in0=aa8.ap, scalar=1.5, in1=yy8.ap, op0=ALU.add, op1=ALU.mult),
                         reads=yy8.keys() + aa8.keys(), writes=yy8.keys())
            steps.append(s2)

            def s3():
                bq = st_["bq"]
                wq = small[:, O_QKW + l * 128:O_QKW + (l + 1) * 128].rearrange("p (a d) -> p a d", a=2)
                S.op("dve", (lambda e: e.tensor_tensor(
                    out=qn.ap, in0=banks[bq][:].rearrange("p (h d) -> p h d", h=8),
                    in1=yy8.ap.unsqueeze(2).to_broadcast([128, 8, 64]), op=ALU.mult)),
                    reads=[bk(bq)] + yy8.keys(), writes=qn.keys())
                S.op("dve", (lambda e: e.tensor_tensor(
                    out=qr.ap.rearrange("p (a h) d -> p a h d", a=2), in0=qn.ap[:, :, 0:16].rearrange("p (a h) d -> p a h d", a=2),
                    in1=wq[:, :, 0:16].unsqueeze(2).to_broadcast([128, 2, 4, 16]), op=ALU.mult)),
                    reads=qn.keys() + ["small"], writes=qr.keys())
                S.op("dve", (lambda e: e.tensor_tensor(
                    out=qt.ap[:, :, 16:64].rearrange("p (a h) d -> p a h d", a=2), in0=qn.ap[:, :, 16:64].rearrange("p (a h) d -> p a h d", a=2),
                    in1=wq[:, :, 16:64].unsqueeze(2).to_broadcast([128, 2, 4, 48]), op=ALU.mult)),
                    reads=qn.keys() + ["small"], writes=qt.keys())
            steps.append(s3)

            def s4():
                cb = cosT[:, t, :].unsqueeze(1).to_broadcast([128, 8, 8])
                sbb = sinT[:, t, :].unsqueeze(1).to_broadcast([128, 8, 8])
                x1 = qr.ap[:, :, 0:8]
                x2 = qr.ap[:, :, 8:16]
                S.op("dve", (lambda e: e.tensor_tensor(out=rt[0].ap, in0=x1, in1=cb, op=ALU.mult)),
                     reads=qr.keys() + ["cosT"], writes=rt[0].keys())
                S.op("dve", (lambda e: e.tensor_tensor(out=rt[1].ap, in0=x2, in1=sbb, op=ALU.mult)),
                     reads=qr.keys() + ["sinT"], writes=rt[1].keys())
                S.op("dve", (lambda e: e.tensor_tensor(out=rt[2].ap, in0=x2, in1=cb, op=ALU.mult)),
                     reads=qr.keys() + ["cosT"], writes=rt[2].keys())
                S.op("dve", (lambda e: e.tensor_tensor(out=rt[3].ap, in0=x1, in1=sbb, op=ALU.mult)),
                     reads=qr.keys() + ["sinT"], writes=rt[3].keys())
                S.op("dve", (lambda e: e.tensor_tensor(out=qt.ap[:, :, 0:8], in0=rt[0].ap, in1=rt[1].ap, op=ALU.subtract)),
                     reads=rt[0].keys() + rt[1].keys(), writes=qt.keys())
                S.op("dve", (lambda e: e.tensor_tensor(out=qt.ap[:, :, 8:16], in0=rt[2].ap, in1=rt[3].ap, op=ALU.add)),
                     reads=rt[2].keys() + rt[3].keys(), writes=qt.keys())
            steps.append(s4)

            def s5():
                btr = nbank("g")
                for h in range(4):
                    S.op("pe", (lambda e, h=h: e.transpose(
                        banks_b[btr][0:72, h * 128:(h + 1) * 128], qt.ap[:, 4 + h, 0:72], identb[:])),
                        reads=qt.keys() + ["identb"], writes=[bk(btr)])
                    S.op("pe", (lambda e, h=h: e.transpose(
                        banks_b[btr][0:64, 512 + h * 128:512 + (h + 1) * 128], qt.ap[:, h, 0:64], identb[:])),
                        reads=qt.keys() + ["identb"], writes=[bk(btr)])
                S.op("act", (lambda e: e.copy(
                    out=KaT.ap[:, :, t * 128:(t + 1) * 128], in_=banks_b[btr][0:72, 0:512].rearrange("p (h n) -> p h n", h=4))),
                    reads=[bk(btr)], writes=[k_ for h in range(4) for k_ in KaT.krange((h,), t * 128, 128)])
                S.op("dve", (lambda e: e.tensor_copy(
                    out=QaT.ap[0:64, :, j * 128:(j + 1) * 128], in_=banks_b[btr][0:64, 512:1024].rearrange("p (h n) -> p h n", h=4))),
                    reads=[bk(btr)], writes=QaT.keys())
                if t % 2 == 1 and blk < 7:
                    S.op("dve", (lambda e: e.tensor_reduce(
                        out=kmf[:], in_=KaT.ap[0:64, :, blk * 256:(blk + 1) * 256], axis=AX.X, op=ALU.add)),
                        reads=[k_ for h in range(4) for k_ in KaT.krange((h,), blk * 256, 256)], writes=["kmf"])
                    S.op("dve", (lambda e: e.tensor_scalar(
                        out=kmT[:, :, blk:blk + 1], in0=kmf[:].unsqueeze(2), scalar1=1.0 / 256, scalar2=None, op0=ALU.mult)),
                        reads=["kmf"], writes=["kmT"])
            steps.append(s5)

            def s6():
                if blk >= 4:
                    bg = nbank("g")
                    for h in range(4):
                        S.op("pe", (lambda e, h=h: e.matmul(
                            banks[bg][:, h * 8:h * 8 + blk], lhsT=QaT.ap[0:64, h, j * 128:(j + 1) * 128],
                            rhs=kmT[:, h, 0:blk], start=True, stop=True, skip_group_check=True)),
                            reads=QaT.keys() + ["kmT"], writes=[bk(bg)])
                    S.op("act", (lambda e: e.copy(
                        out=gsb.ap[:, :, 0:blk], in_=banks[bg][:, 0:32].rearrange("p (h n) -> p h n", h=4)[:, :, 0:blk])),
                        reads=[bk(bg)], writes=gsb.keys())
                    S.op("dve", (lambda e: e.tensor_tensor(
                        out=cmp_s.ap[:, :, 0:blk, 0:blk],
                        in0=gsb.ap[:, :, 0:blk].unsqueeze(2).to_broadcast([128, 4, blk, blk]),
                        in1=gsb.ap[:, :, 0:blk].unsqueeze(3).to_broadcast([128, 4, blk, blk]), op=ALU.is_gt)),
                        reads=gsb.keys(), writes=cmp_s.keys())
                    S.op("dve", (lambda e: e.tensor_reduce(
                        out=rank_s.ap[:, :, 0:blk], in_=cmp_s.ap[:, :, 0:blk, 0:blk], axis=AX.X, op=ALU.add)),
                        reads=cmp_s.keys(), writes=rank_s.keys())
                    S.op("dve", (lambda e: e.tensor_scalar(
                        out=mb.ap[:, :, 64:64 + blk], in0=rank_s.ap[:, :, 0:blk], scalar1=2.5, scalar2=NEGB,
                        op0=ALU.is_gt, op1=ALU.mult)),
                        reads=rank_s.keys(), writes=mb.keys())
                bm = nbank("g")
                for h in range(4):
                    S.op("pe", (lambda e, h=h: e.transpose(
                        banks_b[bm][0:72, h * 128:(h + 1) * 128], mb.ap[:, h, 0:72], identb[:])),
                        reads=mb.keys() + ["identb"], writes=[bk(bm)])
                S.op("act", (lambda e: e.copy(
                    out=QaT.ap[64:72, :, j * 128:(j + 1) * 128], in_=banks_b[bm][64:72, 0:512].rearrange("p (h n) -> p h n", h=4))),
                    reads=[bk(bm)], writes=QaT.keys())
            steps.append(s6)
            return steps

        def attn_steps(hg, qc):
            QaT = QaTs[qc % 2]
            nk = 4 * (qc + 1)
            items = [(h, kt) for h in range(4) for kt in range(nk)]
            st_ = {}

            def stage1(i):
                h, kt = items[i]
                j0 = max(0, kt - 4 * qc)
                bs = nbank("s")
                p_ = pT[i % 4]
                S.op("pe", (lambda e: e.matmul(
                    banks[bs][:, j0 * 128:512], lhsT=KaT.ap[0:72, h, kt * 128:(kt + 1) * 128],
                    rhs=QaT.ap[0:72, h, j0 * 128:512], start=True, stop=True)),
                    reads=KaT.krange((h,), kt * 128, 128) + QaT.keys(), writes=[bk(bs)])
                S.op("act", (lambda e: e.activation(
                    out=p_.ap[:, j0 * 128:512], in_=banks[bs][:, j0 * 128:512], func=AF.Exp, scale=0.125)),
                    reads=[bk(bs)], writes=p_.keys())
                if kt >= 4 * qc:
                    S.op("pool", (lambda e: e.tensor_tensor(
                        out=p_.ap[:, j0 * 128:(j0 + 1) * 128], in0=p_.ap[:, j0 * 128:(j0 + 1) * 128], in1=trib[:], op=ALU.mult)),
                        reads=p_.keys() + ["trib"], writes=p_.keys())

            def stage2(i):
                h, kt = items[i]
                j0 = max(0, kt - 4 * qc)
                p_ = pT[i % 4]
                if kt == 0:
                    st_[h] = nbank("a")
                ba = st_[h]
                accv = banks[ba][:].rearrange("p (j n) -> p j n", j=4)
                for jj in range(j0, 4):
                    S.op("pe", (lambda e, jj=jj: e.matmul(
                        accv[:, jj, 0:65], lhsT=p_.ap[:, jj * 128:(jj + 1) * 128], rhs=Va.ap[:, kt, h, :],
                        start=(kt == 0 and jj == 0), stop=(kt == 4 * qc + jj), skip_group_check=True)),
                        reads=p_.keys() + Va.kidx(kt), writes=[bk(ba)])
                if kt == nk - 1:
                    S.op("dve", (lambda e: e.reciprocal(out=rec_s.ap, in_=accv[:, :, 64])),
                         reads=[bk(ba)], writes=rec_s.keys())
                    S.op("dve", (lambda e: e.tensor_tensor(
                        out=ytok.ap[:, :, h * 64:(h + 1) * 64], in0=accv[:, :, 0:64],
                        in1=rec_s.ap.unsqueeze(2).to_broadcast([128, 4, 64]), op=ALU.mult)),
                        reads=[bk(ba)] + rec_s.keys(), writes=ytok.keys())

            n = len(items)
            LA = int(_os.environ.get('LA', '3'))
            steps = [(lambda: [stage1(i_) for i_ in range(LA)])]
            for i in range(n):
                def sk(i=i):
                    stage2(i)
                    if i + LA < n:
                        stage1(i + LA)
                steps.append(sk)

            def sy():
                by = nbank("g")
                for jj in range(4):
                    for cc in range(2):
                        S.op("pe", (lambda e, jj=jj, cc=cc: e.transpose(
                            banks_b[by][:, (cc * 4 + jj) * 128:(cc * 4 + jj + 1) * 128], ytok.ap[:, jj, cc * 128:(cc + 1) * 128], identb[:])),
                            reads=ytok.keys() + ["identb"], writes=[bk(by)])
                for cc in range(2):
                    copy_op("act" if cc == 0 else "dve", yattnT.ap[:, hg * 2 + cc, qc * 512:(qc + 1) * 512],
                            banks_b[by][:, cc * 512:(cc + 1) * 512], [bk(by)], yattnT.krange((hg * 2 + cc,), qc * 512, 512))
            steps.append(sy)
            return steps

        def attention_phase(l):
            for hg in range(2):
                sqk = wslot()
                wqk = wsl[sqk][:, :].rearrange("p (k n) -> p k n", k=8)
                wload(sqk, wqk[:, :, 0:256], w_in_d[l][:, 1024 + hg * 256:1024 + (hg + 1) * 256].rearrange("(kc p) n -> p kc n", p=128))
                wload(sqk, wqk[:, :, 256:512], w_in_d[l][:, 1536 + hg * 256:1536 + (hg + 1) * 256].rearrange("(kc p) n -> p kc n", p=128))
                sv = wslot()
                wv = wsl[sv][:, 0:2048].rearrange("p (k n) -> p k n", k=8)
                wload(sv, wv, w_in_d[l][:, 2048 + hg * 256:2048 + (hg + 1) * 256].rearrange("(kc p) n -> p kc n", p=128))
                S.op("pool", lambda e: e.memset(Va.ap[:, :, :, 64:65], 1.0), writes=Va.keys())
                for m_ in mb_tok:
                    S.op("pool", (lambda e, m_=m_: e.memset(m_.ap[:, :, 0:64], 0.0)), writes=m_.keys())

                def A(qc):
                    tl = [tile_steps(l, hg, qc, j, wqk, wv, sqk, sv) for j in range(4)]
                    out = []
                    dsk = int(_os.environ.get('DSK', '2'))
                    ns = len(tl[0])
                    for k in range(ns + 3 * dsk):
                        for i in range(4):
                            ix = k - i * dsk
                            if 0 <= ix < ns:
                                out.append(tl[i][ix])
                    return out

                merge_steps([A(0)])
                for qc in range(NTC):
                    lists = [attn_steps(hg, qc)]
                    if qc + 1 < NTC:
                        lists.append(A(qc + 1))
                    merge_steps(lists)

        PW = S_ + 16
        up_s = AV(A1, 128, [PW], F32)
        sA_s = AV(A1 + 8256, 128, [PW], F32)
        sB_s = AV(A1 + 16512, 128, [PW], F32)

        def fm_proj(wv_of_kc, b, tc):
            for kc in range(8):
                lhsT, rk = wv_of_kc(kc)
                S.op("pe", (lambda e, lhsT=lhsT, kc=kc, b=b, tc=tc: e.matmul(
                    banks[b][:], lhsT=lhsT, rhs=hT[:, kc, tcs(tc)], start=(kc == 0), stop=(kc == 7))),
                    reads=[kh(kc, tc)] + rk, writes=[bk(b)])

        def pool_phase(l):
            S.op("dve", lambda e: e.memset(poolW[:], 0.0), writes=[("poolW", g) for g in range(4)])
            for g in range(4):
                r0 = (g % 2) * 64
                S.op("pool", (lambda e, g=g, r0=r0, l=l: e.dma_start(out=poolW[r0:r0 + 64, g // 2, r0:r0 + 64], in_=pool_w_d[l, g])),
                     writes=[("poolW", g)], dma=True)
            s = wslot()
            wv = wsl[s][:, 0:2048].rearrange("p (k n) -> p k n", k=8)
            wload(s, wv, w_in_d[l][:, 0:256].rearrange("(kc p) n -> p kc n", p=128))
            for v_ in (up_s, sA_s, sB_s):
                S.op("dve", (lambda e, v_=v_: e.memset(v_.ap[:, 0:16], 0.0)), writes=v_.keys(0, 16))
            for cc in range(2):
                for tc in range(NTC):
                    b = nbank("G")
                    fm_proj(lambda kc, cc=cc: (wv[:, kc, cc * 128:(cc + 1) * 128], [("ws", s)]), b, tc)
                    copy_op(evac_eng(), up_s.ap[:, 16 + tc * 512:16 + (tc + 1) * 512], banks[b][:], [bk(b)],
                            up_s.keys(16 + tc * 512, 512))
                n_lv = 2 if cc == 0 else 4
                cur = up_s
                nxt = [sA_s, sB_s]
                srcs = {}
                for lv in range(1, n_lv + 1):
                    sh = 1 << (lv - 1)
                    dst = nxt[(lv - 1) % 2]
                    p0 = 64 if lv == n_lv else 0
                    S.op("dve", (lambda e, dst=dst, cur=cur, sh=sh, p0=p0: e.tensor_tensor(
                        out=dst.ap[p0:128, 16:PW], in0=cur.ap[p0:128, 16:PW], in1=cur.ap[p0:128, 16 - sh:PW - sh], op=ALU.add)),
                        reads=cur.keys(), writes=dst.keys(16, S_))
                    srcs[lv] = dst
                    cur = dst
                lo_src = srcs[n_lv - 1]
                hi_src = srcs[n_lv]
                pooledT = yconvT
                for (p0, p1, src) in ((0, 64, lo_src), (64, 128, hi_src)):
                    S.op("dve", (lambda e, p0=p0, p1=p1, src=src, cc=cc: e.tensor_tensor(
                        out=src.ap[p0:p1, 16:32], in0=src.ap[p0:p1, 16:32],
                        in1=consts[p0:p1, O_CORR + cc * 16:O_CORR + (cc + 1) * 16], op=ALU.mult)),
                        reads=src.keys() + ["consts"], writes=src.keys(16, 16))
                    S.op("dve", (lambda e, p0=p0, p1=p1, src=src, cc=cc: e.tensor_tensor(
                        out=pooledT.ap[p0:p1, cc, 0:16], in0=src.ap[p0:p1, 16:32], in1=up_s.ap[p0:p1, 16:32], op=ALU.subtract)),
                        reads=src.keys() + up_s.keys(), writes=pooledT.krange((cc,), 0, 16))
                    S.op("dve", (lambda e, p0=p0, p1=p1, src=src, cc=cc: e.scalar_tensor_tensor(
                        out=pooledT.ap[p0:p1, cc, 16:S_], in0=src.ap[p0:p1, 32:PW], scalar=consts[p0:p1, O_INVW + cc:O_INVW + cc + 1],
                        in1=up_s.ap[p0:p1, 32:PW], op0=ALU.mult, op1=ALU.subtract)),
                        reads=src.keys() + up_s.keys() + ["consts"], writes=pooledT.kidx(cc))
                for tc in range(NTC):
                    b = nbank("G")
                    S.op("pe", (lambda e, b=b, cc=cc, tc=tc: e.matmul(banks[b][:], lhsT=poolW[:, cc, :], rhs=pooledT.ap[:, cc, tcs(tc)],
                                                                      start=True, stop=True)),
                         reads=[("poolW", 2 * cc), ("poolW", 2 * cc + 1)] + pooledT.krange((cc,), tc * 512, 512), writes=[bk(b)])
                    S.op("act", (lambda e, b=b, cc=cc, tc=tc, l=l: e.activation(
                        out=ypoolT.ap[:, cc, tcs(tc)], in_=banks[b][:], func=AF.Identity,
                        scale=small[:, O_PSC + l * 2 + cc:O_PSC + l * 2 + cc + 1])),
                        reads=[bk(b), "small"], writes=ypoolT.krange((cc,), tc * 512, 512))

        def conv_phase(l):
            cu_s, cc_s, ac_s = up_s, sA_s, sB_s
            for cc in range(2):
                s = wslot()
                wv = wsl[s][:, 0:3072].rearrange("p (k i n) -> p k i n", k=8, i=3)
                for i in range(3):
                    c0 = 256 + i * 256 + cc * 128
                    wload(s, wv[:, :, i, :], w_in_d[l][:, c0:c0 + 128].rearrange("(kc p) n -> p kc n", p=128))
                S.op("dve", lambda e: e.memset(cu_s.ap[:, 0:16], 0.0), writes=cu_s.keys(0, 16))
                for tc in range(NTC):
                    b = nbank("G")
                    fm_proj(lambda kc: (wv[:, kc, 0, :], [("ws", s)]), b, tc)
                    copy_op("act", cu_s.ap[:, 16 + tc * 512:16 + (tc + 1) * 512], banks[b][:], [bk(b)], cu_s.keys(16 + tc * 512, 512))
                    b2 = nbank("G")
                    fm_proj(lambda kc: (wv[:, kc, 2, :], [("ws", s)]), b2, tc)
                    S.op("dve", (lambda e, b2=b2, tc=tc: e.tensor_tensor(
                        out=cu_s.ap[:, 16 + tc * 512:16 + (tc + 1) * 512], in0=banks[b2][:],
                        in1=cu_s.ap[:, 16 + tc * 512:16 + (tc + 1) * 512], op=ALU.mult)),
                        reads=[bk(b2)] + cu_s.keys(16 + tc * 512, 512), writes=cu_s.keys(16 + tc * 512, 512))
                cw = lambda k, cc=cc, l=l: small[:, O_CW + l * 6 + k * 2 + cc:O_CW + l * 6 + k * 2 + cc + 1]
                S.op("dve", (lambda e, cw=cw: e.tensor_scalar(out=ac_s.ap[:, 16:PW], in0=cu_s.ap[:, 16:PW], scalar1=cw(2), scalar2=None, op0=ALU.mult)),
                     reads=cu_s.keys() + ["small"], writes=ac_s.keys(16, S_))
                S.op("dve", (lambda e, cw=cw: e.scalar_tensor_tensor(out=ac_s.ap[:, 16:PW], in0=cu_s.ap[:, 15:PW - 1], scalar=cw(1),
                                                                     in1=ac_s.ap[:, 16:PW], op0=ALU.mult, op1=ALU.add)),
                     reads=cu_s.keys() + ac_s.keys() + ["small"], writes=ac_s.keys(16, S_))
                S.op("dve", (lambda e, cw=cw: e.scalar_tensor_tensor(out=ac_s.ap[:, 16:PW], in0=cu_s.ap[:, 14:PW - 2], scalar=cw(0),
                                                                     in1=ac_s.ap[:, 16:PW], op0=ALU.mult, op1=ALU.add)),
                     reads=cu_s.keys() + ac_s.keys() + ["small"], writes=ac_s.keys(16, S_))
                for tc in range(NTC):
                    b = nbank("G")
                    fm_proj(lambda kc: (wv[:, kc, 1, :], [("ws", s)]), b, tc)
                    S.op("dve", (lambda e, b=b, cc=cc, tc=tc: e.tensor_tensor(
                        out=yconvT.ap[:, cc, tcs(tc)], in0=banks[b][:], in1=ac_s.ap[:, 16 + tc * 512:16 + (tc + 1) * 512], op=ALU.mult)),
                        reads=[bk(b)] + ac_s.keys(16 + tc * 512, 512), writes=yconvT.krange((cc,), tc * 512, 512))

        merged = AV(A1, 128, [8, 1024], BF16)
        mo = A1 + 16384
        s_t = [AV(mo, 128, [512], F32), AV(mo + 2048, 128, [512], F32)]
        m_t = [AV(mo + 4096, 128, [512], F32), AV(mo + 6144, 128, [512], F32)]
        t_t = [AV(mo + 8192, 128, [512], F32), AV(mo + 10240, 128, [512], F32)]

        def merge_phase(l):
            lp = cur["lp"]
            avec = avecs[lp]
            ysrc = [(ypoolT, 2, 0), (yconvT, 2, 2), (yattnT, 4, 4)]
            pds = [p_pool_d, p_conv_d, p_attn_d]
            cnt = 0
            for hh in range(2):
                for c in range(8):
                    s = wslot()
                    wg = wsl[s][:, 0:3072].rearrange("p (k i n) -> p k i n", k=8, i=3)
                    wp = wsl[s][:, 3072:4096].rearrange("p (k n) -> p k n", k=8)
                    for i in range(3):
                        c0 = 2560 + i * 1024 + c * 128
                        wload(s, wg[:, :, i, :], w_in_d[l][:, c0:c0 + 128].rearrange("(kc p) n -> p kc n", p=128))
                    wload(s, wp[:, 0:2, :], p_pool_d[l][:, c * 128:(c + 1) * 128].rearrange("(kc p) n -> p kc n", p=128))
                    wload(s, wp[:, 2:4, :], p_conv_d[l][:, c * 128:(c + 1) * 128].rearrange("(kc p) n -> p kc n", p=128))
                    wload(s, wp[:, 4:8, :], p_attn_d[l][:, c * 128:(c + 1) * 128].rearrange("(kc p) n -> p kc n", p=128))
                    for th in range(2):
                        tc = hh * 2 + th
                        mt = m_t[cnt % 2]
                        for i in range(3):
                            bgt = nbank("G")
                            fm_proj(lambda kc, i=i: (wg[:, kc, i, :], [("ws", s)]), bgt, tc)
                            st_ = s_t[(cnt * 3 + i) % 2]
                            S.op("act", (lambda e, bgt=bgt, st_=st_: e.activation(out=st_.ap, in_=banks[bgt][:], func=AF.Tanh, scale=0.5)),
                                 reads=[bk(bgt)], writes=st_.keys())
                            ysv, nkk, koff = ysrc[i]
                            bp = nbank("G")
                            for kk in range(nkk):
                                S.op("pe", (lambda e, bp=bp, kk=kk, koff=koff, ysv=ysv, tc=tc, nkk=nkk, wp=wp: e.matmul(
                                    banks[bp][:], lhsT=wp[:, koff + kk, :], rhs=ysv.ap[:, kk, tcs(tc)], start=(kk == 0), stop=(kk == nkk - 1))),
                                    reads=[("ws", s)] + ysv.krange((kk,), tc * 512, 512), writes=[bk(bp)])
                            if i == 0:
                                S.op("dve", (lambda e, bp=bp, st_=st_, mt=mt: e.scalar_tensor_tensor(
                                    out=mt.ap, in0=st_.ap, scalar=1.0, in1=banks[bp][:], op0=ALU.add, op1=ALU.mult)),
                                    reads=[bk(bp)] + st_.keys(), writes=mt.keys())
                            else:
                                tt_ = t_t[(cnt * 3 + i) % 2]
                                S.op("dve", (lambda e, bp=bp, st_=st_, tt_=tt_: e.scalar_tensor_tensor(
                                    out=tt_.ap, in0=st_.ap, scalar=1.0, in1=banks[bp][:], op0=ALU.add, op1=ALU.mult)),
                                    reads=[bk(bp)] + st_.keys(), writes=tt_.keys())
                                if i == 1:
                                    S.op("dve", (lambda e, tt_=tt_, mt=mt: e.tensor_tensor(out=mt.ap, in0=mt.ap, in1=tt_.ap, op=ALU.add)),
                                         reads=mt.keys() + tt_.keys(), writes=mt.keys())
                                else:
                                    S.op("dve", (lambda e, tt_=tt_, mt=mt, c=c, th=th: e.tensor_tensor(
                                        out=merged.ap[:, c, th * 512:(th + 1) * 512], in0=mt.ap, in1=tt_.ap, op=ALU.add)),
                                        reads=mt.keys() + tt_.keys(), writes=merged.krange((c,), th * 512, 512))
                        cnt += 1
                for ob in range(2):
                    s = wslot()
                    wo = wsl[s][:, :].rearrange("p (k n) -> p k n", k=8)
                    wload(s, wo, w_out_d[l][:, ob * 512:(ob + 1) * 512].rearrange("(kc p) n -> p kc n", p=128))
                    for oo in range(4):
                        o = ob * 4 + oo
                        for th in range(2):
                            tc = hh * 2 + th
                            b = nbank("G")
                            for c in range(8):
                                S.op("pe", (lambda e, b=b, c=c, oo=oo, th=th, wo=wo: e.matmul(
                                    banks[b][:], lhsT=wo[:, c, oo * 128:(oo + 1) * 128], rhs=merged.ap[:, c, th * 512:(th + 1) * 512],
                                    start=(c == 0), stop=(c == 7))),
                                    reads=[("ws", s)] + merged.krange((c,), th * 512, 512), writes=[bk(b)])
                            S.op("dve", (lambda e, b=b, o=o, tc=tc: e.scalar_tensor_tensor(
                                out=xT[:, o, tcs(tc)], in0=banks[b][:], scalar=avec[:, 16 + o:17 + o], in1=xT[:, o, tcs(tc)],
                                op0=ALU.mult, op1=ALU.add)),
                                reads=[bk(b), kx(o, tc), ("avec", lp)], writes=[kx(o, tc)])

        actT = [AV(0, 128, [4, S_], BF16), AV(16384, 128, [4, S_], BF16)]
        sl_t = [AV(32768, 128, [512], F32), AV(32768 + 2048, 128, [512], F32)]

        def ffn_steps(l, pool, tail_norm=None, grp=None):
            lp = l % 2
            modT = modTs[lp]
            nfb = 6
            st_ = {"cnt": 0}
            steps = []
            for fb in range(nfb):
                nch = 4 if fb < 5 else 2
                f0 = fb * 512
                ncol = nch * 128
                W = {}

                def sload(fb=fb, nch=nch, f0=f0, ncol=ncol, W=W):
                    sg = wslot(grp)
                    wg = wsl[sg][:, 0:8 * ncol].rearrange("p (k n) -> p k n", k=8)
                    wload(sg, wg, w_gate_d[l][:, f0:f0 + ncol].rearrange("(kc p) n -> p kc n", p=128))
                    su = wslot(grp)
                    wu = wsl[su][:, 0:8 * ncol].rearrange("p (k n) -> p k n", k=8)
                    wload(su, wu, w_up_d[l][:, f0:f0 + ncol].rearrange("(kc p) n -> p kc n", p=128))
                    sd = wslot(grp)
                    wd = wsl[sd][:, 0:nch * 1024].rearrange("p (j n) -> p j n", j=nch)
                    wload(sd, wd, w_down_d[l][f0:f0 + ncol, :].rearrange("(j p) n -> p j n", p=128))
                    W.update(sg=sg, wg=wg, su=su, wu=wu, sd=sd, wd=wd)
                steps.append(sload)
                at = actT[fb % 2]
                for j in range(nch):
                    for tc in range(NTC):
                        def sgu(j=j, tc=tc, W=W, at=at):
                            wg, wu, sg, su = W["wg"], W["wu"], W["sg"], W["su"]
                            bg_ = nbank(pool)
                            fm_proj(lambda kc: (wg[:, kc, j * 128:(j + 1) * 128], [("ws", sg)]), bg_, tc)
                            bu_ = nbank(pool)
                            fm_proj(lambda kc: (wu[:, kc, j * 128:(j + 1) * 128], [("ws", su)]), bu_, tc)
                            sl = sl_t[st_["cnt"] % 2]
                            st_["cnt"] += 1
                            S.op("act", (lambda e: e.activation(out=sl.ap, in_=banks[bg_][:], func=AF.Silu)),
                                 reads=[bk(bg_)], writes=sl.keys())
                            S.op("dve", (lambda e: e.tensor_tensor(
                                out=at.ap[:, j, tcs(tc)], in0=banks[bu_][:], in1=sl.ap, op=ALU.mult)),
                                reads=[bk(bu_)] + sl.keys(), writes=at.krange((j,), tc * 512, 512))
                        steps.append(sgu)
                dn_order = [(o, tc) for o in range(8) for tc in range(NTC)] if fb < nfb - 1 else \
                           [(o, tc) for tc in range(NTC) for o in range(8)]
                for (o, tc) in dn_order:
                    if True:
                        def sdn(o=o, tc=tc, W=W, at=at, nch=nch):
                            wd, sd = W["wd"], W["sd"]
                            b = nbank(pool)
                            for j in range(nch):
                                S.op("pe", (lambda e, j=j: e.matmul(
                                    banks[b][:], lhsT=wd[:, j, o * 128:(o + 1) * 128], rhs=at.ap[:, j, tcs(tc)],
                                    start=(j == 0), stop=(j == nch - 1))),
                                    reads=[("ws", sd)] + at.krange((j,), tc * 512, 512), writes=[bk(b)])
                            S.op("dve", (lambda e: e.scalar_tensor_tensor(
                                out=xT[:, o, tcs(tc)], in0=banks[b][:], scalar=modT[:, 40 + o:41 + o], in1=xT[:, o, tcs(tc)],
                                op0=ALU.mult, op1=ALU.add)),
                                reads=[bk(b), kx(o, tc), ("modT", lp)], writes=[kx(o, tc)])
                        steps.append(sdn)
                        if fb == nfb - 1 and tail_norm is not None and o == 7 and tc >= 1:
                            steps.append(tail_norm[tc - 1])
            if tail_norm is not None:
                steps.append(tail_norm[3])
            return steps

        KSTOP = int(_os.environ.get("KSTOP", "99"))
        if KSTOP >= 1:
            merge_steps([mod_steps(0)])
        if KSTOP >= 2:
            merge_steps([norm_tc_steps(0, 0, 0)])
        for l in range(L):
            if KSTOP < 3:
                break
            cur["lp"] = l % 2
            attention_phase(l)
            if KSTOP >= 4:
                pool_phase(l)
            if KSTOP >= 5:
                conv_phase(l)
            if KSTOP >= 6:
                merge_phase(l)
            if KSTOP >= 8:
                n2 = norm_tc_steps(8, 24, l % 2, base=40960)
                n1 = None
                if l + 1 < L:
                    n1 = norm_tc_steps(0, 0, (l + 1) % 2, base=40960)
                if n1 is not None and _os.environ.get("TAILN", "1") == "1":
                    fs = ffn_steps(l, "F", tail_norm=n1, grp=("ffn" if l + 1 < L else None))
                    n1 = None
                else:
                    fs = ffn_steps(l, "F", grp=("ffn" if l + 1 < L else None))
                if _os.environ.get("HEADN", "1") == "1":
                    head = [n2[0], n2[1], fs[0], fs[1], n2[2], fs[2], n2[3]]
                    rest = fs[3:]
                else:
                    head = n2
                    rest = fs
                merge_steps([head])
                if l + 1 < L:
                    if _os.environ.get("MODM", "1") == "1":
                        ncut = (len(rest) * 3) // 5
                        merge_steps([rest[:ncut], mod_steps(l + 1, grp="mod")])
                        merge_steps([rest[ncut:]])
                    else:
                        merge_steps([rest])
                        merge_steps([mod_steps(l + 1)])
                else:
                    merge_steps([rest])
                if n1 is not None:
                    merge_steps([n1])
            elif KSTOP >= 7:
                merge_steps([norm_tc_steps(8, 24, l % 2)])

        xo = [AV(0, 128, [D], F32), AV(4096, 128, [D], F32)]
        for t in range(NT):
            xv = xo[t % 2]
            for half in range(2):
                b = nbank("G")
                for cq in range(4):
                    c = half * 4 + cq
                    S.op("pe", (lambda e, b=b, cq=cq, c=c, t=t: e.transpose(
                        banks[b][:, cq * 128:(cq + 1) * 128], xT[:, c, t * 128:(t + 1) * 128], consts[:, O_IDF:O_IDF + 128])),
                        reads=[kx(c, t // 4), "consts"], writes=[bk(b)])
                copy_op(evac_eng(), xv.ap[:, half * 512:(half + 1) * 512], banks[b][:], [bk(b)], xv.keys(half * 512, 512))
            S.op("sp", (lambda e, xv=xv, t=t: e.dma_start(out=out_d[t * 128:(t + 1) * 128, :], in_=xv.ap)),
                 reads=xv.keys(), dma=True)
        S.emit(st)
    return nc


def _consts():
    c = np.zeros((128, NCF), np.float32)
    c[:, O_IDF:O_IDF + 128] = np.eye(128, dtype=np.float32)
    kk = np.arange(128)[:, None]
    qq = np.arange(128)[None, :]
    c[:, O_TRI:O_TRI + 128] = (kk <= qq).astype(np.float32)
    c[:, O_FREQ:O_FREQ + 8] = (500000.0 ** (-np.arange(0, 16, 2, dtype=np.float32) / 16.0)).astype(np.float32)[None, :]
    for p in range(128):
        for cc in range(2):
            w = POOL_W[cc * 2 + (1 if p >= 64 else 0)]
            c[p, O_INVW + cc] = 1.0 / w
            for t in range(16):
                c[p, O_CORR + cc * 16 + t] = 1.0 / min(t + 1, w)
    return c


def _small(b, c, norm1_w, norm2_w, b_ada, pool_scale, conv_w, q_norm_w, k_norm_w, l0, L):
    s = np.zeros((128, NS), np.float32)
    for l in range(L):
        gl = l0 + l
        s[:, O_N1W + l * 8:O_N1W + (l + 1) * 8] = norm1_w[gl].reshape(8, 128).T
        s[:, O_N2W + l * 8:O_N2W + (l + 1) * 8] = norm2_w[gl].reshape(8, 128).T
        s[:, O_BADA + l * 48:O_BADA + (l + 1) * 48] = b_ada[gl].reshape(48, 128).T
        s[:, O_PSC + l * 2:O_PSC + (l + 1) * 2] = pool_scale[gl].reshape(2, 128).T
        s[:, O_CW + l * 6:O_CW + (l + 1) * 6] = conv_w[gl].reshape(3, 2, 128).transpose(2, 0, 1).reshape(128, 6)
        s[:, O_QKW + l * 128:O_QKW + l * 128 + 64] = q_norm_w[gl][None, :]
        s[:, O_QKW + l * 128 + 64:O_QKW + (l + 1) * 128] = k_norm_w[gl][None, :]
    s[:, O_CT:O_CT + 8] = c[b].reshape(8, 128).T
    return s


_NC_CACHE = {}


def _get_nc(L):
    if L not in _NC_CACHE:
        _NC_CACHE[L] = build(L)
    return _NC_CACHE[L]


def _run(x, c, positions, P, l0, L):
    nc = build(L)
    consts = _consts()
    in_maps = []
    wsl_ = slice(l0, l0 + L)
    for b in range(8):
        m = {
            "x": np.ascontiguousarray(x[b]),
            "small": _small(b, c, P["norm1_w"], P["norm2_w"], P["b_ada"], P["pool_scale"], P["conv_w"],
                            P["q_norm_w"], P["k_norm_w"], l0, L),
            "consts": consts,
            "pos": np.ascontiguousarray(positions[b].reshape(NT, 128).T.astype(np.int32)),
        }
        for k in ("w_ada", "w_in", "pool_w", "p_pool", "p_conv", "p_attn", "w_out", "w_gate", "w_up", "w_down"):
            m[k] = np.ascontiguousarray(P[k][wsl_])
        in_maps.append(m)
    res = run_bass_kernel_spmd(nc, in_maps, core_ids=list(range(8)))
    return np.stack([np.asarray(r["out"]) for r in res.results], axis=0)


def kernel(x, c, positions, norm1_w, norm2_w, w_ada, b_ada, w_in, pool_w, pool_scale, conv_w,
           q_norm_w, k_norm_w, p_pool, p_conv, p_attn, w_out, w_gate, w_up, w_down):
    P = dict(norm1_w=np.asarray(norm1_w), norm2_w=np.asarray(norm2_w), w_ada=np.asarray(w_ada), b_ada=np.asarray(b_ada),
             w_in=np.asarray(w_in), pool_w=np.asarray(pool_w), pool_scale=np.asarray(pool_scale), conv_w=np.asarray(conv_w),
             q_norm_w=np.asarray(q_norm_w), k_norm_w=np.asarray(k_norm_w), p_pool=np.asarray(p_pool), p_conv=np.asarray(p_conv),
             p_attn=np.asarray(p_attn), w_out=np.asarray(w_out), w_gate=np.asarray(w_gate), w_up=np.asarray(w_up),
             w_down=np.asarray(w_down))
    x = np.asarray(x, dtype=np.float32)
    c = np.asarray(c)
    positions = np.asarray(positions)
    if FUSED:
        out = _run(x, c, positions, P, 0, DEPTH)
    else:
        out = x
        for l in range(DEPTH):
            out = _run(out, c, positions, P, l, 1)
    return out.astype(np.float32)
```
